# Optimizing a Trainium2 kernel written in Bass

```python
import jax, jax.numpy as jnp
from jax import lax
import numpy as np

D_MODEL = 1024
BATCH = 8
SEQ = 4096
DEPTH = 1

N_META = 16
SWA_HEADS = 8
SWA_KV_HEADS = 2
SWA_HEAD_DIM = 64
SWA_WIDTH = SWA_HEADS * SWA_HEAD_DIM
SWA_KV_WIDTH = SWA_KV_HEADS * SWA_HEAD_DIM
WINDOW = 128
GLA_HEADS = 4
GLA_WIDTH = D_MODEL - SWA_WIDTH
GLA_V_DIM = GLA_WIDTH // GLA_HEADS
GLA_K_DIM = GLA_V_DIM // 2
GLA_K_WIDTH = GLA_HEADS * GLA_K_DIM
GLA_GATE_RANK = 16
GLA_TAU = 16.0
GLA_CHUNK = 64
D_FF = ((8 * D_MODEL + 3 * 256 - 1) // (3 * 256)) * 256
LN_EPS = 1e-5
RMS_EPS = 1e-6
ALPHA = (2.0 * DEPTH) ** 0.25
BETA = (8.0 * DEPTH) ** -0.25
IN_SPLITS = (SWA_WIDTH, SWA_KV_WIDTH, SWA_KV_WIDTH,
             GLA_K_WIDTH, GLA_K_WIDTH, GLA_WIDTH,
             GLA_WIDTH, GLA_GATE_RANK)
D_IN = int(sum(IN_SPLITS))
NEG_INF = -1e30

kernel_name = "hymba_gla_swa_sink_alibi_deepnorm"


def layer_norm(x, g, b):
    xf = x.astype(jnp.float32)
    mu = jnp.mean(xf, axis=-1, keepdims=True)
    var = jnp.mean(jnp.square(xf - mu), axis=-1, keepdims=True)
    y = (xf - mu) * lax.rsqrt(var + LN_EPS)
    return (y * g.astype(jnp.float32) + b.astype(jnp.float32)).astype(x.dtype)


def rms_norm(x, g):
    xf = x.astype(jnp.float32)
    y = xf * lax.rsqrt(jnp.mean(jnp.square(xf), axis=-1, keepdims=True) + RMS_EPS)
    return (y * g.astype(jnp.float32)).astype(x.dtype)


def alibi_slopes(n_heads):
    return jnp.asarray(2.0 ** (-8.0 * (np.arange(n_heads) + 1) / n_heads), dtype=jnp.float32)


def sink_softmax(scores, sink):
    sink = jnp.broadcast_to(sink, scores.shape[:-1] + (1,)).astype(jnp.float32)
    p = jax.nn.softmax(jnp.concatenate([scores, sink], axis=-1), axis=-1)
    return p[..., :-1]


def sliding_window_gqa(q, k, v, sinks):
    B, L, Hq, dh = q.shape
    Hkv = k.shape[2]
    G = Hq // Hkv
    S = L - N_META
    nb = S // WINDOW
    scale = dh ** -0.5
    slopes = alibi_slopes(Hq).reshape(Hkv, G)
    sink = sinks.astype(jnp.float32).reshape(Hkv, G)
    q = q.reshape(B, L, Hkv, G, dh)
    qm, qr = q[:, :N_META], q[:, N_META:]
    km, kr = k[:, :N_META], k[:, N_META:]
    vm, vr = v[:, :N_META], v[:, N_META:]

    qb = qr.reshape(B, nb, WINDOW, Hkv, G, dh)
    kb = kr.reshape(B, nb, WINDOW, Hkv, dh)
    vb = vr.reshape(B, nb, WINDOW, Hkv, dh)
    pad = ((0, 0), (1, 0), (0, 0), (0, 0), (0, 0))
    kband = jnp.concatenate([jnp.pad(kb, pad)[:, :-1], kb], axis=2)
    vband = jnp.concatenate([jnp.pad(vb, pad)[:, :-1], vb], axis=2)
    s_band = jnp.einsum('bnikgd,bnjkd->bkgnij', qb, kband).astype(jnp.float32) * scale
    s_meta = jnp.einsum('bnikgd,bmkd->bkgnim', qb, km).astype(jnp.float32) * scale
    qi = jnp.arange(nb)[:, None] * WINDOW + jnp.arange(WINDOW)[None, :]
    kj = jnp.arange(nb)[:, None] * WINDOW - WINDOW + jnp.arange(2 * WINDOW)[None, :]
    dist_band = qi[:, :, None] - kj[:, None, :]
    valid = (dist_band >= 0) & (dist_band < WINDOW) & (kj[:, None, :] >= 0)
    dist_meta = (qi + N_META)[:, :, None] - jnp.arange(N_META)[None, None, :]
    sl = slopes[:, :, None, None, None]
    s_band = jnp.where(valid, s_band - sl * dist_band.astype(jnp.float32), NEG_INF)
    s_meta = s_meta - sl * dist_meta.astype(jnp.float32)
    p = sink_softmax(jnp.concatenate([s_meta, s_band], axis=-1), sink[None, :, :, None, None, None])
    p = p.astype(v.dtype)
    o_real = (jnp.einsum('bkgnim,bmkd->bnikgd', p[..., :N_META], vm)
              + jnp.einsum('bkgnij,bnjkd->bnikgd', p[..., N_META:], vband))
    o_real = o_real.reshape(B, S, Hq * dh)

    s_mm = jnp.einsum('bikgd,bjkd->bkgij', qm, km).astype(jnp.float32) * scale
    dist_mm = jnp.arange(N_META)[:, None] - jnp.arange(N_META)[None, :]
    s_mm = jnp.where(dist_mm >= 0, s_mm - slopes[:, :, None, None] * dist_mm.astype(jnp.float32), NEG_INF)
    p_mm = sink_softmax(s_mm, sink[None, :, :, None, None]).astype(v.dtype)
    o_meta = jnp.einsum('bkgij,bjkd->bikgd', p_mm, vm).reshape(B, N_META, Hq * dh)
    return jnp.concatenate([o_meta, o_real], axis=1)


def chunked_gla(q, k, v, log_g):
    B, L, H, dk = q.shape
    dv = v.shape[-1]
    C = GLA_CHUNK
    pad_front = (-L) % C
    padw = ((0, 0), (pad_front, 0), (0, 0), (0, 0))
    q = jnp.pad(q * (dk ** -0.5), padw)
    k = jnp.pad(k, padw)
    v = jnp.pad(v, padw)
    log_g = jnp.pad(log_g, padw)
    Lp = L + pad_front
    n = Lp // C

    def to_chunks(t):
        return t.reshape(B, n, C, H, t.shape[-1]).transpose(1, 0, 3, 2, 4)

    qc, kc, vc = to_chunks(q), to_chunks(k), to_chunks(v)
    bc = jnp.cumsum(to_chunks(log_g).astype(jnp.float32), axis=3)
    causal = jnp.tril(jnp.ones((C, C), dtype=bool))[..., None]

    def step(state, inp):
        qx, kx, vx, bx = inp
        o_inter = jnp.einsum('bhcd,bhde->bhce', qx * jnp.exp(bx), state)
        diff = bx[:, :, :, None, :] - bx[:, :, None, :, :]
        decay = jnp.exp(jnp.where(causal, diff, NEG_INF))
        A = jnp.einsum('bhid,bhjd,bhijd->bhij', qx.astype(jnp.float32), kx.astype(jnp.float32), decay)
        o_intra = jnp.einsum('bhij,bhje->bhie', A, vx.astype(jnp.float32))
        b_last = bx[:, :, -1:, :]
        state = (jnp.exp(b_last[:, :, 0, :])[..., None] * state
                 + jnp.einsum('bhjd,bhje->bhde', kx * jnp.exp(b_last - bx), vx.astype(jnp.float32)))
        return state, o_inter + o_intra

    s0 = jnp.zeros((B, H, dk, dv), dtype=jnp.float32)
    _, o = lax.scan(step, s0, (qc, kc, vc, bc))
    o = o.transpose(1, 0, 3, 2, 4).reshape(B, Lp, H, dv)[:, pad_front:]
    return o.astype(v.dtype)


def hybrid_mixer(h, w_in, b_in, w_gate_lr2, b_gate_lr2, sinks, gla_norm_g, w_out):
    B, L, _ = h.shape
    proj = jnp.einsum('bld,de->ble', h, w_in) + b_in
    cuts = tuple(int(c) for c in np.cumsum(IN_SPLITS)[:-1])
    q_s, k_s, v_s, q_g, k_g, v_g, r_g, g_lr = jnp.split(proj, cuts, axis=-1)
    o_s = sliding_window_gqa(q_s.reshape(B, L, SWA_HEADS, SWA_HEAD_DIM),
                             k_s.reshape(B, L, SWA_KV_HEADS, SWA_HEAD_DIM),
                             v_s.reshape(B, L, SWA_KV_HEADS, SWA_HEAD_DIM), sinks)
    gate_logit = jnp.einsum('blr,rk->blk', g_lr, w_gate_lr2) + b_gate_lr2
    log_g = jax.nn.log_sigmoid(gate_logit.astype(jnp.float32)) / GLA_TAU
    o_g = chunked_gla(q_g.reshape(B, L, GLA_HEADS, GLA_K_DIM),
                      k_g.reshape(B, L, GLA_HEADS, GLA_K_DIM),
                      v_g.reshape(B, L, GLA_HEADS, GLA_V_DIM),
                      log_g.reshape(B, L, GLA_HEADS, GLA_K_DIM))
    o_g = rms_norm(o_g, gla_norm_g).reshape(B, L, GLA_WIDTH) * jax.nn.silu(r_g)
    o = jnp.concatenate([o_s, o_g], axis=-1)
    return jnp.einsum('ble,ed->bld', o, w_out)


def swiglu(h, w_gate, w_up, w_down):
    a = jax.nn.silu(jnp.einsum('bld,df->blf', h, w_gate)) * jnp.einsum('bld,df->blf', h, w_up)
    return jnp.einsum('blf,fd->bld', a, w_down)


def setup_inputs(seed: int = 0) -> dict:
    key = jax.random.key(seed)
    ks = jax.random.split(key, 20)
    f32 = jnp.float32
    nrm = lambda k, shape, s: jax.random.normal(k, shape, dtype=f32) * s
    col_scale = np.ones((D_IN,), dtype=np.float32)
    off = np.concatenate([[0], np.cumsum(IN_SPLITS)])
    col_scale[off[2]:off[3]] = BETA
    col_scale[off[5]:off[6]] = BETA
    return {
        "x": nrm(ks[0], (BATCH, SEQ, D_MODEL), 1.0),
        "meta_tokens": nrm(ks[1], (N_META, D_MODEL), 1.0),
        "ln_in_g": 1.0 + nrm(ks[2], (D_MODEL,), 0.02),
        "ln_in_b": nrm(ks[3], (D_MODEL,), 0.02),
        "w_in": nrm(ks[4], (DEPTH, D_MODEL, D_IN), D_MODEL ** -0.5) * jnp.asarray(col_scale),
        "b_in": nrm(ks[5], (DEPTH, D_IN), 0.02),
        "w_gate_lr2": nrm(ks[6], (DEPTH, GLA_GATE_RANK, GLA_K_WIDTH), GLA_GATE_RANK ** -0.5),
        "b_gate_lr2": nrm(ks[7], (DEPTH, GLA_K_WIDTH), 0.1),
        "attn_sinks": nrm(ks[8], (DEPTH, SWA_HEADS), 0.5),
        "gla_norm_g": 1.0 + nrm(ks[9], (DEPTH, GLA_V_DIM), 0.02),
        "w_out": nrm(ks[10], (DEPTH, D_MODEL, D_MODEL), BETA * D_MODEL ** -0.5),
        "ln1_g": 1.0 + nrm(ks[11], (DEPTH, D_MODEL), 0.02),
        "ln1_b": nrm(ks[12], (DEPTH, D_MODEL), 0.02),
        "w_ffn_gate": nrm(ks[13], (DEPTH, D_MODEL, D_FF), D_MODEL ** -0.5),
        "w_ffn_up": nrm(ks[14], (DEPTH, D_MODEL, D_FF), D_MODEL ** -0.5),
        "w_ffn_down": nrm(ks[15], (DEPTH, D_FF, D_MODEL), BETA * D_FF ** -0.5),
        "ln2_g": 1.0 + nrm(ks[16], (DEPTH, D_MODEL), 0.02),
        "ln2_b": nrm(ks[17], (DEPTH, D_MODEL), 0.02),
    }


def reference(x, meta_tokens, ln_in_g, ln_in_b, w_in, b_in, w_gate_lr2, b_gate_lr2,
              attn_sinks, gla_norm_g, w_out, ln1_g, ln1_b, w_ffn_gate, w_ffn_up,
              w_ffn_down, ln2_g, ln2_b):
    B = x.shape[0]
    meta = jnp.broadcast_to(meta_tokens[None].astype(x.dtype), (B, N_META, x.shape[-1]))
    h = layer_norm(jnp.concatenate([meta, x], axis=1), ln_in_g, ln_in_b)
    for l in range(DEPTH):
        mix = hybrid_mixer(h, w_in[l], b_in[l], w_gate_lr2[l], b_gate_lr2[l],
                           attn_sinks[l], gla_norm_g[l], w_out[l])
        h = layer_norm(ALPHA * h + mix, ln1_g[l], ln1_b[l])
        ffn = swiglu(h, w_ffn_gate[l], w_ffn_up[l], w_ffn_down[l])
        h = layer_norm(ALPHA * h + ffn, ln2_g[l], ln2_b[l])
    return h[:, N_META:]
```

```python
import numpy as np
from contextlib import ExitStack
import ml_dtypes
import concourse.bass as bass
import concourse.mybir as mybir
from concourse.bass_utils import run_bass_kernel_spmd

F32 = mybir.dt.float32
BF16 = mybir.dt.bfloat16
AF = mybir.ActivationFunctionType
ALU = mybir.AluOpType

D = 1024
NM = 16
DIN = 2320
DFF = 2816
NFC = DFF // 128
ALPHA = 2.0 ** 0.25
LN_EPS = 1e-5
RMS_EPS = 1e-6
TB = 512
LN8 = float(np.log(0.125))


class Tk:
    __slots__ = ("sem", "val", "key")

    def __init__(self, sem, val, key):
        self.sem, self.val, self.key = sem, val, key


class Buf:
    def __init__(self, name):
        self.name = name
        self.w = None
        self.r = {}


class Prog:
    ENG = ("pe", "act", "dve", "pool", "sp")
    R = 8

    def __init__(self, nc, es):
        self.nc = nc
        self.q = {e: [] for e in self.ENG}
        self.cnt = {e: 0 for e in self.ENG}
        self.waited = {e: {} for e in self.ENG}
        self.sem = {e: es.enter_context(nc.semaphore("s_" + e)) for e in ("pe", "act", "dve", "pool")}
        self.ring = {qn: [es.enter_context(nc.semaphore(f"d_{qn}{i}")) for i in range(self.R)]
                     for qn in ("sp", "pool")}
        self.dma_n = {"sp": 0, "pool": 0}

    def _deps(self, eng, reads, writes):
        ts = []
        for b in reads:
            if b.w is not None:
                ts.append(b.w)
            if b.name.startswith("PB"):
                ts.extend(t for e2, t in b.r.items() if e2 != eng)
        for b in writes:
            if b.w is not None:
                ts.append(b.w)
            ts.extend(b.r.values())
        best = {}
        for t in ts:
            if self.waited[eng].get(t.key, 0) < t.val:
                if t.key not in best or best[t.key].val < t.val:
                    best[t.key] = t
        for t in best.values():
            self.waited[eng][t.key] = t.val
        return list(best.values())

    def _mark(self, eng, tk, reads, writes):
        for b in reads:
            b.r[eng] = tk
        for b in writes:
            b.w = tk
            b.r = {}

    def op(self, eng, fn, reads=(), writes=()):
        waits = self._deps(eng, reads, writes)
        self.cnt[eng] += 1
        tk = Tk(self.sem[eng], self.cnt[eng], eng)
        self.q[eng].append((waits, fn, (self.sem[eng], 1)))
        self._mark(eng, tk, reads, writes)
        return tk

    def dma(self, qn, out, in_, reads=(), writes=()):
        waits = self._deps(qn, reads, writes)
        j = self.dma_n[qn]
        self.dma_n[qn] += 1
        slot, val = j % self.R, 16 * (j // self.R + 1)
        key = f"{qn}{slot}"
        if val > 16 and self.waited[qn].get(key, 0) < val - 16:
            waits.append(Tk(self.ring[qn][slot], val - 16, key))
            self.waited[qn][key] = val - 16
        tk = Tk(self.ring[qn][slot], val, key)
        self.q[qn].append((waits, lambda e: e.dma_start(out=out, in_=in_), (self.ring[qn][slot], 16)))
        self._mark(qn, tk, reads, writes)
        return tk

    def handoff(self, src, dst):
        best = {}
        for b in src:
            for t in ([b.w] if b.w is not None else []) + list(b.r.values()):
                if t.key not in best or best[t.key].val < t.val:
                    best[t.key] = t
        for b in dst:
            for k, t in best.items():
                b.r["h_" + k] = t

    def finish(self):
        waits = []
        for qn in ("sp", "pool"):
            n = self.dma_n[qn]
            for slot in range(self.R):
                cnt = (n - slot + self.R - 1) // self.R if n > slot else 0
                if cnt > 0:
                    waits.append(Tk(self.ring[qn][slot], 16 * cnt, f"{qn}{slot}"))
        self.q["sp"].append((waits, None, None))

    def emit(self):
        nc = self.nc
        with nc.Block() as block:
            def replay(name):
                def f(e):
                    for waits, fn, inc in self.q[name]:
                        for t in waits:
                            e.wait_ge(t.sem, t.val)
                        if fn is not None:
                            ins = fn(e)
                            ins.then_inc(inc[0], inc[1])
                return f
            block.tensor(replay("pe"))
            block.scalar(replay("act"))
            block.vector(replay("dve"))
            block.gpsimd(replay("pool"))
            block.sync(replay("sp"))


def host_consts():
    bf = ml_dtypes.bfloat16
    j = np.arange(128)[:, None]
    i = np.arange(128)[None, :]
    c = {}
    c["ident"] = np.eye(128, dtype=np.float32)
    c["tri1"] = np.where(j <= i, -1.0 / 16.0, 0.0).astype(np.float32)
    c["tri2"] = np.where(j > i, -1.0 / 16.0, 0.0).astype(np.float32)
    c["masku"] = (j <= i).astype(np.float32).astype(bf)
    c["mcur"] = (j <= i).astype(np.float32).astype(bf)
    c["mprev"] = (j > i).astype(np.float32).astype(bf)
    slopes = 2.0 ** (-8.0 * (np.arange(8) + 1) / 8.0)
    a = np.arange(128)
    qrows = np.zeros((3, 8, TB), np.float32)
    for h in range(8):
        qrows[0, h, :] = slopes[h]
        qrows[1, h, :] = -slopes[h] * np.tile(a, TB // 128)
        qrows[2, h, :] = -128.0 * slopes[h]
    c["qrows"] = qrows.astype(bf)
    kb = np.zeros((3, 2, 640), np.float32)
    kb[0, :, :] = np.tile(a, 5)[None, :]
    kb[1] = 1.0
    kb[2] = 1.0
    c["kbrows"] = kb.astype(bf)
    km = np.zeros((3, 2, 32, 16), np.float32)
    km[0] = (np.arange(16) - 16.0)[None, None, :]
    km[1] = 1.0
    km[2] = np.arange(32)[None, :, None]
    c["kmrows"] = km.astype(bf)
    om = np.zeros((33, 128), np.float32)
    om[0:16] = 1.0
    om[32] = 1.0
    c["onesm"] = om.astype(bf)
    for k, v in c.items():
        assert np.all(np.isfinite(v.astype(np.float32)))
    return c


CONST_SPECS = [("ident", [128, 128], F32), ("tri1", [128, 128], F32), ("tri2", [128, 128], F32),
               ("masku", [128, 128], BF16), ("mcur", [128, 128], BF16), ("mprev", [128, 128], BF16),
               ("qrows", [3, 8, TB], BF16), ("kbrows", [3, 2, 640], BF16),
               ("kmrows", [3, 2, 32, 16], BF16), ("onesm", [33, 128], BF16)]


class _Stop(Exception):
    pass


def build(nblk, debug=False, stage=99):
    try:
        return _build(nblk, debug, stage)
    except _Stop as e:
        return e.args[0]


def _build(nblk, debug=False, stage=99):
    S = nblk * TB
    nc = bass.Bass("TRN2", target_bir_lowering=False)
    di = lambda n, s, d=F32: nc.dram_tensor(n, s, d, kind="ExternalInput").ap()
    x = di("x", [S, D])
    meta = di("meta_tokens", [NM, D])
    lnv = [di(n, [D]) for n in ("ln_in_g", "ln_in_b", "ln1_g", "ln1_b", "ln2_g", "ln2_b")]
    w_in = di("w_in", [D, DIN])
    b_in = di("b_in", [DIN])
    wg2 = di("w_gate_lr2", [16, 256])
    bg2 = di("b_gate_lr2", [256])
    sinks = di("attn_sinks", [8])
    gnorm = di("gla_norm_g", [128])
    w_out = di("w_out", [D, D])
    w_fg = di("w_ffn_gate", [D, DFF])
    w_fu = di("w_ffn_up", [D, DFF])
    w_fd = di("w_ffn_down", [DFF, D])
    cst = {n: di(n, s, d) for n, s, d in CONST_SPECS}
    out = nc.dram_tensor("out", [S, D], F32, kind="ExternalOutput").ap()
    if debug:
        dbg_h0 = nc.dram_tensor("dbg_h0", [S, D], F32, kind="ExternalOutput").ap()
        dbg_h1 = nc.dram_tensor("dbg_h1", [S, D], F32, kind="ExternalOutput").ap()
        dbg_o = nc.dram_tensor("dbg_o", [nblk, 128, 8, TB], BF16, kind="ExternalOutput").ap()
    wgu_s = nc.dram_tensor("wgu_s", [NFC, 128, 2, 8, 128], BF16).ap()
    wd_s = nc.dram_tensor("wd_s", [2, 11, 128, 2, 512], BF16).ap()

    with ExitStack() as es:
        P = Prog(nc, es)
        sb = lambda n, s, d: es.enter_context(nc.sbuf_tensor(n, s, d))
        W_IN = sb("W_IN", [128, 8, DIN], BF16)
        W_OUT = sb("W_OUT", [128, 8, D], BF16)
        WGU = [sb(f"WGU{i}", [128, 2, 8, 128], BF16) for i in range(2)]
        WD = [sb(f"WD{i}", [128, 2, 512], BF16) for i in range(3)]
        LNC = sb("LNC", [128, 6, D], F32)
        BTOK = sb("BTOK", [128, 896], F32)
        IDENT = sb("IDENT", [128, 128], F32)
        TRI1 = sb("TRI1", [128, 128], F32)
        TRI2 = sb("TRI2", [128, 128], F32)
        ONESF = sb("ONESF", [128, 128], F32)
        ONESB = sb("ONESB", [128, 128], BF16)
        ONESM = sb("ONESM", [33, 128], BF16)
        WG2 = sb("WG2", [32, 256], F32)
        MASKU = sb("MASKU", [128, 128], BF16)
        MCUR = sb("MCUR", [128, 128], BF16)
        MPREV = sb("MPREV", [128, 128], BF16)
        BCOL = sb("BCOL", [128, 32], F32)
        NEGH = sb("NEGH", [128, 1], F32)
        SINK32 = sb("SINK32", [33, 8], F32)
        NXS = 2
        XS = [sb(f"XS{i}", [128, D], F32) for i in range(NXS)]
        H = sb("H", [128, 4, D], F32)
        HT0 = sb("HT0", [128, 8, TB], BF16)
        HT1 = sb("HT1", [128, 8, TB], BF16)
        QST = sb("QST", [67, 8, TB], BF16)
        KST = sb("KST", [67, 2, 640], BF16)
        VS = sb("VS", [128, 5, 128], BF16)
        AT = sb("AT", [128, NFC, TB], BF16)
        QTT = sb("QTT", [64, 4, 128], BF16)
        KTT = sb("KTT", [64, 4, 128], BF16)
        KH = sb("KH", [128, 256], BF16)
        VG = sb("VG", [128, 512], BF16)
        GLR = sb("GLR", [32, TB], F32)
        OST = sb("OST", [128, 4, TB], BF16)
        OGT = sb("OGT", [128, 4, TB], BF16)
        KMT = sb("KMT", [67, 2, 32, 16], BF16)
        VM = sb("VM", [33, 2, 64], BF16)
        PT = sb("PT", [128, 2, TB], BF16)
        PTM = sb("PTM", [33, 2, TB], BF16)
        GB = sb("GB", [128, TB], F32)
        EX = GB[:, 0:256]
        SPL = GB[:, 256:512]
        EB2 = GB[0:64, :].rearrange("p (h c) -> p h c", h=4)
        SGM = GB
        EB1 = sb("EB1", [64, 4, 128], F32)
        EBL = sb("EBL", [64, 4], F32)
        AM = sb("AM", [128, 4, 128], BF16)
        S32 = sb("S32", [64, 4, 128], F32)
        SBF = sb("SBF", [64, 4, 128], BF16)
        SQ = sb("SQ", [128, TB], F32)
        KTMP = SQ[:, 0:256]
        ER = SQ[:, 256:512]
        OG32 = SQ
        RSTD = sb("RSTD", [128, TB], F32)
        RDEN = sb("RDEN", [128, 256], F32)
        SGF = sb("SGF", [128, TB], F32)
        NLS = 12
        LST = [sb(f"LST{i}", [128, 16], F32) for i in range(NLS)]
        PB = [es.enter_context(nc.psum_tensor(f"PB{i}", [128, 512], F32)) for i in range(8)]
        HMT = AT[:, 0, 0:128].rearrange("p (k t) -> p k t", k=8)
        XT = [AT[:, 4 * i:4 * i + 4, :].bitcast(F32).rearrange("p a b -> p (a b)") for i in range(2)]

        B = {}
        def bufs(*names):
            for n in names:
                B[n] = Buf(n)
        bufs("W_IN", "W_OUT", "LNC", "BTOK", "CONST", "WG2", "BCOL", "SINK32", "QSTx", "KSTx", "KMT", "KMTx",
             "VM", "PTMx", "GLR", "GB", "EB1", "EBL", "AM", "S32", "SBF", "SQ", "RSTD", "RDEN", "SGF", "PTM", "KSTp", "VSp",
             "PTc", "PTp", "QTT", "KTT", "KH", "VG")
        for i in range(NXS):
            bufs(f"XS{i}")
        for i in range(NLS):
            bufs(f"LST{i}")
        for i in range(2):
            bufs(f"WGU{i}")
        for i in range(4):
            bufs(f"WD{i}", f"H{i}", f"HT0_{i}", f"HT1_{i}", f"QST{i}", f"KST{i}", f"VS{i}", f"OST{i}", f"OGT{i}")
        for i in range(8):
            bufs(f"PB{i}")
        for i in range(NFC):
            bufs(f"AT{i}", f"wgu_s{i}")
        for i in range(11):
            bufs(f"wd_s0_{i}", f"wd_s1_{i}")

        def bl(*names):
            return [B[n] for n in names]

        P.dma("pool", W_IN[:], w_in.rearrange("(k p) n -> p k n", p=128), writes=bl("W_IN"))
        P.dma("sp", XS[0][0:16, :], meta, writes=bl("XS0"))
        for i in range(6):
            P.dma("sp", LNC[:, i, :], lnv[i].partition_broadcast(128), writes=bl("LNC"))
        for (n, t) in (("ident", IDENT), ("tri1", TRI1), ("tri2", TRI2), ("masku", MASKU), ("mcur", MCUR),
                       ("mprev", MPREV), ("onesm", ONESM)):
            P.dma("sp", t[:], cst[n], writes=bl("CONST"))
        P.dma("sp", QST[64:67, :, :], cst["qrows"], writes=bl("QSTx"))
        P.dma("sp", KST[64:67, :, :], cst["kbrows"], writes=bl("KSTx"))
        P.dma("sp", KMT[64:67, :, :, :], cst["kmrows"], writes=bl("KMTx"))
        P.dma("sp", BTOK[:, 0:768], b_in[1024:1792].partition_broadcast(128), writes=bl("BTOK"))
        P.dma("sp", BTOK[:, 768:896], b_in[640:768].partition_broadcast(128), writes=bl("BTOK"))
        P.op("pool", lambda e: e.memset(WG2[:], 0.0), writes=bl("WG2"))
        P.dma("sp", WG2[0:16, :], wg2, writes=bl("WG2"))
        P.dma("sp", WG2[16:17, :], bg2.rearrange("(o n) -> o n", o=1), writes=bl("WG2"))
        P.op("pool", lambda e: e.memset(BCOL[:], 0.0), writes=bl("BCOL"))
        col = lambda off, n: b_in[off:off + n].rearrange("(p o) -> p o", o=1)
        for h in range(8):
            P.dma("sp", BCOL[0:64, h:h + 1], col(64 * h, 64), writes=bl("BCOL"))
        for k in range(2):
            P.dma("sp", BCOL[0:64, 8 + k:9 + k], col(512 + 64 * k, 64), writes=bl("BCOL"))
        for hh in range(4):
            P.dma("sp", BCOL[0:64, 10 + hh:11 + hh], col(768 + 64 * hh, 64), writes=bl("BCOL"))
            P.dma("sp", BCOL[0:64, 14 + hh:15 + hh], col(1024 + 64 * hh, 64), writes=bl("BCOL"))
            P.dma("sp", BCOL[:, 18 + hh:19 + hh], col(1792 + 128 * hh, 128), writes=bl("BCOL"))
        P.dma("sp", BCOL[0:16, 22:23], col(2304, 16), writes=bl("BCOL"))
        P.dma("sp", BCOL[:, 23:24], gnorm.rearrange("(p o) -> p o", o=1), writes=bl("BCOL"))
        P.op("act", lambda e: e.mul(out=BCOL[0:64, 0:8], in_=BCOL[0:64, 0:8], mul=0.125), reads=bl("BCOL"), writes=bl("BCOL"))
        P.op("act", lambda e: e.mul(out=BCOL[:, 24:28], in_=BCOL[:, 18:22], mul=0.5), reads=bl("BCOL"), writes=bl("BCOL"))
        P.op("act", lambda e: e.mul(out=BCOL[:, 28:29], in_=BCOL[:, 23:24], mul=0.5), reads=bl("BCOL"), writes=bl("BCOL"))
        P.op("dve", lambda e: e.memset(NEGH[:], -0.5), writes=bl("CONST"))
        P.dma("sp", SINK32[32:33, :], sinks.rearrange("(o n) -> o n", o=1), writes=bl("SINK32"))
        P.op("act", lambda e: e.activation(out=SINK32[32:33, :], in_=SINK32[32:33, :], func=AF.Exp),
             reads=bl("SINK32"), writes=bl("SINK32"))
        P.op("dve", lambda e: e.memset(ONESF[:], 1.0), writes=bl("CONST"))
        P.op("dve", lambda e: e.memset(ONESB[:], 1.0), writes=bl("CONST"))
        P.op("dve", lambda e: e.memset(GLR[:], 1.0), writes=bl("GLR"))
        P.op("dve", lambda e: e.memset(PTM[:], 0.0), writes=bl("PTMx"))
        P.op("dve", lambda e: e.memset(VM[:], 0.0), writes=bl("VM"))
        for h in range(8):
            k, g = h // 4, h % 4
            P.op("dve", (lambda k, g, h: lambda e: e.tensor_scalar(
                out=PTM[32:33, k, g * 128:(g + 1) * 128], in0=ONESF[32:33, :], scalar1=SINK32[32:33, h:h + 1],
                scalar2=None, op0=ALU.mult))(k, g, h), reads=bl("SINK32", "CONST"), writes=bl("PTMx"))
        P.dma("pool", W_OUT[:], w_out.rearrange("(k p) n -> p k n", p=128), writes=bl("W_OUT"))
        def casts_more():
            return iter(())

        def casts():
            for fc in range(NFC):
                for m, w in enumerate((w_fg, w_fu)):
                    P.dma("pool", wgu_s[fc, :, m, :, :], w[:, fc * 128:(fc + 1) * 128].rearrange("(k p) n -> p k n", p=128),
                          writes=bl(f"wgu_s{fc}"))
                yield 1
            for hf in range(2):
                for s in range(11):
                    P.dma("pool", wd_s[hf, s], w_fd[s * 256:(s + 1) * 256, hf * 512:(hf + 1) * 512]
                          .rearrange("(f p) n -> p f n", p=128), writes=bl(f"wd_s{hf}_{s}"))
                    yield 1

        def ckpt(k):
            if stage == k:
                P.finish()
                P.emit()
                raise _Stop(nc)

        ckpt(0)
        ctr = {"ls": 0, "xs": 0}

        def layer_norm_stats(src, np_, rb):
            i = ctr["ls"] % NLS
            ctr["ls"] += 1
            L, lb = LST[i], bl(f"LST{i}")
            P.op("dve", lambda e: e.bn_stats(out=L[0:np_, 0:6], in_=src[:, 0:512]), reads=rb, writes=lb)
            P.op("dve", lambda e: e.bn_stats(out=L[0:np_, 6:12], in_=src[:, 512:1024]), reads=rb, writes=lb)
            P.op("dve", lambda e: e.bn_aggr(out=L[0:np_, 12:14], in_=L[0:np_, 0:12]), reads=lb, writes=lb)
            P.op("dve", lambda e: e.tensor_scalar(out=L[0:np_, 15:16], in0=L[0:np_, 12:13], scalar1=-1.0, scalar2=None, op0=ALU.mult),
                 reads=lb, writes=lb)
            P.op("act", lambda e: e.activation(out=L[0:np_, 14:15], in_=L[0:np_, 13:14], func=AF.Ln, bias=LN_EPS, scale=1.0), reads=lb, writes=lb)
            P.op("act", lambda e: e.activation(out=L[0:np_, 14:15], in_=L[0:np_, 14:15], func=AF.Exp, scale=-0.5), reads=lb, writes=lb)
            P.op("act", lambda e: e.activation(out=L[0:np_, 15:16], in_=L[0:np_, 15:16], func=AF.Identity, scale=L[0:np_, 14:15]), reads=lb, writes=lb)
            return L, lb

        def layer_norm_apply(src, dst, np_, gi, rb, wb, L, lb):
            P.op("act", lambda e: e.activation(out=dst, in_=src, func=AF.Identity, bias=L[0:np_, 15:16], scale=L[0:np_, 14:15]),
                 reads=rb + lb, writes=wb)
            P.op("dve", lambda e: e.tensor_tensor(out=dst, in0=dst, in1=LNC[0:np_, gi, :], op=ALU.mult),
                 reads=wb + bl("LNC"), writes=wb)
            P.op("pool", lambda e: e.tensor_tensor(out=dst, in0=dst, in1=LNC[0:np_, gi + 1, :], op=ALU.add),
                 reads=wb + bl("LNC"), writes=wb)

        def next_xs():
            i = ctr["xs"] % NXS
            ctr["xs"] += 1
            return XS[i], bl(f"XS{i}")

        def ln_in_tile(T):
            xs, xb = XS[T % 2], bl(f"XS{T % 2}")
            P.dma("sp", xs[:], x[T * 128:(T + 1) * 128, :], writes=xb)
            L, lb = layer_norm_stats(xs[:, :], 128, xb)
            layer_norm_apply(xs[:, :], xs[:, :], 128, 0, xb, xb, L, lb)
            return xs, xb

        def transpose_tile(src, srcb, HTd, dstb, j, banks):
            for half, eng in ((0, "act"), (1, "dve")):
                pb = banks[half]
                def tr(e, half=half, pb=pb):
                    for k in range(4):
                        kk = half * 4 + k
                        i = e.transpose(out=PB[pb][:, k * 128:(k + 1) * 128], in_=src[:, kk * 128:(kk + 1) * 128],
                                        identity=IDENT[:])
                    return i
                P.op("pe", tr, reads=srcb + bl("CONST"), writes=bl(f"PB{pb}"))
                dst = HTd[:, half * 4:(half + 1) * 4, j * 128:(j + 1) * 128]
                psv = PB[pb][:].rearrange("p (k t) -> p k t", k=4)
                if eng == "act":
                    P.op("act", lambda e, dst=dst, psv=psv: e.activation(out=dst, in_=psv, func=AF.Copy),
                         reads=bl(f"PB{pb}"), writes=dstb)
                else:
                    P.op("dve", lambda e, dst=dst, psv=psv: e.tensor_copy(out=dst, in_=psv),
                         reads=bl(f"PB{pb}"), writes=dstb)

        def mm_group(e, out_ap, pairs):
            n = len(pairs)
            for idx, (l, r) in enumerate(pairs):
                i = e.matmul(out_ap, lhsT=l, rhs=r, start=(idx == 0), stop=(idx == n - 1))
            return i

        HT0all = [f"HT0_{j}" for j in range(4)]
        HT1all = [f"HT1_{j}" for j in range(4)]

        def gate_pipeline(ntok, tcol, pbi):
            np_ = ntok
            pbn = f"PB{pbi}"
            P.op("pe", lambda e: e.matmul(PB[pbi][0:np_, 0:256], lhsT=GLR[0:32, tcol:tcol + ntok], rhs=WG2[:, :],
                                          start=True, stop=True), reads=bl("GLR", "WG2"), writes=bl(pbn))
            P.op("act", lambda e: e.activation(out=EX[0:np_, :], in_=PB[pbi][0:np_, 0:256], func=AF.Exp, scale=-1.0),
                 reads=bl(pbn), writes=bl("GB"))
            P.op("act", lambda e: e.activation(out=SPL[0:np_, :], in_=EX[0:np_, :], func=AF.Ln, bias=1.0),
                 reads=bl("GB"), writes=bl("GB"))
            P.op("pe", lambda e: e.matmul(PB[pbi][0:np_, 256:512], lhsT=TRI2[0:np_, 0:np_], rhs=SPL[0:np_, :],
                                          start=True, stop=True), reads=bl("GB", "CONST"), writes=bl(pbn))
            P.op("act", lambda e: e.activation(out=ER[0:np_, :], in_=PB[pbi][0:np_, 256:512], func=AF.Exp),
                 reads=bl(pbn), writes=bl("SQ"))

        L, lb = layer_norm_stats(XS[0][0:16, :], 16, bl("XS0"))
        layer_norm_apply(XS[0][0:16, :], XS[0][0:16, :], 16, 0, bl("XS0"), bl("XS0"), L, lb)
        def trm(e):
            for kk in range(8):
                i = e.transpose(out=PB[0][:, kk * 16:(kk + 1) * 16], in_=XS[0][0:16, kk * 128:(kk + 1) * 128],
                                identity=IDENT[0:16, 0:16])
            return i
        P.op("pe", trm, reads=bl("XS0", "CONST"), writes=bl("PB0"))
        P.op("act", lambda e: e.activation(out=HMT, in_=PB[0][:, 0:128].rearrange("p (k t) -> p k t", k=8), func=AF.Copy),
             reads=bl("PB0"), writes=bl("AT0"))
        P.op("pe", lambda e: mm_group(e, PB[2][0:16, 0:512], [(HMT[:, kk, :], W_IN[:, kk, 1024:1536]) for kk in range(8)]),
             reads=bl("AT0", "W_IN"), writes=bl("PB2"))
        def mmeta3(e):
            mm_group(e, PB[3][0:16, 0:256], [(HMT[:, kk, :], W_IN[:, kk, 1536:1792]) for kk in range(8)])
            return mm_group(e, PB[3][0:16, 256:384], [(HMT[:, kk, :], W_IN[:, kk, 640:768]) for kk in range(8)])
        P.op("pe", mmeta3, reads=bl("AT0", "W_IN"), writes=bl("PB3"))
        def mmeta4(e):
            for k in range(2):
                mm_group(e, PB[1][0:64, k * 16:(k + 1) * 16],
                         [(W_IN[:, kk, 512 + 64 * k:576 + 64 * k], HMT[:, kk, :]) for kk in range(8)])
            return mm_group(e, PB[1][0:16, 64:80], [(W_IN[:, kk, 2304:2320], HMT[:, kk, :]) for kk in range(8)])
        P.op("pe", mmeta4, reads=bl("AT0", "W_IN"), writes=bl("PB1"))
        for k in range(2):
            P.op("act", (lambda k: lambda e: e.activation(
                out=KMT[0:64, k, :, :], in_=PB[1][0:64, k * 16:(k + 1) * 16].unsqueeze(1).broadcast_to([64, 32, 16]),
                func=AF.Identity, bias=BCOL[0:64, 8 + k:9 + k], scale=1.0))(k), reads=bl("PB1", "BCOL"), writes=bl("KMT"))
        P.op("act", lambda e: e.activation(out=GLR[0:16, 0:16], in_=PB[1][0:16, 64:80], func=AF.Identity,
                                           bias=BCOL[0:16, 22:23], scale=1.0), reads=bl("PB1", "BCOL"), writes=bl("GLR"))
        P.op("dve", lambda e: e.tensor_tensor(out=VM[0:16, :, :], in0=PB[3][0:16, 256:384].rearrange("p (k d) -> p k d", k=2),
                                              in1=BTOK[0:16, 768:896].rearrange("p (k d) -> p k d", k=2), op=ALU.add),
             reads=bl("PB3", "BTOK"), writes=bl("VM"))
        P.op("dve", lambda e: e.tensor_tensor(out=VG[0:16, 0:256], in0=PB[2][0:16, 256:512], in1=BTOK[0:16, 256:512], op=ALU.add),
             reads=bl("PB2", "BTOK"), writes=bl("VG"))
        P.op("dve", lambda e: e.tensor_tensor(out=VG[0:16, 256:512], in0=PB[3][0:16, 0:256], in1=BTOK[0:16, 512:768], op=ALU.add),
             reads=bl("PB3", "BTOK"), writes=bl("VG"))
        P.op("dve", lambda e: e.tensor_tensor(out=KTMP[0:16, :], in0=PB[2][0:16, 0:256], in1=BTOK[0:16, 0:256], op=ALU.add),
             reads=bl("PB2", "BTOK"), writes=bl("SQ"))
        gate_pipeline(16, 0, 0)
        P.op("dve", lambda e: e.tensor_tensor(out=KH[0:16, :], in0=KTMP[0:16, :], in1=ER[0:16, :], op=ALU.mult),
             reads=bl("SQ"), writes=bl("KH"))
        def mstate0(e):
            for hh in range(4):
                i = e.matmul(PB[2][0:64, hh * 128:(hh + 1) * 128], lhsT=KH[0:16, hh * 64:(hh + 1) * 64],
                             rhs=VG[0:16, hh * 128:(hh + 1) * 128], start=True, stop=True)
            return i
        P.op("pe", mstate0, reads=bl("KH", "VG"), writes=bl("PB2"))
        P.op("dve", lambda e: e.tensor_copy(out=S32[:], in_=PB[2][0:64, :].rearrange("p (h c) -> p h c", h=4)),
             reads=bl("PB2"), writes=bl("S32"))
        P.op("act", lambda e: e.activation(out=SBF[:], in_=S32[:], func=AF.Copy), reads=bl("S32"), writes=bl("SBF"))
        ckpt(1)

        def merge(g1, g2):
            gens = [g1, g2]
            live = [True, True]
            i = 0
            while live[0] or live[1]:
                if live[i]:
                    try:
                        yield next(gens[i])
                    except StopIteration:
                        live[i] = False
                i ^= 1

        prog = {"gla": 0, "swa": 0, "next": 0}

        def mixer_head(b):
            prog["gla"] = prog["swa"] = 0
            def tick():
                return 1
            if b > 0:
                P.op("act", lambda e: e.activation(out=KST[0:64, :, 0:128], in_=KST[0:64, :, 512:640], func=AF.Copy),
                     reads=bl("KST3"), writes=bl("KSTp"))
                P.op("dve", lambda e: e.tensor_copy(out=VS[:, 0, :], in_=VS[:, 4, :]), reads=bl("VS3"), writes=bl("VSp"))
            for j in range(4):
                T = 4 * b + j
                xs, xb = ln_in_tile(T)
                if debug:
                    P.dma("pool", dbg_h0[T * 128:(T + 1) * 128, :], xs[:, :], reads=xb)
                yield 6.0
                transpose_tile(xs, xb, HT0, bl(f"HT0_{j}"), j, (0, 1))
                yield 1.5
            yield from m_qk(b)
            yield from merge(m_gla(b), m_swa(b))
            if debug:
                P.dma("pool", dbg_o[b, :, 0:4, :], OST[:], reads=bl(*[f"OST{j}" for j in range(4)]))
                P.dma("pool", dbg_o[b, :, 4:8, :], OGT[:], reads=bl(*[f"OGT{j}" for j in range(4)]))

        def m_gla(b):
            def tick():
                return 1
            for j in range(4):
                tc = slice(j * 128, (j + 1) * 128)
                hb = f"HT0_{j}"
                P.op("pe", lambda e, tc=tc: mm_group(e, PB[2][:, 0:512], [(HT0[:, kk, tc], W_IN[:, kk, 1024:1536]) for kk in range(8)]),
                     reads=bl(hb, "W_IN"), writes=bl("PB2"))
                P.op("pe", lambda e, tc=tc: mm_group(e, PB[3][:, 0:256], [(HT0[:, kk, tc], W_IN[:, kk, 1536:1792]) for kk in range(8)]),
                     reads=bl(hb, "W_IN"), writes=bl("PB3"))
                P.op("dve", lambda e: e.tensor_tensor(out=VG[:, 0:256], in0=PB[2][:, 256:512], in1=BTOK[:, 256:512], op=ALU.add),
                     reads=bl("PB2", "BTOK"), writes=bl("VG"))
                P.op("dve", lambda e: e.tensor_tensor(out=VG[:, 256:512], in0=PB[3][:, 0:256], in1=BTOK[:, 512:768], op=ALU.add),
                     reads=bl("PB3", "BTOK"), writes=bl("VG"))
                P.op("dve", lambda e: e.tensor_tensor(out=KTMP[:, :], in0=PB[2][:, 0:256], in1=BTOK[:, 0:256], op=ALU.add),
                     reads=bl("PB2", "BTOK"), writes=bl("SQ"))
                yield 1.0
                if j == 0:
                    P.op("pe", lambda e: mm_group(e, PB[0][0:16, :], [(W_IN[:, kk, 2304:2320], HT0[:, kk, :]) for kk in range(8)]),
                         reads=bl(*HT0all, "W_IN"), writes=bl("PB0"))
                    P.op("act", lambda e: e.activation(out=GLR[0:16, :], in_=PB[0][0:16, :], func=AF.Identity,
                                                       bias=BCOL[0:16, 22:23], scale=1.0), reads=bl("PB0", "BCOL"), writes=bl("GLR"))
                gate_pipeline(128, j * 128, 0)
                P.op("dve", lambda e: e.tensor_tensor(out=KH[:, :], in0=KTMP[:, :], in1=ER[:, :], op=ALU.mult),
                     reads=bl("SQ"), writes=bl("KH"))
                yield 3.0
                def mbt(e):
                    for hh in range(4):
                        i = e.matmul(PB[1][0:64, hh * 128:(hh + 1) * 128], lhsT=SPL[:, hh * 64:(hh + 1) * 64], rhs=TRI1[:, :],
                                     start=True, stop=True)
                    return i
                P.op("pe", mbt, reads=bl("GB", "CONST"), writes=bl("PB1"))
                pbv = PB[1][0:64, :].rearrange("p (h c) -> p h c", h=4)
                P.op("act", lambda e, pbv=pbv: e.activation(out=EB1[:], in_=pbv, func=AF.Exp, bias=LN8, scale=1.0),
                     reads=bl("PB1"), writes=bl("EB1"))
                P.op("act", lambda e, pbv=pbv: e.activation(out=EBL[:, :], in_=pbv[:, :, 127], func=AF.Exp),
                     reads=bl("PB1"), writes=bl("EBL"))
                P.op("act", lambda e, pbv=pbv: e.activation(out=EB2, in_=pbv, func=AF.Exp, scale=-1.0),
                     reads=bl("PB1"), writes=bl("GB"))
                yield 1.5
                def mqg(e, tc=tc):
                    for hh in range(4):
                        i = mm_group(e, PB[2][0:64, hh * 128:(hh + 1) * 128],
                                     [(W_IN[:, kk, 768 + 64 * hh:832 + 64 * hh], HT0[:, kk, tc]) for kk in range(8)])
                    return i
                P.op("pe", mqg, reads=bl(hb, "W_IN"), writes=bl("PB2"))
                def mkg(e, tc=tc):
                    for hh in range(4):
                        i = mm_group(e, PB[3][0:64, hh * 128:(hh + 1) * 128],
                                     [(W_IN[:, kk, 1024 + 64 * hh:1088 + 64 * hh], HT0[:, kk, tc]) for kk in range(8)])
                    return i
                P.op("pe", mkg, reads=bl(hb, "W_IN"), writes=bl("PB3"))
                for hh in range(4):
                    P.op("dve", lambda e, hh=hh: e.scalar_tensor_tensor(
                        out=QTT[:, hh, :], in0=PB[2][0:64, hh * 128:(hh + 1) * 128], scalar=BCOL[0:64, 10 + hh:11 + hh],
                        in1=EB1[:, hh, :], op0=ALU.add, op1=ALU.mult), reads=bl("PB2", "BCOL", "EB1"), writes=bl("QTT"))
                    P.op("dve", lambda e, hh=hh: e.scalar_tensor_tensor(
                        out=KTT[:, hh, :], in0=PB[3][0:64, hh * 128:(hh + 1) * 128], scalar=BCOL[0:64, 14 + hh:15 + hh],
                        in1=EB2[:, hh, :], op0=ALU.add, op1=ALU.mult), reads=bl("PB3", "BCOL", "GB"), writes=bl("KTT"))
                yield 2.5
                def ma(e):
                    for hh in range(4):
                        i = e.matmul(PB[0][:, hh * 128:(hh + 1) * 128], lhsT=KTT[:, hh, :], rhs=QTT[:, hh, :], start=True, stop=True)
                    return i
                P.op("pe", ma, reads=bl("KTT", "QTT"), writes=bl("PB0"))
                P.op("dve", lambda e: e.tensor_tensor(out=AM[:], in0=PB[0][:].rearrange("p (h c) -> p h c", h=4),
                                                      in1=MASKU[:].unsqueeze(1).broadcast_to([128, 4, 128]), op=ALU.mult),
                     reads=bl("PB0", "CONST"), writes=bl("AM"))
                yield 0.7
                def mo(e):
                    for hh in range(4):
                        e.matmul(PB[1][:, hh * 128:(hh + 1) * 128], lhsT=VG[:, hh * 128:(hh + 1) * 128], rhs=AM[:, hh, :],
                                 start=True, stop=False)
                        i = e.matmul(PB[1][:, hh * 128:(hh + 1) * 128], lhsT=SBF[:, hh, :], rhs=QTT[:, hh, :], start=False, stop=True)
                    return i
                P.op("pe", mo, reads=bl("VG", "AM", "SBF", "QTT"), writes=bl("PB1"))
                def mst(e):
                    for hh in range(4):
                        i = e.matmul(PB[2][0:64, hh * 128:(hh + 1) * 128], lhsT=KH[:, hh * 64:(hh + 1) * 64],
                                     rhs=VG[:, hh * 128:(hh + 1) * 128], start=True, stop=True)
                    return i
                P.op("pe", mst, reads=bl("KH", "VG"), writes=bl("PB2"))
                P.op("dve", lambda e: e.tensor_tensor(out=S32[:], in0=S32[:], in1=EBL[:, :].unsqueeze(2).broadcast_to([64, 4, 128]), op=ALU.mult),
                     reads=bl("S32", "EBL"), writes=bl("S32"))
                P.op("dve", lambda e: e.tensor_tensor(out=S32[:], in0=S32[:], in1=PB[2][0:64, :].rearrange("p (h c) -> p h c", h=4), op=ALU.add),
                     reads=bl("S32", "PB2"), writes=bl("S32"))
                P.op("act", lambda e: e.activation(out=SBF[:], in_=S32[:], func=AF.Copy), reads=bl("S32"), writes=bl("SBF"))
                P.op("act", lambda e: e.activation(out=SQ[:], in_=PB[1][:], func=AF.Square), reads=bl("PB1"), writes=bl("SQ"))
                P.op("pe", lambda e: e.matmul(PB[0][:, :], lhsT=ONESF[:, :], rhs=SQ[:, :], start=True, stop=True),
                     reads=bl("SQ", "CONST"), writes=bl("PB0"))
                P.op("act", lambda e: e.activation(out=RSTD[:], in_=PB[0][:], func=AF.Ln, bias=RMS_EPS, scale=1.0 / 128.0),
                     reads=bl("PB0"), writes=bl("RSTD"))
                P.op("act", lambda e: e.activation(out=RSTD[:], in_=RSTD[:], func=AF.Exp, scale=-0.5), reads=bl("RSTD"), writes=bl("RSTD"))
                P.op("dve", lambda e: e.scalar_tensor_tensor(out=OG32[:], in0=PB[1][:], scalar=BCOL[:, 28:29], in1=RSTD[:],
                                                             op0=ALU.mult, op1=ALU.mult), reads=bl("PB1", "BCOL", "RSTD"), writes=bl("SQ"))
                yield 3.0
                def mrg(e, tc=tc):
                    for hh in range(4):
                        i = mm_group(e, PB[3][:, hh * 128:(hh + 1) * 128],
                                     [(W_IN[:, kk, 1792 + 128 * hh:1920 + 128 * hh], HT0[:, kk, tc]) for kk in range(8)])
                    return i
                P.op("pe", mrg, reads=bl(hb, "W_IN"), writes=bl("PB3"))
                for hh in range(4):
                    P.op("act", lambda e, hh=hh: e.activation(out=SGM[:, hh * 128:(hh + 1) * 128], in_=PB[3][:, hh * 128:(hh + 1) * 128],
                                                              func=AF.Tanh, bias=BCOL[:, 24 + hh:25 + hh], scale=0.5),
                         reads=bl("PB3", "BCOL"), writes=bl("GB"))
                P.op("dve", lambda e: e.scalar_tensor_tensor(out=SGM[:], in0=SGM[:], scalar=1.0, in1=OG32[:], op0=ALU.add, op1=ALU.mult),
                     reads=bl("SQ", "GB"), writes=bl("GB"))
                for hh in range(4):
                    P.op("dve", lambda e, hh=hh, tc=tc: e.scalar_tensor_tensor(
                        out=OGT[:, hh, tc], in0=PB[3][:, hh * 128:(hh + 1) * 128], scalar=BCOL[:, 18 + hh:19 + hh],
                        in1=SGM[:, hh * 128:(hh + 1) * 128], op0=ALU.add, op1=ALU.mult),
                        reads=bl("PB3", "BCOL", "GB"), writes=bl(f"OGT{j}"))
                prog["gla"] = j + 1
                yield 1.5
        def m_qk(b):
            def mvs(e):
                for j in range(4):
                    i = mm_group(e, PB[2][:, j * 128:(j + 1) * 128],
                                 [(HT0[:, kk, j * 128:(j + 1) * 128], W_IN[:, kk, 640:768]) for kk in range(8)])
                return i
            P.op("pe", mvs, reads=bl(*HT0all, "W_IN"), writes=bl("PB2"))
            P.op("dve", lambda e: e.tensor_tensor(out=VS[:, 1:5, :], in0=PB[2][:, :].rearrange("p (j c) -> p j c", j=4),
                                                  in1=BTOK[:, 768:896].unsqueeze(1).broadcast_to([128, 4, 128]), op=ALU.add),
                 reads=bl("PB2", "BTOK"), writes=bl("VS0", "VS1", "VS2", "VS3"))
            yield 0.3
            for h in range(8):
                pb = h % 2
                P.op("pe", lambda e, h=h, pb=pb: mm_group(e, PB[pb][0:64, :], [(W_IN[:, kk, 64 * h:64 * h + 64], HT0[:, kk, :]) for kk in range(8)]),
                     reads=bl(*HT0all, "W_IN"), writes=bl(f"PB{pb}"))
                P.op("act", lambda e, h=h, pb=pb: e.activation(out=QST[0:64, h, :], in_=PB[pb][0:64, :], func=AF.Identity,
                                                               bias=BCOL[0:64, h:h + 1], scale=0.125),
                     reads=bl(f"PB{pb}", "BCOL"), writes=bl(*[f"QST{j}" for j in range(4)]))
                yield 0.3
            for k in range(2):
                pb = 2 + k
                P.op("pe", lambda e, k=k, pb=pb: mm_group(e, PB[pb][0:64, :], [(W_IN[:, kk, 512 + 64 * k:576 + 64 * k], HT0[:, kk, :]) for kk in range(8)]),
                     reads=bl(*HT0all, "W_IN"), writes=bl(f"PB{pb}"))
                P.op("act", lambda e, k=k, pb=pb: e.activation(out=KST[0:64, k, 128:640], in_=PB[pb][0:64, :], func=AF.Identity,
                                                               bias=BCOL[0:64, 8 + k:9 + k], scale=1.0),
                     reads=bl(f"PB{pb}", "BCOL"), writes=bl(*[f"KST{j}" for j in range(4)]))
                yield 0.3

        def m_swa(b):
            def tick():
                return 1
            for j in range(4):
                T = 4 * b + j
                tc = slice(j * 128, (j + 1) * 128)
                for k in range(2):
                    qv = QST[:, 4 * k:4 * k + 4, tc]
                    kprev_b = f"KST{j - 1}" if j > 0 else "KSTp"
                    vprev_b = f"VS{j - 1}" if j > 0 else "VSp"
                    P.op("pe", lambda e, k=k, j=j, qv=qv: e.matmul(PB[0][:, :], lhsT=KST[0:66, k, 128 + 128 * j:256 + 128 * j], rhs=qv[0:66],
                                                                   start=True, stop=True),
                         reads=bl(f"KST{j}", "KSTx", f"QST{j}", "QSTx"), writes=bl("PB0"))
                    if T > 0:
                        P.op("pe", lambda e, k=k, j=j, qv=qv: e.matmul(PB[1][:, :], lhsT=KST[0:67, k, 128 * j:128 + 128 * j], rhs=qv[0:67],
                                                                       start=True, stop=True),
                             reads=bl(kprev_b, "KSTx", f"QST{j}", "QSTx"), writes=bl("PB1"))
                    P.op("pe", lambda e, k=k, T=T, qv=qv: e.matmul(PB[2][0:16, :], lhsT=KMT[0:67, k, T, :], rhs=qv[0:67], start=True, stop=True),
                         reads=bl("KMT", "KMTx", f"QST{j}", "QSTx"), writes=bl("PB2"))
                    P.op("act", lambda e: e.activation(out=PT[:, 0, :], in_=PB[0][:, :], func=AF.Exp), reads=bl("PB0"), writes=bl("PTc"))
                    P.op("pool", lambda e: e.tensor_tensor(out=PT[:, 0, :].rearrange("p (g q) -> p g q", g=4),
                                                           in0=PT[:, 0, :].rearrange("p (g q) -> p g q", g=4),
                                                           in1=MCUR[:].unsqueeze(1).broadcast_to([128, 4, 128]), op=ALU.mult),
                         reads=bl("PTc", "CONST"), writes=bl("PTc"))
                    if T > 0:
                        P.op("act", lambda e: e.activation(out=PT[:, 1, :], in_=PB[1][:, :], func=AF.Exp), reads=bl("PB1"), writes=bl("PTp"))
                        P.op("pool", lambda e: e.tensor_tensor(out=PT[:, 1, :].rearrange("p (g q) -> p g q", g=4),
                                                               in0=PT[:, 1, :].rearrange("p (g q) -> p g q", g=4),
                                                               in1=MPREV[:].unsqueeze(1).broadcast_to([128, 4, 128]), op=ALU.mult),
                             reads=bl("PTp", "CONST"), writes=bl("PTp"))
                    P.op("act", lambda e, k=k: e.activation(out=PTM[0:16, k, :], in_=PB[2][0:16, :], func=AF.Exp), reads=bl("PB2"), writes=bl("PTM"))
                    yield 3.0
                    ptc = PT[:, 0, :].rearrange("p (a b q) -> p a b q", a=2, b=2)
                    ptp = PT[:, 1, :].rearrange("p (a b q) -> p a b q", a=2, b=2)
                    def mpv(e, k=k, j=j, T=T, ptc=ptc, ptp=ptp):
                        for hf in range(2):
                            o = PB[3][64 * hf:64 * hf + 64, 0:256]
                            e.matmul(o, lhsT=VS[:, j + 1, k * 64:(k + 1) * 64], rhs=ptc[:, :, hf, :], start=True, stop=False)
                            if T > 0:
                                e.matmul(o, lhsT=VS[:, j, k * 64:(k + 1) * 64], rhs=ptp[:, :, hf, :], start=False, stop=False)
                            i = e.matmul(o, lhsT=VM[0:33, k, :], rhs=PTM[0:33, k, :].rearrange("p (a b q) -> p a b q", a=2, b=2)[:, :, hf, :],
                                         start=False, stop=True)
                        for hf in range(2):
                            o = PB[3][64 * hf:64 * hf + 64, 256:512]
                            e.matmul(o, lhsT=ONESB[:, 0:64], rhs=ptc[:, :, hf, :], start=True, stop=False)
                            if T > 0:
                                e.matmul(o, lhsT=ONESB[:, 0:64], rhs=ptp[:, :, hf, :], start=False, stop=False)
                            i = e.matmul(o, lhsT=ONESM[0:33, 0:64], rhs=PTM[0:33, k, :].rearrange("p (a b q) -> p a b q", a=2, b=2)[:, :, hf, :],
                                         start=False, stop=True)
                        return i
                    P.op("pe", mpv, reads=bl(f"VS{j}", vprev_b, "PTc", "PTp", "PTM", "PTMx", "VM", "CONST"), writes=bl("PB3"))
                    P.op("dve", lambda e: e.reciprocal(out=RDEN[:, 0:256], in_=PB[3][:, 256:512]), reads=bl("PB3"), writes=bl("RDEN"))
                    P.op("dve", lambda e, k=k, tc=tc: e.tensor_tensor(
                        out=OST[:, 2 * k:2 * k + 2, tc], in0=PB[3][:, 0:256].rearrange("p (a q) -> p a q", a=2),
                        in1=RDEN[:, 0:256].rearrange("p (a q) -> p a q", a=2), op=ALU.mult),
                        reads=bl("PB3", "RDEN"), writes=bl(f"OST{j}"))
                    if k == 1:
                        prog["swa"] = j + 1
                    yield 2.0

        def ffn(b):
            nst = 44.0
            for fc in range(NFC):
                wb = (b * NFC + fc) % 2
                P.dma("sp", WGU[wb][:], wgu_s[fc], reads=bl(f"wgu_s{fc}"), writes=bl(f"WGU{wb}"))
                pg, pu = (4, 5) if fc % 2 == 0 else (6, 7)
                P.op("pe", lambda e, wb=wb, pg=pg: mm_group(e, PB[pg][:, :], [(WGU[wb][:, 0, kk, :], HT1[:, kk, :]) for kk in range(8)]),
                     reads=bl(f"WGU{wb}", *HT1all), writes=bl(f"PB{pg}"))
                P.op("pe", lambda e, wb=wb, pu=pu: mm_group(e, PB[pu][:, :], [(WGU[wb][:, 1, kk, :], HT1[:, kk, :]) for kk in range(8)]),
                     reads=bl(f"WGU{wb}", *HT1all), writes=bl(f"PB{pu}"))
                P.op("act", lambda e, pg=pg: e.activation(out=SGF[:], in_=PB[pg][:, :], func=AF.Tanh, scale=0.5), reads=bl(f"PB{pg}"), writes=bl("SGF"))
                P.op("dve", lambda e, pg=pg: e.scalar_tensor_tensor(out=SGF[:], in0=SGF[:], scalar=1.0, in1=PB[pg][:, :], op0=ALU.add, op1=ALU.mult),
                     reads=bl("SGF", f"PB{pg}"), writes=bl("SGF"))
                P.op("dve", lambda e, fc=fc, pu=pu: e.scalar_tensor_tensor(out=AT[:, fc, :], in0=SGF[:], scalar=0.5, in1=PB[pu][:, :], op0=ALU.mult, op1=ALU.mult),
                     reads=bl("SGF", f"PB{pu}"), writes=bl(f"AT{fc}"))
                yield (fc + 1) / nst
            for hf in range(2):
                cs = slice(hf * 512, (hf + 1) * 512)
                for s in range(11):
                    wb = ((b * 2 + hf) * 11 + s) % 3
                    P.dma("sp", WD[wb][:], wd_s[hf, s], reads=bl(f"wd_s{hf}_{s}"), writes=bl(f"WD{wb}"))
                    def mdn(e, s=s, wb=wb):
                        for sub in range(2):
                            fc = 2 * s + sub
                            for j in range(4):
                                i = e.matmul(PB[4 + j][:, :], lhsT=AT[:, fc, j * 128:(j + 1) * 128], rhs=WD[wb][:, sub, :],
                                             start=(fc == 0), stop=(fc == NFC - 1))
                        return i
                    P.op("pe", mdn, reads=bl(f"WD{wb}", f"AT{2 * s}", f"AT{2 * s + 1}"), writes=bl("PB4", "PB5", "PB6", "PB7"))
                    if s < 10:
                        yield (22 + hf * 11 + s + 1) / nst
                for j in range(4):
                    P.op("dve", lambda e, j=j, cs=cs: e.scalar_tensor_tensor(out=H[:, j, cs], in0=H[:, j, cs], scalar=ALPHA, in1=PB[4 + j][:, :],
                                                                             op0=ALU.mult, op1=ALU.add),
                         reads=bl(f"H{j}", f"PB{4 + j}"), writes=bl(f"H{j}"))
                yield (22 + hf * 11 + 11) / nst

        def ln_a(src, rb):
            i = ctr["ls"] % NLS
            ctr["ls"] += 1
            L, lb = LST[i], bl(f"LST{i}")
            P.op("dve", lambda e: e.bn_stats(out=L[:, 0:6], in_=src[:, 0:512]), reads=rb, writes=lb)
            P.op("dve", lambda e: e.bn_stats(out=L[:, 6:12], in_=src[:, 512:1024]), reads=rb, writes=lb)
            P.op("dve", lambda e: e.bn_aggr(out=L[:, 12:14], in_=L[:, 0:12]), reads=lb, writes=lb)
            P.op("dve", lambda e: e.tensor_scalar(out=L[:, 15:16], in0=L[:, 12:13], scalar1=-1.0, scalar2=None, op0=ALU.mult),
                 reads=lb, writes=lb)
            return L, lb

        def ln_b(L, lb):
            P.op("act", lambda e: e.activation(out=L[:, 14:15], in_=L[:, 13:14], func=AF.Ln, bias=LN_EPS, scale=1.0), reads=lb, writes=lb)
            P.op("act", lambda e: e.activation(out=L[:, 14:15], in_=L[:, 14:15], func=AF.Exp, scale=-0.5), reads=lb, writes=lb)
            P.op("act", lambda e: e.activation(out=L[:, 15:16], in_=L[:, 15:16], func=AF.Identity, scale=L[:, 14:15]), reads=lb, writes=lb)

        def ln_c(L, lb):
            pass

        def ln_n(t, tb, L, lb):
            P.op("act", lambda e: e.activation(out=t, in_=t, func=AF.Identity, bias=L[:, 15:16], scale=L[:, 14:15]),
                 reads=tb + lb, writes=tb)

        def ln_g(t, tb, gi):
            P.op("dve", lambda e: e.tensor_tensor(out=t, in0=t, in1=LNC[:, gi, :], op=ALU.mult), reads=tb + bl("LNC"), writes=tb)

        def ln_bias(t, tb, gi):
            P.op("pool", lambda e: e.tensor_tensor(out=t, in0=t, in1=LNC[:, gi + 1, :], op=ALU.add), reads=tb + bl("LNC"), writes=tb)

        XTB = [bl("AT0", "AT1", "AT2", "AT3"), bl("AT4", "AT5", "AT6", "AT7")]

        def post_f(b):
            items = []
            if b >= 0:
                for j in range(4):
                    items.append((H[:, j, :], bl(f"H{j}"), 4, j))
            if b + 1 < nblk:
                for j in range(2):
                    T = 4 * (b + 1) + j
                    P.dma("sp", XT[j], x[T * 128:(T + 1) * 128, :], writes=XTB[j])
                    items.append((XT[j], XTB[j], 0, None))
            yield 1
            stats = []
            for (t, tb, gi, j) in items:
                stats.append(ln_a(t, tb))
                yield 1
            for (L, lb) in stats:
                ln_b(L, lb)
            yield 1
            for (L, lb) in stats:
                ln_c(L, lb)
            yield 1
            for (t, tb, gi, j), (L, lb) in zip(items, stats):
                ln_n(t, tb, L, lb)
                yield 1
            for (t, tb, gi, j) in items:
                ln_g(t, tb, gi)
                yield 1
            for (t, tb, gi, j) in items:
                ln_bias(t, tb, gi)
                if j is not None:
                    T = 4 * b + j
                    P.dma("pool", out[T * 128:(T + 1) * 128, :], H[:, j, :], reads=bl(f"H{j}"))
                yield 1

        def tails_early(b):
            XA = [(XT[0], XTB[0]), (XT[1], XTB[1]), (XS[0][:, :], bl("XS0")), (XS[1][:, :], bl("XS1"))]
            prog["next"] = 0
            while prog["next"] < 3:
                j = prog["next"]
                if min(prog["gla"], prog["swa"]) <= j:
                    yield 0
                    continue
                tc = slice(j * 128, (j + 1) * 128)
                xs, xb = XA[j]
                pbs = (4, 5) if j % 2 == 0 else (6, 7)
                for hf in range(2):
                    cs = slice(hf * 512, (hf + 1) * 512)
                    pbi = pbs[hf]
                    P.op("pe", lambda e, tc=tc, cs=cs, pbi=pbi: mm_group(e, PB[pbi][:, :],
                         [((OST[:, kk, tc] if kk < 4 else OGT[:, kk - 4, tc]), W_OUT[:, kk, cs]) for kk in range(8)]),
                         reads=bl(f"OST{j}", f"OGT{j}", "W_OUT"), writes=bl(f"PB{pbi}"))
                    yield 0
                    P.op("dve", lambda e, j=j, cs=cs, pbi=pbi, xs=xs: e.scalar_tensor_tensor(
                        out=H[:, j, cs], in0=xs[:, cs], scalar=ALPHA, in1=PB[pbi][:, :], op0=ALU.mult, op1=ALU.add),
                        reads=xb + bl(f"PB{pbi}"), writes=bl(f"H{j}"))
                t, tb = H[:, j, :], bl(f"H{j}")
                L, lb = ln_a(t, tb)
                yield 0
                ln_b(L, lb)
                ln_n(t, tb, L, lb)
                yield 0
                ln_g(t, tb, 2)
                ln_bias(t, tb, 2)
                yield 0
                T = 4 * (b + 1) + j
                if debug:
                    P.dma("pool", dbg_h1[T * 128:(T + 1) * 128, :], H[:, j, :], reads=bl(f"H{j}"))
                transpose_tile(H[:, j, :], bl(f"H{j}"), HT1, bl(f"HT1_{j}"), j, pbs)
                prog["next"] = j + 1
                yield 0

        def tails(b, start=0):
            XA = [(XT[0], XTB[0]), (XT[1], XTB[1]), (XS[0][:, :], bl("XS0")), (XS[1][:, :], bl("XS1"))]
            for j in range(start, 4):
                tc = slice(j * 128, (j + 1) * 128)
                for hf in range(2):
                    cs = slice(hf * 512, (hf + 1) * 512)
                    pbi = 2 * j + hf
                    P.op("pe", lambda e, tc=tc, cs=cs, pbi=pbi: mm_group(e, PB[pbi][:, :],
                         [((OST[:, kk, tc] if kk < 4 else OGT[:, kk - 4, tc]), W_OUT[:, kk, cs]) for kk in range(8)]),
                         reads=bl(f"OST{j}", f"OGT{j}", "W_OUT"), writes=bl(f"PB{pbi}"))
            for j in range(start, 4):
                xs, xb = XA[j]
                for hf in range(2):
                    cs = slice(hf * 512, (hf + 1) * 512)
                    pbi = 2 * j + hf
                    P.op("dve", lambda e, j=j, cs=cs, pbi=pbi, xs=xs: e.scalar_tensor_tensor(
                        out=H[:, j, cs], in0=xs[:, cs], scalar=ALPHA, in1=PB[pbi][:, :], op0=ALU.mult, op1=ALU.add),
                        reads=xb + bl(f"PB{pbi}"), writes=bl(f"H{j}"))
            items = [(H[:, j, :], bl(f"H{j}"), 2) for j in range(start, 4)]
            stats = [ln_a(t, tb) for (t, tb, gi) in items]
            for (L, lb) in stats:
                ln_b(L, lb)
            for (L, lb) in stats:
                ln_c(L, lb)
            for (t, tb, gi), (L, lb) in zip(items, stats):
                ln_n(t, tb, L, lb)
            for (t, tb, gi) in items:
                ln_g(t, tb, gi)
            for (t, tb, gi) in items:
                ln_bias(t, tb, gi)
            for j in range(start, 4):
                T = 4 * (b + 1) + j
                if debug:
                    P.dma("pool", dbg_h1[T * 128:(T + 1) * 128, :], H[:, j, :], reads=bl(f"H{j}"))
                transpose_tile(H[:, j, :], bl(f"H{j}"), HT1, bl(f"HT1_{j}"), j, (2 * j, 2 * j + 1))

        def run(gen):
            for _ in gen:
                pass

        NM_UNITS = 126.0
        F_FRAC = 0.6

        def step(g):
            try:
                next(g)
                return True
            except StopIteration:
                return False

        def interleave(gm, gf, extras):
            nm = nf = 0
            fl = gf is not None
            el = [True] * len(extras)
            if gm is not None:
                for w in gm:
                    nm += float(w)
                    while fl and nf / 44.0 <= nm / (F_FRAC * NM_UNITS):
                        fl = step(gf)
                        nf += 1
                    if not fl:
                        for i, g in enumerate(extras):
                            for _k in range(3 if gf is not None else 1):
                                if el[i]:
                                    el[i] = step(g)
            while fl:
                fl = step(gf)
            for i, g in enumerate(extras):
                while el[i]:
                    el[i] = step(g)

        def chain(*gens):
            for g in gens:
                yield from g

        interleave(mixer_head(0), None, [casts(), chain(post_f(-1), tails_early(-1))])
        tails(-1, prog["next"])
        ckpt(2)
        for b in range(nblk):
            if b + 1 < nblk:
                interleave(mixer_head(b + 1), ffn(b), [chain(post_f(b), tails_early(b))])
                tails(b, prog["next"])
            else:
                interleave(None, ffn(b), [post_f(b)])
        print("sbuf bytes remaining", nc.sbuf_bytes_remaining)
        P.finish()
        P.emit()
    return nc


WEIGHT_KEYS = ["meta_tokens", "ln_in_g", "ln_in_b", "w_in", "b_in", "w_gate_lr2", "b_gate_lr2", "attn_sinks",
               "gla_norm_g", "w_out", "ln1_g", "ln1_b", "w_ffn_gate", "w_ffn_up", "w_ffn_down", "ln2_g", "ln2_b"]


def make_in_map(inputs, bi):
    f = lambda a: np.ascontiguousarray(np.asarray(a, dtype=np.float32))
    m = {"x": f(inputs["x"][bi])}
    for k in WEIGHT_KEYS:
        a = f(inputs[k])
        if k not in ("meta_tokens", "ln_in_g", "ln_in_b"):
            a = a[0]
        m[k] = np.ascontiguousarray(a)
    m.update(host_consts())
    return m


def kernel(**inputs):
    x = np.asarray(inputs["x"])
    Bn, S, _ = x.shape
    nc = build(S // TB)
    in_maps = [make_in_map(inputs, bi) for bi in range(Bn)]
    res = run_bass_kernel_spmd(nc, in_maps, core_ids=list(range(Bn)))
    return np.stack([np.asarray(r["out"], dtype=np.float32) for r in res.results], axis=0)
```

```python
import numpy as np
from contextlib import ExitStack
import ml_dtypes
import concourse.bass as bass
import concourse.mybir as mybir
from concourse.bass_utils import run_bass_kernel_spmd

F32 = mybir.dt.float32
BF16 = mybir.dt.bfloat16
AF = mybir.ActivationFunctionType
ALU = mybir.AluOpType

D = 1024
NM = 16
DIN = 2320
DFF = 2816
NFC = DFF // 128
ALPHA = 2.0 ** 0.25
LN_EPS = 1e-5
RMS_EPS = 1e-6
TB = 512
LN8 = float(np.log(0.125))


class Tk:
    __slots__ = ("sem", "val", "key")

    def __init__(self, sem, val, key):
        self.sem, self.val, self.key = sem, val, key


class Buf:
    def __init__(self, name):
        self.name = name
        self.w = None
        self.r = {}


class Prog:
    ENG = ("pe", "act", "dve", "pool", "sp")
    R = 8

    def __init__(self, nc, es):
        self.nc = nc
        self.q = {e: [] for e in self.ENG}
        self.cnt = {e: 0 for e in self.ENG}
        self.waited = {e: {} for e in self.ENG}
        self.sem = {e: es.enter_context(nc.semaphore("s_" + e)) for e in ("pe", "act", "dve", "pool")}
        self.ring = {qn: [es.enter_context(nc.semaphore(f"d_{qn}{i}")) for i in range(self.R)]
                     for qn in ("sp", "pool")}
        self.dma_n = {"sp": 0, "pool": 0}

    def _deps(self, eng, reads, writes):
        ts = []
        for b in reads:
            if b.w is not None:
                ts.append(b.w)
            if b.name.startswith("PB"):
                ts.extend(t for e2, t in b.r.items() if e2 != eng)
        for b in writes:
            if b.w is not None:
                ts.append(b.w)
            ts.extend(b.r.values())
        best = {}
        for t in ts:
            if self.waited[eng].get(t.key, 0) < t.val:
                if t.key not in best or best[t.key].val < t.val:
                    best[t.key] = t
        for t in best.values():
            self.waited[eng][t.key] = t.val
        return list(best.values())

    def _mark(self, eng, tk, reads, writes):
        for b in reads:
            b.r[eng] = tk
        for b in writes:
            b.w = tk
            b.r = {}

    def op(self, eng, fn, reads=(), writes=()):
        waits = self._deps(eng, reads, writes)
        self.cnt[eng] += 1
        tk = Tk(self.sem[eng], self.cnt[eng], eng)
        self.q[eng].append((waits, fn, (self.sem[eng], 1)))
        self._mark(eng, tk, reads, writes)
        return tk

    def dma(self, qn, out, in_, reads=(), writes=()):
        waits = self._deps(qn, reads, writes)
        j = self.dma_n[qn]
        self.dma_n[qn] += 1
        slot, val = j % self.R, 16 * (j // self.R + 1)
        key = f"{qn}{slot}"
        if val > 16 and self.waited[qn].get(key, 0) < val - 16:
            waits.append(Tk(self.ring[qn][slot], val - 16, key))
            self.waited[qn][key] = val - 16
        tk = Tk(self.ring[qn][slot], val, key)
        self.q[qn].append((waits, lambda e: e.dma_start(out=out, in_=in_), (self.ring[qn][slot], 16)))
        self._mark(qn, tk, reads, writes)
        return tk

    def handoff(self, src, dst):
        best = {}
        for b in src:
            for t in ([b.w] if b.w is not None else []) + list(b.r.values()):
                if t.key not in best or best[t.key].val < t.val:
                    best[t.key] = t
        for b in dst:
            for k, t in best.items():
                b.r["h_" + k] = t

    def finish(self):
        waits = []
        for qn in ("sp", "pool"):
            n = self.dma_n[qn]
            for slot in range(self.R):
                cnt = (n - slot + self.R - 1) // self.R if n > slot else 0
                if cnt > 0:
                    waits.append(Tk(self.ring[qn][slot], 16 * cnt, f"{qn}{slot}"))
        self.q["sp"].append((waits, None, None))

    def emit(self):
        nc = self.nc
        with nc.Block() as block:
            def replay(name):
                def f(e):
                    for waits, fn, inc in self.q[name]:
                        for t in waits:
                            e.wait_ge(t.sem, t.val)
                        if fn is not None:
                            ins = fn(e)
                            ins.then_inc(inc[0], inc[1])
                return f
            block.tensor(replay("pe"))
            block.scalar(replay("act"))
            block.vector(replay("dve"))
            block.gpsimd(replay("pool"))
            block.sync(replay("sp"))


def host_consts():
    bf = ml_dtypes.bfloat16
    j = np.arange(128)[:, None]
    i = np.arange(128)[None, :]
    c = {}
    c["ident"] = np.eye(128, dtype=np.float32)
    c["tri1"] = np.where(j <= i, -1.0 / 16.0, 0.0).astype(np.float32)
    c["tri2"] = np.where(j > i, -1.0 / 16.0, 0.0).astype(np.float32)
    c["masku"] = (j <= i).astype(np.float32).astype(bf)
    c["mcur"] = (j <= i).astype(np.float32).astype(bf)
    c["mprev"] = (j > i).astype(np.float32).astype(bf)
    slopes = 2.0 ** (-8.0 * (np.arange(8) + 1) / 8.0)
    a = np.arange(128)
    qrows = np.zeros((3, 8, TB), np.float32)
    for h in range(8):
        qrows[0, h, :] = slopes[h]
        qrows[1, h, :] = -slopes[h] * np.tile(a, TB // 128)
        qrows[2, h, :] = -128.0 * slopes[h]
    c["qrows"] = qrows.astype(bf)
    kb = np.zeros((3, 2, 640), np.float32)
    kb[0, :, :] = np.tile(a, 5)[None, :]
    kb[1] = 1.0
    kb[2] = 1.0
    c["kbrows"] = kb.astype(bf)
    km = np.zeros((3, 2, 32, 16), np.float32)
    km[0] = (np.arange(16) - 16.0)[None, None, :]
    km[1] = 1.0
    km[2] = np.arange(32)[None, :, None]
    c["kmrows"] = km.astype(bf)
    om = np.zeros((33, 128), np.float32)
    om[0:16] = 1.0
    om[32] = 1.0
    c["onesm"] = om.astype(bf)
    for k, v in c.items():
        assert np.all(np.isfinite(v.astype(np.float32)))
    return c


CONST_SPECS = [("ident", [128, 128], F32), ("tri1", [128, 128], F32), ("tri2", [128, 128], F32),
               ("masku", [128, 128], BF16), ("mcur", [128, 128], BF16), ("mprev", [128, 128], BF16),
               ("qrows", [3, 8, TB], BF16), ("kbrows", [3, 2, 640], BF16),
               ("kmrows", [3, 2, 32, 16], BF16), ("onesm", [33, 128], BF16)]


class _Stop(Exception):
    pass


def build(nblk, debug=False, stage=99):
    try:
        return _build(nblk, debug, stage)
    except _Stop as e:
        return e.args[0]


def _build(nblk, debug=False, stage=99):
    S = nblk * TB
    nc = bass.Bass("TRN2", target_bir_lowering=False)
    di = lambda n, s, d=F32: nc.dram_tensor(n, s, d, kind="ExternalInput").ap()
    x = di("x", [S, D])
    meta = di("meta_tokens", [NM, D])
    lnv = [di(n, [D]) for n in ("ln_in_g", "ln_in_b", "ln1_g", "ln1_b", "ln2_g", "ln2_b")]
    w_in = di("w_in", [D, DIN])
    b_in = di("b_in", [DIN])
    wg2 = di("w_gate_lr2", [16, 256])
    bg2 = di("b_gate_lr2", [256])
    sinks = di("attn_sinks", [8])
    gnorm = di("gla_norm_g", [128])
    w_out = di("w_out", [D, D])
    w_fg = di("w_ffn_gate", [D, DFF])
    w_fu = di("w_ffn_up", [D, DFF])
    w_fd = di("w_ffn_down", [DFF, D])
    cst = {n: di(n, s, d) for n, s, d in CONST_SPECS}
    out = nc.dram_tensor("out", [S, D], F32, kind="ExternalOutput").ap()
    if debug:
        dbg_h0 = nc.dram_tensor("dbg_h0", [S, D], F32, kind="ExternalOutput").ap()
        dbg_h1 = nc.dram_tensor("dbg_h1", [S, D], F32, kind="ExternalOutput").ap()
        dbg_o = nc.dram_tensor("dbg_o", [nblk, 128, 8, TB], BF16, kind="ExternalOutput").ap()
    wgu_s = nc.dram_tensor("wgu_s", [NFC, 128, 2, 8, 128], BF16).ap()
    wd_s = nc.dram_tensor("wd_s", [2, 11, 128, 2, 512], BF16).ap()

    with ExitStack() as es:
        P = Prog(nc, es)
        sb = lambda n, s, d: es.enter_context(nc.sbuf_tensor(n, s, d))
        W_IN = sb("W_IN", [128, 8, DIN], BF16)
        W_OUT = sb("W_OUT", [128, 8, D], BF16)
        WGU = [sb(f"WGU{i}", [128, 2, 8, 128], BF16) for i in range(2)]
        WD = [sb(f"WD{i}", [128, 2, 512], BF16) for i in range(3)]
        LNC = sb("LNC", [128, 6, D], F32)
        BTOK = sb("BTOK", [128, 896], F32)
        IDENT = sb("IDENT", [128, 128], F32)
        TRI1 = sb("TRI1", [128, 128], F32)
        TRI2 = sb("TRI2", [128, 128], F32)
        ONESF = sb("ONESF", [128, 128], F32)
        ONESB = sb("ONESB", [128, 128], BF16)
        ONESM = sb("ONESM", [33, 128], BF16)
        WG2 = sb("WG2", [32, 256], F32)
        MASKU = sb("MASKU", [128, 128], BF16)
        MCUR = sb("MCUR", [128, 128], BF16)
        MPREV = sb("MPREV", [128, 128], BF16)
        BCOL = sb("BCOL", [128, 32], F32)
        NEGH = sb("NEGH", [128, 1], F32)
        LNCOL = sb("LNCOL", [128, 32], F32)
        SINK32 = sb("SINK32", [33, 8], F32)
        NXS = 2
        XS = [sb(f"XS{i}", [128, D], F32) for i in range(NXS)]
        H = sb("H", [128, 4, D], F32)
        HT0 = sb("HT0", [128, 8, TB], BF16)
        HT1 = sb("HT1", [128, 8, TB], BF16)
        QST = sb("QST", [67, 8, TB], BF16)
        KST = sb("KST", [67, 2, 640], BF16)
        VS = sb("VS", [128, 5, 128], BF16)
        AT = sb("AT", [128, NFC, TB], BF16)
        QTT = sb("QTT", [64, 4, 128], BF16)
        KTT = sb("KTT", [64, 4, 128], BF16)
        KH = sb("KH", [128, 256], BF16)
        VG = sb("VG", [128, 512], BF16)
        GLR = sb("GLR", [32, TB], F32)
        OST = sb("OST", [128, 4, TB], BF16)
        OGT = sb("OGT", [128, 4, TB], BF16)
        KMT = sb("KMT", [67, 2, 32, 16], BF16)
        VM = sb("VM", [33, 2, 64], BF16)
        PT = sb("PT", [128, 2, TB], BF16)
        PTM = sb("PTM", [33, 2, TB], BF16)
        GB = sb("GB", [128, TB], F32)
        EX = GB[:, 0:256]
        SPL = GB[:, 256:512]
        EB2 = GB[0:64, :].rearrange("p (h c) -> p h c", h=4)
        SGM = GB
        EB1 = sb("EB1", [64, 4, 128], F32)
        EBL = sb("EBL", [64, 4], F32)
        AM = sb("AM", [128, 4, 128], BF16)
        S32 = sb("S32", [64, 4, 128], F32)
        SBF = sb("SBF", [64, 4, 128], BF16)
        SQ = sb("SQ", [128, TB], F32)
        KTMP = SQ[:, 0:256]
        ER = SQ[:, 256:512]
        OG32 = SQ
        RSTD = sb("RSTD", [128, TB], F32)
        RDEN = sb("RDEN", [128, 256], F32)
        SGF = sb("SGF", [128, TB], F32)
        NLS = 12
        LST = [sb(f"LST{i}", [128, 16], F32) for i in range(NLS)]
        PB = [es.enter_context(nc.psum_tensor(f"PB{i}", [128, 512], F32)) for i in range(8)]
        HMT = AT[:, 0, 0:128].rearrange("p (k t) -> p k t", k=8)
        XT = [AT[:, 4 * i:4 * i + 4, :].bitcast(F32).rearrange("p a b -> p (a b)") for i in range(2)]

        B = {}
        def bufs(*names):
            for n in names:
                B[n] = Buf(n)
        bufs("W_IN", "W_OUT", "LNC", "BTOK", "CONST", "WG2", "BCOL", "SINK32", "QSTx", "KSTx", "KMT", "KMTx",
             "VM", "PTMx", "GLR", "GB", "EB1", "EBL", "AM", "S32", "SBF", "SQ", "RSTD", "RDEN", "SGF", "PTM", "KSTp", "VSp",
             "PTc", "PTp", "QTT", "KTT", "KH", "VG")
        for i in range(NXS):
            bufs(f"XS{i}")
        for i in range(NLS):
            bufs(f"LST{i}")
        for i in range(2):
            bufs(f"WGU{i}")
        for i in range(4):
            bufs(f"WD{i}", f"H{i}", f"HT0_{i}", f"HT1_{i}", f"QST{i}", f"KST{i}", f"VS{i}", f"OST{i}", f"OGT{i}")
        for i in range(8):
            bufs(f"PB{i}")
        for i in range(NFC):
            bufs(f"AT{i}", f"wgu_s{i}")
        for i in range(11):
            bufs(f"wd_s0_{i}", f"wd_s1_{i}")

        def bl(*names):
            return [B[n] for n in names]

        P.dma("pool", W_IN[:], w_in.rearrange("(k p) n -> p k n", p=128), writes=bl("W_IN"))
        P.dma("sp", XS[0][0:16, :], meta, writes=bl("XS0"))
        for i in range(6):
            P.dma("sp", LNC[:, i, :], lnv[i].partition_broadcast(128), writes=bl("LNC"))
        for (n, t) in (("ident", IDENT), ("tri1", TRI1), ("tri2", TRI2), ("masku", MASKU), ("mcur", MCUR),
                       ("mprev", MPREV), ("onesm", ONESM)):
            P.dma("sp", t[:], cst[n], writes=bl("CONST"))
        P.dma("sp", QST[64:67, :, :], cst["qrows"], writes=bl("QSTx"))
        P.dma("sp", KST[64:67, :, :], cst["kbrows"], writes=bl("KSTx"))
        P.dma("sp", KMT[64:67, :, :, :], cst["kmrows"], writes=bl("KMTx"))
        P.dma("sp", BTOK[:, 0:768], b_in[1024:1792].partition_broadcast(128), writes=bl("BTOK"))
        P.dma("sp", BTOK[:, 768:896], b_in[640:768].partition_broadcast(128), writes=bl("BTOK"))
        P.op("pool", lambda e: e.memset(WG2[:], 0.0), writes=bl("WG2"))
        P.dma("sp", WG2[0:16, :], wg2, writes=bl("WG2"))
        P.dma("sp", WG2[16:17, :], bg2.rearrange("(o n) -> o n", o=1), writes=bl("WG2"))
        P.op("pool", lambda e: e.memset(BCOL[:], 0.0), writes=bl("BCOL"))
        col = lambda off, n: b_in[off:off + n].rearrange("(p o) -> p o", o=1)
        for h in range(8):
            P.dma("sp", BCOL[0:64, h:h + 1], col(64 * h, 64), writes=bl("BCOL"))
        for k in range(2):
            P.dma("sp", BCOL[0:64, 8 + k:9 + k], col(512 + 64 * k, 64), writes=bl("BCOL"))
        for hh in range(4):
            P.dma("sp", BCOL[0:64, 10 + hh:11 + hh], col(768 + 64 * hh, 64), writes=bl("BCOL"))
            P.dma("sp", BCOL[0:64, 14 + hh:15 + hh], col(1024 + 64 * hh, 64), writes=bl("BCOL"))
            P.dma("sp", BCOL[:, 18 + hh:19 + hh], col(1792 + 128 * hh, 128), writes=bl("BCOL"))
        P.dma("sp", BCOL[0:16, 22:23], col(2304, 16), writes=bl("BCOL"))
        P.dma("sp", BCOL[:, 23:24], gnorm.rearrange("(p o) -> p o", o=1), writes=bl("BCOL"))
        P.op("act", lambda e: e.mul(out=BCOL[0:64, 0:8], in_=BCOL[0:64, 0:8], mul=0.125), reads=bl("BCOL"), writes=bl("BCOL"))
        P.op("act", lambda e: e.mul(out=BCOL[:, 24:28], in_=BCOL[:, 18:22], mul=0.5), reads=bl("BCOL"), writes=bl("BCOL"))
        P.op("act", lambda e: e.mul(out=BCOL[:, 28:29], in_=BCOL[:, 23:24], mul=0.5), reads=bl("BCOL"), writes=bl("BCOL"))
        P.op("dve", lambda e: e.memset(NEGH[:], -0.5), writes=bl("CONST"))
        for vi, base in ((0, 0), (1, 8), (2, 16), (3, 24)):
            for kk in range(8):
                P.dma("sp", LNCOL[:, base + kk:base + kk + 1], lnv[vi][kk * 128:(kk + 1) * 128].rearrange("(p o) -> p o", o=1),
                      writes=bl("BCOL"))
        P.dma("sp", SINK32[32:33, :], sinks.rearrange("(o n) -> o n", o=1), writes=bl("SINK32"))
        P.op("act", lambda e: e.activation(out=SINK32[32:33, :], in_=SINK32[32:33, :], func=AF.Exp),
             reads=bl("SINK32"), writes=bl("SINK32"))
        P.op("dve", lambda e: e.memset(ONESF[:], 1.0), writes=bl("CONST"))
        P.op("dve", lambda e: e.memset(ONESB[:], 1.0), writes=bl("CONST"))
        P.op("dve", lambda e: e.memset(GLR[:], 1.0), writes=bl("GLR"))
        P.op("dve", lambda e: e.memset(PTM[:], 0.0), writes=bl("PTMx"))
        P.op("dve", lambda e: e.memset(VM[:], 0.0), writes=bl("VM"))
        for h in range(8):
            k, g = h // 4, h % 4
            P.op("dve", (lambda k, g, h: lambda e: e.tensor_scalar(
                out=PTM[32:33, k, g * 128:(g + 1) * 128], in0=ONESF[32:33, :], scalar1=SINK32[32:33, h:h + 1],
                scalar2=None, op0=ALU.mult))(k, g, h), reads=bl("SINK32", "CONST"), writes=bl("PTMx"))
        P.dma("pool", W_OUT[:], w_out.rearrange("(k p) n -> p k n", p=128), writes=bl("W_OUT"))
        def casts_more():
            return iter(())

        def casts():
            for fc in range(NFC):
                for m, w in enumerate((w_fg, w_fu)):
                    P.dma("pool", wgu_s[fc, :, m, :, :], w[:, fc * 128:(fc + 1) * 128].rearrange("(k p) n -> p k n", p=128),
                          writes=bl(f"wgu_s{fc}"))
                yield 1
            for hf in range(2):
                for s in range(11):
                    P.dma("pool", wd_s[hf, s], w_fd[s * 256:(s + 1) * 256, hf * 512:(hf + 1) * 512]
                          .rearrange("(f p) n -> p f n", p=128), writes=bl(f"wd_s{hf}_{s}"))
                    yield 1

        def ckpt(k):
            if stage == k:
                P.finish()
                P.emit()
                raise _Stop(nc)

        ckpt(0)
        ctr = {"ls": 0, "xs": 0}

        def layer_norm_stats(src, np_, rb):
            i = ctr["ls"] % NLS
            ctr["ls"] += 1
            L, lb = LST[i], bl(f"LST{i}")
            P.op("dve", lambda e: e.bn_stats(out=L[0:np_, 0:6], in_=src[:, 0:512]), reads=rb, writes=lb)
            P.op("dve", lambda e: e.bn_stats(out=L[0:np_, 6:12], in_=src[:, 512:1024]), reads=rb, writes=lb)
            P.op("dve", lambda e: e.bn_aggr(out=L[0:np_, 12:14], in_=L[0:np_, 0:12]), reads=lb, writes=lb)
            P.op("dve", lambda e: e.tensor_scalar(out=L[0:np_, 15:16], in0=L[0:np_, 12:13], scalar1=-1.0, scalar2=None, op0=ALU.mult),
                 reads=lb, writes=lb)
            P.op("act", lambda e: e.activation(out=L[0:np_, 14:15], in_=L[0:np_, 13:14], func=AF.Ln, bias=LN_EPS, scale=1.0), reads=lb, writes=lb)
            P.op("act", lambda e: e.activation(out=L[0:np_, 14:15], in_=L[0:np_, 14:15], func=AF.Exp, scale=-0.5), reads=lb, writes=lb)
            P.op("act", lambda e: e.activation(out=L[0:np_, 15:16], in_=L[0:np_, 15:16], func=AF.Identity, scale=L[0:np_, 14:15]), reads=lb, writes=lb)
            return L, lb

        def layer_norm_apply(src, dst, np_, gi, rb, wb, L, lb):
            P.op("act", lambda e: e.activation(out=dst, in_=src, func=AF.Identity, bias=L[0:np_, 15:16], scale=L[0:np_, 14:15]),
                 reads=rb + lb, writes=wb)
            P.op("dve", lambda e: e.tensor_tensor(out=dst, in0=dst, in1=LNC[0:np_, gi, :], op=ALU.mult),
                 reads=wb + bl("LNC"), writes=wb)
            P.op("pool", lambda e: e.tensor_tensor(out=dst, in0=dst, in1=LNC[0:np_, gi + 1, :], op=ALU.add),
                 reads=wb + bl("LNC"), writes=wb)

        def next_xs():
            i = ctr["xs"] % NXS
            ctr["xs"] += 1
            return XS[i], bl(f"XS{i}")

        def ln_in_tile(T):
            xs, xb = XS[T % 2], bl(f"XS{T % 2}")
            P.dma("sp", xs[:], x[T * 128:(T + 1) * 128, :], writes=xb)
            L, lb = layer_norm_stats(xs[:, :], 128, xb)
            P.op("act", lambda e: e.activation(out=xs[:, :], in_=xs[:, :], func=AF.Identity, bias=L[:, 15:16], scale=L[:, 14:15]),
                 reads=xb + lb, writes=xb)
            return xs, xb

        def ln_in_affine(xs, xb):
            P.op("dve", lambda e: e.tensor_tensor(out=xs[:, :], in0=xs[:, :], in1=LNC[:, 0, :], op=ALU.mult), reads=xb + bl("LNC"), writes=xb)
            P.op("pool", lambda e: e.tensor_tensor(out=xs[:, :], in0=xs[:, :], in1=LNC[:, 1, :], op=ALU.add), reads=xb + bl("LNC"), writes=xb)

        def transpose_tile(src, srcb, HTd, dstb, j, banks, gcol=None):
            for half, eng in ((0, "act"), (1, "dve")):
                pb = banks[half]
                def tr(e, half=half, pb=pb):
                    for k in range(4):
                        kk = half * 4 + k
                        i = e.transpose(out=PB[pb][:, k * 128:(k + 1) * 128], in_=src[:, kk * 128:(kk + 1) * 128],
                                        identity=IDENT[:])
                    return i
                P.op("pe", tr, reads=srcb + bl("CONST"), writes=bl(f"PB{pb}"))
                if gcol is not None:
                    for k in range(4):
                        kk = half * 4 + k
                        dst1 = HTd[:, kk, j * 128:(j + 1) * 128]
                        ps1 = PB[pb][:, k * 128:(k + 1) * 128]
                        gA = LNCOL[:, gcol + kk:gcol + kk + 1]
                        bA = LNCOL[:, gcol + 8 + kk:gcol + 9 + kk]
                        if eng == "act":
                            P.op("act", lambda e, dst1=dst1, ps1=ps1, gA=gA, bA=bA: e.activation(out=dst1, in_=ps1, func=AF.Identity, bias=bA, scale=gA),
                                 reads=bl(f"PB{pb}", "BCOL"), writes=dstb)
                        else:
                            P.op("dve", lambda e, dst1=dst1, ps1=ps1, gA=gA, bA=bA: e.tensor_scalar(out=dst1, in0=ps1, scalar1=gA, scalar2=bA,
                                                                                                   op0=ALU.mult, op1=ALU.add),
                                 reads=bl(f"PB{pb}", "BCOL"), writes=dstb)
                    continue
                dst = HTd[:, half * 4:(half + 1) * 4, j * 128:(j + 1) * 128]
                psv = PB[pb][:].rearrange("p (k t) -> p k t", k=4)
                if eng == "act":
                    P.op("act", lambda e, dst=dst, psv=psv: e.activation(out=dst, in_=psv, func=AF.Copy),
                         reads=bl(f"PB{pb}"), writes=dstb)
                else:
                    P.op("dve", lambda e, dst=dst, psv=psv: e.tensor_copy(out=dst, in_=psv),
                         reads=bl(f"PB{pb}"), writes=dstb)

        def mm_group(e, out_ap, pairs):
            n = len(pairs)
            for idx, (l, r) in enumerate(pairs):
                i = e.matmul(out_ap, lhsT=l, rhs=r, start=(idx == 0), stop=(idx == n - 1))
            return i

        HT0all = [f"HT0_{j}" for j in range(4)]
        HT1all = [f"HT1_{j}" for j in range(4)]

        def gate_pipeline(ntok, tcol, pbi):
            np_ = ntok
            pbn = f"PB{pbi}"
            P.op("pe", lambda e: e.matmul(PB[pbi][0:np_, 0:256], lhsT=GLR[0:32, tcol:tcol + ntok], rhs=WG2[:, :],
                                          start=True, stop=True), reads=bl("GLR", "WG2"), writes=bl(pbn))
            P.op("act", lambda e: e.activation(out=EX[0:np_, :], in_=PB[pbi][0:np_, 0:256], func=AF.Exp, scale=-1.0),
                 reads=bl(pbn), writes=bl("GB"))
            P.op("act", lambda e: e.activation(out=SPL[0:np_, :], in_=EX[0:np_, :], func=AF.Ln, bias=1.0),
                 reads=bl("GB"), writes=bl("GB"))
            P.op("pe", lambda e: e.matmul(PB[pbi][0:np_, 256:512], lhsT=TRI2[0:np_, 0:np_], rhs=SPL[0:np_, :],
                                          start=True, stop=True), reads=bl("GB", "CONST"), writes=bl(pbn))
            P.op("act", lambda e: e.activation(out=ER[0:np_, :], in_=PB[pbi][0:np_, 256:512], func=AF.Exp),
                 reads=bl(pbn), writes=bl("SQ"))

        L, lb = layer_norm_stats(XS[0][0:16, :], 16, bl("XS0"))
        layer_norm_apply(XS[0][0:16, :], XS[0][0:16, :], 16, 0, bl("XS0"), bl("XS0"), L, lb)
        def trm(e):
            for kk in range(8):
                i = e.transpose(out=PB[0][:, kk * 16:(kk + 1) * 16], in_=XS[0][0:16, kk * 128:(kk + 1) * 128],
                                identity=IDENT[0:16, 0:16])
            return i
        P.op("pe", trm, reads=bl("XS0", "CONST"), writes=bl("PB0"))
        P.op("act", lambda e: e.activation(out=HMT, in_=PB[0][:, 0:128].rearrange("p (k t) -> p k t", k=8), func=AF.Copy),
             reads=bl("PB0"), writes=bl("AT0"))
        P.op("pe", lambda e: mm_group(e, PB[2][0:16, 0:512], [(HMT[:, kk, :], W_IN[:, kk, 1024:1536]) for kk in range(8)]),
             reads=bl("AT0", "W_IN"), writes=bl("PB2"))
        def mmeta3(e):
            mm_group(e, PB[3][0:16, 0:256], [(HMT[:, kk, :], W_IN[:, kk, 1536:1792]) for kk in range(8)])
            return mm_group(e, PB[3][0:16, 256:384], [(HMT[:, kk, :], W_IN[:, kk, 640:768]) for kk in range(8)])
        P.op("pe", mmeta3, reads=bl("AT0", "W_IN"), writes=bl("PB3"))
        def mmeta4(e):
            for k in range(2):
                mm_group(e, PB[1][0:64, k * 16:(k + 1) * 16],
                         [(W_IN[:, kk, 512 + 64 * k:576 + 64 * k], HMT[:, kk, :]) for kk in range(8)])
            return mm_group(e, PB[1][0:16, 64:80], [(W_IN[:, kk, 2304:2320], HMT[:, kk, :]) for kk in range(8)])
        P.op("pe", mmeta4, reads=bl("AT0", "W_IN"), writes=bl("PB1"))
        for k in range(2):
            P.op("act", (lambda k: lambda e: e.activation(
                out=KMT[0:64, k, :, :], in_=PB[1][0:64, k * 16:(k + 1) * 16].unsqueeze(1).broadcast_to([64, 32, 16]),
                func=AF.Identity, bias=BCOL[0:64, 8 + k:9 + k], scale=1.0))(k), reads=bl("PB1", "BCOL"), writes=bl("KMT"))
        P.op("act", lambda e: e.activation(out=GLR[0:16, 0:16], in_=PB[1][0:16, 64:80], func=AF.Identity,
                                           bias=BCOL[0:16, 22:23], scale=1.0), reads=bl("PB1", "BCOL"), writes=bl("GLR"))
        P.op("dve", lambda e: e.tensor_tensor(out=VM[0:16, :, :], in0=PB[3][0:16, 256:384].rearrange("p (k d) -> p k d", k=2),
                                              in1=BTOK[0:16, 768:896].rearrange("p (k d) -> p k d", k=2), op=ALU.add),
             reads=bl("PB3", "BTOK"), writes=bl("VM"))
        P.op("dve", lambda e: e.tensor_tensor(out=VG[0:16, 0:256], in0=PB[2][0:16, 256:512], in1=BTOK[0:16, 256:512], op=ALU.add),
             reads=bl("PB2", "BTOK"), writes=bl("VG"))
        P.op("dve", lambda e: e.tensor_tensor(out=VG[0:16, 256:512], in0=PB[3][0:16, 0:256], in1=BTOK[0:16, 512:768], op=ALU.add),
             reads=bl("PB3", "BTOK"), writes=bl("VG"))
        P.op("dve", lambda e: e.tensor_tensor(out=KTMP[0:16, :], in0=PB[2][0:16, 0:256], in1=BTOK[0:16, 0:256], op=ALU.add),
             reads=bl("PB2", "BTOK"), writes=bl("SQ"))
        gate_pipeline(16, 0, 0)
        P.op("dve", lambda e: e.tensor_tensor(out=KH[0:16, :], in0=KTMP[0:16, :], in1=ER[0:16, :], op=ALU.mult),
             reads=bl("SQ"), writes=bl("KH"))
        def mstate0(e):
            for hh in range(4):
                i = e.matmul(PB[2][0:64, hh * 128:(hh + 1) * 128], lhsT=KH[0:16, hh * 64:(hh + 1) * 64],
                             rhs=VG[0:16, hh * 128:(hh + 1) * 128], start=True, stop=True)
            return i
        P.op("pe", mstate0, reads=bl("KH", "VG"), writes=bl("PB2"))
        P.op("dve", lambda e: e.tensor_copy(out=S32[:], in_=PB[2][0:64, :].rearrange("p (h c) -> p h c", h=4)),
             reads=bl("PB2"), writes=bl("S32"))
        P.op("act", lambda e: e.activation(out=SBF[:], in_=S32[:], func=AF.Copy), reads=bl("S32"), writes=bl("SBF"))
        ckpt(1)

        def merge(g1, g2):
            gens = [g1, g2]
            live = [True, True]
            i = 0
            while live[0] or live[1]:
                if live[i]:
                    try:
                        yield next(gens[i])
                    except StopIteration:
                        live[i] = False
                i ^= 1

        def mixer_head(b):
            def tick():
                return 1
            if b > 0:
                P.op("act", lambda e: e.activation(out=KST[0:64, :, 0:128], in_=KST[0:64, :, 512:640], func=AF.Copy),
                     reads=bl("KST3"), writes=bl("KSTp"))
                P.op("dve", lambda e: e.tensor_copy(out=VS[:, 0, :], in_=VS[:, 4, :]), reads=bl("VS3"), writes=bl("VSp"))
            for j in range(4):
                T = 4 * b + j
                xs, xb = ln_in_tile(T)
                yield 6.0
                transpose_tile(xs, xb, HT0, bl(f"HT0_{j}"), j, (0, 1), gcol=0)
                if j >= 2 or debug:
                    ln_in_affine(xs, xb)
                if debug:
                    P.dma("pool", dbg_h0[T * 128:(T + 1) * 128, :], xs[:, :], reads=xb)
                yield 1.5
            yield from m_qk(b)
            yield from merge(m_gla(b), m_swa(b))
            if debug:
                P.dma("pool", dbg_o[b, :, 0:4, :], OST[:], reads=bl(*[f"OST{j}" for j in range(4)]))
                P.dma("pool", dbg_o[b, :, 4:8, :], OGT[:], reads=bl(*[f"OGT{j}" for j in range(4)]))

        def m_gla(b):
            def tick():
                return 1
            for j in range(4):
                tc = slice(j * 128, (j + 1) * 128)
                hb = f"HT0_{j}"
                P.op("pe", lambda e, tc=tc: mm_group(e, PB[2][:, 0:512], [(HT0[:, kk, tc], W_IN[:, kk, 1024:1536]) for kk in range(8)]),
                     reads=bl(hb, "W_IN"), writes=bl("PB2"))
                P.op("pe", lambda e, tc=tc: mm_group(e, PB[3][:, 0:256], [(HT0[:, kk, tc], W_IN[:, kk, 1536:1792]) for kk in range(8)]),
                     reads=bl(hb, "W_IN"), writes=bl("PB3"))
                P.op("dve", lambda e: e.tensor_tensor(out=VG[:, 0:256], in0=PB[2][:, 256:512], in1=BTOK[:, 256:512], op=ALU.add),
                     reads=bl("PB2", "BTOK"), writes=bl("VG"))
                P.op("dve", lambda e: e.tensor_tensor(out=VG[:, 256:512], in0=PB[3][:, 0:256], in1=BTOK[:, 512:768], op=ALU.add),
                     reads=bl("PB3", "BTOK"), writes=bl("VG"))
                P.op("dve", lambda e: e.tensor_tensor(out=KTMP[:, :], in0=PB[2][:, 0:256], in1=BTOK[:, 0:256], op=ALU.add),
                     reads=bl("PB2", "BTOK"), writes=bl("SQ"))
                yield 1.0
                if j == 0:
                    P.op("pe", lambda e: mm_group(e, PB[0][0:16, :], [(W_IN[:, kk, 2304:2320], HT0[:, kk, :]) for kk in range(8)]),
                         reads=bl(*HT0all, "W_IN"), writes=bl("PB0"))
                    P.op("act", lambda e: e.activation(out=GLR[0:16, :], in_=PB[0][0:16, :], func=AF.Identity,
                                                       bias=BCOL[0:16, 22:23], scale=1.0), reads=bl("PB0", "BCOL"), writes=bl("GLR"))
                gate_pipeline(128, j * 128, 0)
                P.op("dve", lambda e: e.tensor_tensor(out=KH[:, :], in0=KTMP[:, :], in1=ER[:, :], op=ALU.mult),
                     reads=bl("SQ"), writes=bl("KH"))
                yield 3.0
                def mbt(e):
                    for hh in range(4):
                        i = e.matmul(PB[1][0:64, hh * 128:(hh + 1) * 128], lhsT=SPL[:, hh * 64:(hh + 1) * 64], rhs=TRI1[:, :],
                                     start=True, stop=True)
                    return i
                P.op("pe", mbt, reads=bl("GB", "CONST"), writes=bl("PB1"))
                pbv = PB[1][0:64, :].rearrange("p (h c) -> p h c", h=4)
                P.op("act", lambda e, pbv=pbv: e.activation(out=EB1[:], in_=pbv, func=AF.Exp, bias=LN8, scale=1.0),
                     reads=bl("PB1"), writes=bl("EB1"))
                P.op("act", lambda e, pbv=pbv: e.activation(out=EBL[:, :], in_=pbv[:, :, 127], func=AF.Exp),
                     reads=bl("PB1"), writes=bl("EBL"))
                P.op("act", lambda e, pbv=pbv: e.activation(out=EB2, in_=pbv, func=AF.Exp, scale=-1.0),
                     reads=bl("PB1"), writes=bl("GB"))
                yield 1.5
                def mqg(e, tc=tc):
                    for hh in range(4):
                        i = mm_group(e, PB[2][0:64, hh * 128:(hh + 1) * 128],
                                     [(W_IN[:, kk, 768 + 64 * hh:832 + 64 * hh], HT0[:, kk, tc]) for kk in range(8)])
                    return i
                P.op("pe", mqg, reads=bl(hb, "W_IN"), writes=bl("PB2"))
                def mkg(e, tc=tc):
                    for hh in range(4):
                        i = mm_group(e, PB[3][0:64, hh * 128:(hh + 1) * 128],
                                     [(W_IN[:, kk, 1024 + 64 * hh:1088 + 64 * hh], HT0[:, kk, tc]) for kk in range(8)])
                    return i
                P.op("pe", mkg, reads=bl(hb, "W_IN"), writes=bl("PB3"))
                for hh in range(4):
                    P.op("dve", lambda e, hh=hh: e.scalar_tensor_tensor(
                        out=QTT[:, hh, :], in0=PB[2][0:64, hh * 128:(hh + 1) * 128], scalar=BCOL[0:64, 10 + hh:11 + hh],
                        in1=EB1[:, hh, :], op0=ALU.add, op1=ALU.mult), reads=bl("PB2", "BCOL", "EB1"), writes=bl("QTT"))
                    P.op("dve", lambda e, hh=hh: e.scalar_tensor_tensor(
                        out=KTT[:, hh, :], in0=PB[3][0:64, hh * 128:(hh + 1) * 128], scalar=BCOL[0:64, 14 + hh:15 + hh],
                        in1=EB2[:, hh, :], op0=ALU.add, op1=ALU.mult), reads=bl("PB3", "BCOL", "GB"), writes=bl("KTT"))
                yield 2.5
                def ma(e):
                    for hh in range(4):
                        i = e.matmul(PB[0][:, hh * 128:(hh + 1) * 128], lhsT=KTT[:, hh, :], rhs=QTT[:, hh, :], start=True, stop=True)
                    return i
                P.op("pe", ma, reads=bl("KTT", "QTT"), writes=bl("PB0"))
                P.op("dve", lambda e: e.tensor_tensor(out=AM[:], in0=PB[0][:].rearrange("p (h c) -> p h c", h=4),
                                                      in1=MASKU[:].unsqueeze(1).broadcast_to([128, 4, 128]), op=ALU.mult),
                     reads=bl("PB0", "CONST"), writes=bl("AM"))
                yield 0.7
                def mo(e):
                    for hh in range(4):
                        e.matmul(PB[1][:, hh * 128:(hh + 1) * 128], lhsT=VG[:, hh * 128:(hh + 1) * 128], rhs=AM[:, hh, :],
                                 start=True, stop=False)
                        i = e.matmul(PB[1][:, hh * 128:(hh + 1) * 128], lhsT=SBF[:, hh, :], rhs=QTT[:, hh, :], start=False, stop=True)
                    return i
                P.op("pe", mo, reads=bl("VG", "AM", "SBF", "QTT"), writes=bl("PB1"))
                def mst(e):
                    for hh in range(4):
                        i = e.matmul(PB[2][0:64, hh * 128:(hh + 1) * 128], lhsT=KH[:, hh * 64:(hh + 1) * 64],
                                     rhs=VG[:, hh * 128:(hh + 1) * 128], start=True, stop=True)
                    return i
                P.op("pe", mst, reads=bl("KH", "VG"), writes=bl("PB2"))
                P.op("dve", lambda e: e.tensor_tensor(out=S32[:], in0=S32[:], in1=EBL[:, :].unsqueeze(2).broadcast_to([64, 4, 128]), op=ALU.mult),
                     reads=bl("S32", "EBL"), writes=bl("S32"))
                P.op("dve", lambda e: e.tensor_tensor(out=S32[:], in0=S32[:], in1=PB[2][0:64, :].rearrange("p (h c) -> p h c", h=4), op=ALU.add),
                     reads=bl("S32", "PB2"), writes=bl("S32"))
                P.op("act", lambda e: e.activation(out=SBF[:], in_=S32[:], func=AF.Copy), reads=bl("S32"), writes=bl("SBF"))
                P.op("act", lambda e: e.activation(out=SQ[:], in_=PB[1][:], func=AF.Square), reads=bl("PB1"), writes=bl("SQ"))
                P.op("pe", lambda e: e.matmul(PB[0][:, :], lhsT=ONESF[:, :], rhs=SQ[:, :], start=True, stop=True),
                     reads=bl("SQ", "CONST"), writes=bl("PB0"))
                P.op("act", lambda e: e.activation(out=RSTD[:], in_=PB[0][:], func=AF.Ln, bias=RMS_EPS, scale=1.0 / 128.0),
                     reads=bl("PB0"), writes=bl("RSTD"))
                P.op("act", lambda e: e.activation(out=RSTD[:], in_=RSTD[:], func=AF.Exp, scale=-0.5), reads=bl("RSTD"), writes=bl("RSTD"))
                P.op("dve", lambda e: e.scalar_tensor_tensor(out=OG32[:], in0=PB[1][:], scalar=BCOL[:, 28:29], in1=RSTD[:],
                                                             op0=ALU.mult, op1=ALU.mult), reads=bl("PB1", "BCOL", "RSTD"), writes=bl("SQ"))
                yield 3.0
                def mrg(e, tc=tc):
                    for hh in range(4):
                        i = mm_group(e, PB[3][:, hh * 128:(hh + 1) * 128],
                                     [(W_IN[:, kk, 1792 + 128 * hh:1920 + 128 * hh], HT0[:, kk, tc]) for kk in range(8)])
                    return i
                P.op("pe", mrg, reads=bl(hb, "W_IN"), writes=bl("PB3"))
                for hh in range(4):
                    P.op("act", lambda e, hh=hh: e.activation(out=SGM[:, hh * 128:(hh + 1) * 128], in_=PB[3][:, hh * 128:(hh + 1) * 128],
                                                              func=AF.Tanh, bias=BCOL[:, 24 + hh:25 + hh], scale=0.5),
                         reads=bl("PB3", "BCOL"), writes=bl("GB"))
                P.op("dve", lambda e: e.scalar_tensor_tensor(out=SGM[:], in0=SGM[:], scalar=1.0, in1=OG32[:], op0=ALU.add, op1=ALU.mult),
                     reads=bl("SQ", "GB"), writes=bl("GB"))
                for hh in range(4):
                    P.op("dve", lambda e, hh=hh, tc=tc: e.scalar_tensor_tensor(
                        out=OGT[:, hh, tc], in0=PB[3][:, hh * 128:(hh + 1) * 128], scalar=BCOL[:, 18 + hh:19 + hh],
                        in1=SGM[:, hh * 128:(hh + 1) * 128], op0=ALU.add, op1=ALU.mult),
                        reads=bl("PB3", "BCOL", "GB"), writes=bl(f"OGT{j}"))
                yield 1.5
        def m_qk(b):
            def mvs(e):
                for j in range(4):
                    i = mm_group(e, PB[2][:, j * 128:(j + 1) * 128],
                                 [(HT0[:, kk, j * 128:(j + 1) * 128], W_IN[:, kk, 640:768]) for kk in range(8)])
                return i
            P.op("pe", mvs, reads=bl(*HT0all, "W_IN"), writes=bl("PB2"))
            P.op("dve", lambda e: e.tensor_tensor(out=VS[:, 1:5, :], in0=PB[2][:, :].rearrange("p (j c) -> p j c", j=4),
                                                  in1=BTOK[:, 768:896].unsqueeze(1).broadcast_to([128, 4, 128]), op=ALU.add),
                 reads=bl("PB2", "BTOK"), writes=bl("VS0", "VS1", "VS2", "VS3"))
            yield 0.3
            for h in range(8):
                pb = h % 2
                P.op("pe", lambda e, h=h, pb=pb: mm_group(e, PB[pb][0:64, :], [(W_IN[:, kk, 64 * h:64 * h + 64], HT0[:, kk, :]) for kk in range(8)]),
                     reads=bl(*HT0all, "W_IN"), writes=bl(f"PB{pb}"))
                P.op("act", lambda e, h=h, pb=pb: e.activation(out=QST[0:64, h, :], in_=PB[pb][0:64, :], func=AF.Identity,
                                                               bias=BCOL[0:64, h:h + 1], scale=0.125),
                     reads=bl(f"PB{pb}", "BCOL"), writes=bl(*[f"QST{j}" for j in range(4)]))
                yield 0.3
            for k in range(2):
                pb = 2 + k
                P.op("pe", lambda e, k=k, pb=pb: mm_group(e, PB[pb][0:64, :], [(W_IN[:, kk, 512 + 64 * k:576 + 64 * k], HT0[:, kk, :]) for kk in range(8)]),
                     reads=bl(*HT0all, "W_IN"), writes=bl(f"PB{pb}"))
                P.op("act", lambda e, k=k, pb=pb: e.activation(out=KST[0:64, k, 128:640], in_=PB[pb][0:64, :], func=AF.Identity,
                                                               bias=BCOL[0:64, 8 + k:9 + k], scale=1.0),
                     reads=bl(f"PB{pb}", "BCOL"), writes=bl(*[f"KST{j}" for j in range(4)]))
                yield 0.3

        def m_swa(b):
            def tick():
                return 1
            for j in range(4):
                T = 4 * b + j
                tc = slice(j * 128, (j + 1) * 128)
                for k in range(2):
                    qv = QST[:, 4 * k:4 * k + 4, tc]
                    kprev_b = f"KST{j - 1}" if j > 0 else "KSTp"
                    vprev_b = f"VS{j - 1}" if j > 0 else "VSp"
                    P.op("pe", lambda e, k=k, j=j, qv=qv: e.matmul(PB[0][:, :], lhsT=KST[0:66, k, 128 + 128 * j:256 + 128 * j], rhs=qv[0:66],
                                                                   start=True, stop=True),
                         reads=bl(f"KST{j}", "KSTx", f"QST{j}", "QSTx"), writes=bl("PB0"))
                    if T > 0:
                        P.op("pe", lambda e, k=k, j=j, qv=qv: e.matmul(PB[1][:, :], lhsT=KST[0:67, k, 128 * j:128 + 128 * j], rhs=qv[0:67],
                                                                       start=True, stop=True),
                             reads=bl(kprev_b, "KSTx", f"QST{j}", "QSTx"), writes=bl("PB1"))
                    P.op("pe", lambda e, k=k, T=T, qv=qv: e.matmul(PB[2][0:16, :], lhsT=KMT[0:67, k, T, :], rhs=qv[0:67], start=True, stop=True),
                         reads=bl("KMT", "KMTx", f"QST{j}", "QSTx"), writes=bl("PB2"))
                    P.op("act", lambda e: e.activation(out=PT[:, 0, :], in_=PB[0][:, :], func=AF.Exp), reads=bl("PB0"), writes=bl("PTc"))
                    P.op("pool", lambda e: e.tensor_tensor(out=PT[:, 0, :].rearrange("p (g q) -> p g q", g=4),
                                                           in0=PT[:, 0, :].rearrange("p (g q) -> p g q", g=4),
                                                           in1=MCUR[:].unsqueeze(1).broadcast_to([128, 4, 128]), op=ALU.mult),
                         reads=bl("PTc", "CONST"), writes=bl("PTc"))
                    if T > 0:
                        P.op("act", lambda e: e.activation(out=PT[:, 1, :], in_=PB[1][:, :], func=AF.Exp), reads=bl("PB1"), writes=bl("PTp"))
                        P.op("pool", lambda e: e.tensor_tensor(out=PT[:, 1, :].rearrange("p (g q) -> p g q", g=4),
                                                               in0=PT[:, 1, :].rearrange("p (g q) -> p g q", g=4),
                                                               in1=MPREV[:].unsqueeze(1).broadcast_to([128, 4, 128]), op=ALU.mult),
                             reads=bl("PTp", "CONST"), writes=bl("PTp"))
                    P.op("act", lambda e, k=k: e.activation(out=PTM[0:16, k, :], in_=PB[2][0:16, :], func=AF.Exp), reads=bl("PB2"), writes=bl("PTM"))
                    yield 3.0
                    ptc = PT[:, 0, :].rearrange("p (a b q) -> p a b q", a=2, b=2)
                    ptp = PT[:, 1, :].rearrange("p (a b q) -> p a b q", a=2, b=2)
                    def mpv(e, k=k, j=j, T=T, ptc=ptc, ptp=ptp):
                        for hf in range(2):
                            o = PB[3][64 * hf:64 * hf + 64, 0:256]
                            e.matmul(o, lhsT=VS[:, j + 1, k * 64:(k + 1) * 64], rhs=ptc[:, :, hf, :], start=True, stop=False)
                            if T > 0:
                                e.matmul(o, lhsT=VS[:, j, k * 64:(k + 1) * 64], rhs=ptp[:, :, hf, :], start=False, stop=False)
                            i = e.matmul(o, lhsT=VM[0:33, k, :], rhs=PTM[0:33, k, :].rearrange("p (a b q) -> p a b q", a=2, b=2)[:, :, hf, :],
                                         start=False, stop=True)
                        for hf in range(2):
                            o = PB[3][64 * hf:64 * hf + 64, 256:512]
                            e.matmul(o, lhsT=ONESB[:, 0:64], rhs=ptc[:, :, hf, :], start=True, stop=False)
                            if T > 0:
                                e.matmul(o, lhsT=ONESB[:, 0:64], rhs=ptp[:, :, hf, :], start=False, stop=False)
                            i = e.matmul(o, lhsT=ONESM[0:33, 0:64], rhs=PTM[0:33, k, :].rearrange("p (a b q) -> p a b q", a=2, b=2)[:, :, hf, :],
                                         start=False, stop=True)
                        return i
                    P.op("pe", mpv, reads=bl(f"VS{j}", vprev_b, "PTc", "PTp", "PTM", "PTMx", "VM", "CONST"), writes=bl("PB3"))
                    P.op("dve", lambda e: e.reciprocal(out=RDEN[:, 0:256], in_=PB[3][:, 256:512]), reads=bl("PB3"), writes=bl("RDEN"))
                    P.op("dve", lambda e, k=k, tc=tc: e.tensor_tensor(
                        out=OST[:, 2 * k:2 * k + 2, tc], in0=PB[3][:, 0:256].rearrange("p (a q) -> p a q", a=2),
                        in1=RDEN[:, 0:256].rearrange("p (a q) -> p a q", a=2), op=ALU.mult),
                        reads=bl("PB3", "RDEN"), writes=bl(f"OST{j}"))
                    yield 2.0

        def ffn(b):
            nst = 44.0
            for fc in range(NFC):
                wb = (b * NFC + fc) % 2
                P.dma("sp", WGU[wb][:], wgu_s[fc], reads=bl(f"wgu_s{fc}"), writes=bl(f"WGU{wb}"))
                pg, pu = (4, 5) if fc % 2 == 0 else (6, 7)
                P.op("pe", lambda e, wb=wb, pg=pg: mm_group(e, PB[pg][:, :], [(WGU[wb][:, 0, kk, :], HT1[:, kk, :]) for kk in range(8)]),
                     reads=bl(f"WGU{wb}", *HT1all), writes=bl(f"PB{pg}"))
                P.op("pe", lambda e, wb=wb, pu=pu: mm_group(e, PB[pu][:, :], [(WGU[wb][:, 1, kk, :], HT1[:, kk, :]) for kk in range(8)]),
                     reads=bl(f"WGU{wb}", *HT1all), writes=bl(f"PB{pu}"))
                P.op("act", lambda e, pg=pg: e.activation(out=SGF[:], in_=PB[pg][:, :], func=AF.Tanh, scale=0.5), reads=bl(f"PB{pg}"), writes=bl("SGF"))
                P.op("dve", lambda e, pg=pg: e.scalar_tensor_tensor(out=SGF[:], in0=SGF[:], scalar=1.0, in1=PB[pg][:, :], op0=ALU.add, op1=ALU.mult),
                     reads=bl("SGF", f"PB{pg}"), writes=bl("SGF"))
                P.op("dve", lambda e, fc=fc, pu=pu: e.scalar_tensor_tensor(out=AT[:, fc, :], in0=SGF[:], scalar=0.5, in1=PB[pu][:, :], op0=ALU.mult, op1=ALU.mult),
                     reads=bl("SGF", f"PB{pu}"), writes=bl(f"AT{fc}"))
                yield (fc + 1) / nst
            for hf in range(2):
                cs = slice(hf * 512, (hf + 1) * 512)
                for s in range(11):
                    wb = ((b * 2 + hf) * 11 + s) % 3
                    P.dma("sp", WD[wb][:], wd_s[hf, s], reads=bl(f"wd_s{hf}_{s}"), writes=bl(f"WD{wb}"))
                    def mdn(e, s=s, wb=wb):
                        for sub in range(2):
                            fc = 2 * s + sub
                            for j in range(4):
                                i = e.matmul(PB[4 + j][:, :], lhsT=AT[:, fc, j * 128:(j + 1) * 128], rhs=WD[wb][:, sub, :],
                                             start=(fc == 0), stop=(fc == NFC - 1))
                        return i
                    P.op("pe", mdn, reads=bl(f"WD{wb}", f"AT{2 * s}", f"AT{2 * s + 1}"), writes=bl("PB4", "PB5", "PB6", "PB7"))
                    if s < 10:
                        yield (22 + hf * 11 + s + 1) / nst
                for j in range(4):
                    P.op("dve", lambda e, j=j, cs=cs: e.scalar_tensor_tensor(out=H[:, j, cs], in0=H[:, j, cs], scalar=ALPHA, in1=PB[4 + j][:, :],
                                                                             op0=ALU.mult, op1=ALU.add),
                         reads=bl(f"H{j}", f"PB{4 + j}"), writes=bl(f"H{j}"))
                yield (22 + hf * 11 + 11) / nst

        def ln_a(src, rb):
            i = ctr["ls"] % NLS
            ctr["ls"] += 1
            L, lb = LST[i], bl(f"LST{i}")
            P.op("dve", lambda e: e.bn_stats(out=L[:, 0:6], in_=src[:, 0:512]), reads=rb, writes=lb)
            P.op("dve", lambda e: e.bn_stats(out=L[:, 6:12], in_=src[:, 512:1024]), reads=rb, writes=lb)
            P.op("dve", lambda e: e.bn_aggr(out=L[:, 12:14], in_=L[:, 0:12]), reads=lb, writes=lb)
            P.op("dve", lambda e: e.tensor_scalar(out=L[:, 15:16], in0=L[:, 12:13], scalar1=-1.0, scalar2=None, op0=ALU.mult),
                 reads=lb, writes=lb)
            return L, lb

        def ln_b(L, lb):
            P.op("act", lambda e: e.activation(out=L[:, 14:15], in_=L[:, 13:14], func=AF.Ln, bias=LN_EPS, scale=1.0), reads=lb, writes=lb)
            P.op("act", lambda e: e.activation(out=L[:, 14:15], in_=L[:, 14:15], func=AF.Exp, scale=-0.5), reads=lb, writes=lb)
            P.op("act", lambda e: e.activation(out=L[:, 15:16], in_=L[:, 15:16], func=AF.Identity, scale=L[:, 14:15]), reads=lb, writes=lb)

        def ln_c(L, lb):
            pass

        def ln_n(t, tb, L, lb):
            P.op("act", lambda e: e.activation(out=t, in_=t, func=AF.Identity, bias=L[:, 15:16], scale=L[:, 14:15]),
                 reads=tb + lb, writes=tb)

        def ln_g(t, tb, gi):
            P.op("dve", lambda e: e.tensor_tensor(out=t, in0=t, in1=LNC[:, gi, :], op=ALU.mult), reads=tb + bl("LNC"), writes=tb)

        def ln_bias(t, tb, gi):
            P.op("pool", lambda e: e.tensor_tensor(out=t, in0=t, in1=LNC[:, gi + 1, :], op=ALU.add), reads=tb + bl("LNC"), writes=tb)

        XTB = [bl("AT0", "AT1", "AT2", "AT3"), bl("AT4", "AT5", "AT6", "AT7")]

        def post_f(b):
            items = []
            if b >= 0:
                for j in range(4):
                    items.append((H[:, j, :], bl(f"H{j}"), 4, j))
            if b + 1 < nblk:
                for j in range(2):
                    T = 4 * (b + 1) + j
                    P.dma("sp", XT[j], x[T * 128:(T + 1) * 128, :], writes=XTB[j])
                    items.append((XT[j], XTB[j], 0, None))
            yield 1
            stats = []
            for (t, tb, gi, j) in items:
                stats.append(ln_a(t, tb))
                yield 1
            for (L, lb) in stats:
                ln_b(L, lb)
            yield 1
            for (L, lb) in stats:
                ln_c(L, lb)
            yield 1
            for (t, tb, gi, j), (L, lb) in zip(items, stats):
                ln_n(t, tb, L, lb)
                yield 1
            for (t, tb, gi, j) in items:
                ln_g(t, tb, gi)
                yield 1
            for (t, tb, gi, j) in items:
                ln_bias(t, tb, gi)
                if j is not None:
                    T = 4 * b + j
                    P.dma("pool", out[T * 128:(T + 1) * 128, :], H[:, j, :], reads=bl(f"H{j}"))
                yield 1

        def tails(b):
            XA = [(XT[0], XTB[0]), (XT[1], XTB[1]), (XS[0][:, :], bl("XS0")), (XS[1][:, :], bl("XS1"))]
            for j in range(4):
                tc = slice(j * 128, (j + 1) * 128)
                for hf in range(2):
                    cs = slice(hf * 512, (hf + 1) * 512)
                    pbi = 2 * j + hf
                    P.op("pe", lambda e, tc=tc, cs=cs, pbi=pbi: mm_group(e, PB[pbi][:, :],
                         [((OST[:, kk, tc] if kk < 4 else OGT[:, kk - 4, tc]), W_OUT[:, kk, cs]) for kk in range(8)]),
                         reads=bl(f"OST{j}", f"OGT{j}", "W_OUT"), writes=bl(f"PB{pbi}"))
            for j in range(4):
                xs, xb = XA[j]
                for hf in range(2):
                    cs = slice(hf * 512, (hf + 1) * 512)
                    pbi = 2 * j + hf
                    P.op("dve", lambda e, j=j, cs=cs, pbi=pbi, xs=xs: e.scalar_tensor_tensor(
                        out=H[:, j, cs], in0=xs[:, cs], scalar=ALPHA, in1=PB[pbi][:, :], op0=ALU.mult, op1=ALU.add),
                        reads=xb + bl(f"PB{pbi}"), writes=bl(f"H{j}"))
            items = [(H[:, j, :], bl(f"H{j}"), 2) for j in range(4)]
            stats = [ln_a(t, tb) for (t, tb, gi) in items]
            for (L, lb) in stats:
                ln_b(L, lb)
            for (L, lb) in stats:
                ln_c(L, lb)
            for (t, tb, gi), (L, lb) in zip(items, stats):
                ln_n(t, tb, L, lb)
            for j in range(4):
                transpose_tile(H[:, j, :], bl(f"H{j}"), HT1, bl(f"HT1_{j}"), j, (2 * j, 2 * j + 1), gcol=16)
            for (t, tb, gi) in items:
                ln_g(t, tb, gi)
            for (t, tb, gi) in items:
                ln_bias(t, tb, gi)
            for j in range(4):
                T = 4 * (b + 1) + j
                if debug:
                    P.dma("pool", dbg_h1[T * 128:(T + 1) * 128, :], H[:, j, :], reads=bl(f"H{j}"))

        def run(gen):
            for _ in gen:
                pass

        NM_UNITS = 126.0
        F_FRAC = 0.75

        def step(g):
            try:
                next(g)
                return True
            except StopIteration:
                return False

        def interleave(gm, gf, extras):
            nm = nf = 0
            fl = gf is not None
            el = [True] * len(extras)
            if gm is not None:
                for w in gm:
                    nm += float(w)
                    while fl and nf / 44.0 <= nm / (F_FRAC * NM_UNITS):
                        fl = step(gf)
                        nf += 1
                    if not fl:
                        for i, g in enumerate(extras):
                            if el[i]:
                                el[i] = step(g)
                                if el[i] and getattr(g, "double", False):
                                    el[i] = step(g)
            while fl:
                fl = step(gf)
            for i, g in enumerate(extras):
                while el[i]:
                    el[i] = step(g)

        interleave(mixer_head(0), None, [casts(), casts_more(), post_f(-1)])
        tails(-1)
        ckpt(2)
        for b in range(nblk):
            gm = mixer_head(b + 1) if b + 1 < nblk else None
            interleave(gm, ffn(b), [post_f(b)])
            if b + 1 < nblk:
                tails(b)
        print("sbuf bytes remaining", nc.sbuf_bytes_remaining)
        P.finish()
        P.emit()
    return nc


WEIGHT_KEYS = ["meta_tokens", "ln_in_g", "ln_in_b", "w_in", "b_in", "w_gate_lr2", "b_gate_lr2", "attn_sinks",
               "gla_norm_g", "w_out", "ln1_g", "ln1_b", "w_ffn_gate", "w_ffn_up", "w_ffn_down", "ln2_g", "ln2_b"]


def make_in_map(inputs, bi):
    f = lambda a: np.ascontiguousarray(np.asarray(a, dtype=np.float32))
    m = {"x": f(inputs["x"][bi])}
    for k in WEIGHT_KEYS:
        a = f(inputs[k])
        if k not in ("meta_tokens", "ln_in_g", "ln_in_b"):
            a = a[0]
        m[k] = np.ascontiguousarray(a)
    m.update(host_consts())
    return m


def kernel(**inputs):
    x = np.asarray(inputs["x"])
    Bn, S, _ = x.shape
    nc = build(S // TB)
    in_maps = [make_in_map(inputs, bi) for bi in range(Bn)]
    res = run_bass_kernel_spmd(nc, in_maps, core_ids=list(range(Bn)))
    return np.stack([np.asarray(r["out"], dtype=np.float32) for r in res.results], axis=0)
```

```python
import numpy as np
from contextlib import ExitStack
import ml_dtypes
import concourse.bass as bass
import concourse.mybir as mybir
from concourse.bass_utils import run_bass_kernel_spmd

F32 = mybir.dt.float32
BF16 = mybir.dt.bfloat16
AF = mybir.ActivationFunctionType
ALU = mybir.AluOpType

D = 1024
NM = 16
DIN = 2320
DFF = 2816
NFC = DFF // 128
ALPHA = 2.0 ** 0.25
LN_EPS = 1e-5
RMS_EPS = 1e-6
TB = 512
LN8 = float(np.log(0.125))


class Tk:
    __slots__ = ("sem", "val", "key")

    def __init__(self, sem, val, key):
        self.sem, self.val, self.key = sem, val, key


class Buf:
    def __init__(self, name):
        self.name = name
        self.w = None
        self.r = {}


class Prog:
    ENG = ("pe", "act", "dve", "pool", "sp")
    R = 8

    def __init__(self, nc, es):
        self.nc = nc
        self.q = {e: [] for e in self.ENG}
        self.cnt = {e: 0 for e in self.ENG}
        self.waited = {e: {} for e in self.ENG}
        self.sem = {e: es.enter_context(nc.semaphore("s_" + e)) for e in ("pe", "act", "dve", "pool")}
        self.ring = {qn: [es.enter_context(nc.semaphore(f"d_{qn}{i}")) for i in range(self.R)]
                     for qn in ("sp", "pool")}
        self.dma_n = {"sp": 0, "pool": 0}

    def _deps(self, eng, reads, writes):
        ts = []
        for b in reads:
            if b.w is not None:
                ts.append(b.w)
            if b.name.startswith("PB"):
                ts.extend(t for e2, t in b.r.items() if e2 != eng)
        for b in writes:
            if b.w is not None:
                ts.append(b.w)
            ts.extend(b.r.values())
        best = {}
        for t in ts:
            if self.waited[eng].get(t.key, 0) < t.val:
                if t.key not in best or best[t.key].val < t.val:
                    best[t.key] = t
        for t in best.values():
            self.waited[eng][t.key] = t.val
        return list(best.values())

    def _mark(self, eng, tk, reads, writes):
        for b in reads:
            b.r[eng] = tk
        for b in writes:
            b.w = tk
            b.r = {}

    def op(self, eng, fn, reads=(), writes=()):
        waits = self._deps(eng, reads, writes)
        self.cnt[eng] += 1
        tk = Tk(self.sem[eng], self.cnt[eng], eng)
        self.q[eng].append((waits, fn, (self.sem[eng], 1)))
        self._mark(eng, tk, reads, writes)
        return tk

    def dma(self, qn, out, in_, reads=(), writes=()):
        waits = self._deps(qn, reads, writes)
        j = self.dma_n[qn]
        self.dma_n[qn] += 1
        slot, val = j % self.R, 16 * (j // self.R + 1)
        key = f"{qn}{slot}"
        if val > 16 and self.waited[qn].get(key, 0) < val - 16:
            waits.append(Tk(self.ring[qn][slot], val - 16, key))
            self.waited[qn][key] = val - 16
        tk = Tk(self.ring[qn][slot], val, key)
        self.q[qn].append((waits, lambda e: e.dma_start(out=out, in_=in_), (self.ring[qn][slot], 16)))
        self._mark(qn, tk, reads, writes)
        return tk

    def handoff(self, src, dst):
        best = {}
        for b in src:
            for t in ([b.w] if b.w is not None else []) + list(b.r.values()):
                if t.key not in best or best[t.key].val < t.val:
                    best[t.key] = t
        for b in dst:
            for k, t in best.items():
                b.r["h_" + k] = t

    def finish(self):
        waits = []
        for qn in ("sp", "pool"):
            n = self.dma_n[qn]
            for slot in range(self.R):
                cnt = (n - slot + self.R - 1) // self.R if n > slot else 0
                if cnt > 0:
                    waits.append(Tk(self.ring[qn][slot], 16 * cnt, f"{qn}{slot}"))
        self.q["sp"].append((waits, None, None))

    def emit(self):
        nc = self.nc
        with nc.Block() as block:
            def replay(name):
                def f(e):
                    for waits, fn, inc in self.q[name]:
                        for t in waits:
                            e.wait_ge(t.sem, t.val)
                        if fn is not None:
                            ins = fn(e)
                            ins.then_inc(inc[0], inc[1])
                return f
            block.tensor(replay("pe"))
            block.scalar(replay("act"))
            block.vector(replay("dve"))
            block.gpsimd(replay("pool"))
            block.sync(replay("sp"))


def host_consts():
    bf = ml_dtypes.bfloat16
    j = np.arange(128)[:, None]
    i = np.arange(128)[None, :]
    c = {}
    c["ident"] = np.eye(128, dtype=np.float32)
    c["tri1"] = np.where(j <= i, -1.0 / 16.0, 0.0).astype(np.float32)
    c["tri2"] = np.where(j > i, -1.0 / 16.0, 0.0).astype(np.float32)
    c["masku"] = (j <= i).astype(np.float32).astype(bf)
    c["mcur"] = (j <= i).astype(np.float32).astype(bf)
    c["mprev"] = (j > i).astype(np.float32).astype(bf)
    slopes = 2.0 ** (-8.0 * (np.arange(8) + 1) / 8.0)
    a = np.arange(128)
    qrows = np.zeros((3, 8, TB), np.float32)
    for h in range(8):
        qrows[0, h, :] = slopes[h]
        qrows[1, h, :] = -slopes[h] * np.tile(a, TB // 128)
        qrows[2, h, :] = -128.0 * slopes[h]
    c["qrows"] = qrows.astype(bf)
    kb = np.zeros((3, 2, 640), np.float32)
    kb[0, :, :] = np.tile(a, 5)[None, :]
    kb[1] = 1.0
    kb[2] = 1.0
    c["kbrows"] = kb.astype(bf)
    km = np.zeros((3, 2, 32, 16), np.float32)
    km[0] = (np.arange(16) - 16.0)[None, None, :]
    km[1] = 1.0
    km[2] = np.arange(32)[None, :, None]
    c["kmrows"] = km.astype(bf)
    om = np.zeros((33, 128), np.float32)
    om[0:16] = 1.0
    om[32] = 1.0
    c["onesm"] = om.astype(bf)
    for k, v in c.items():
        assert np.all(np.isfinite(v.astype(np.float32)))
    return c


CONST_SPECS = [("ident", [128, 128], F32), ("tri1", [128, 128], F32), ("tri2", [128, 128], F32),
               ("masku", [128, 128], BF16), ("mcur", [128, 128], BF16), ("mprev", [128, 128], BF16),
               ("qrows", [3, 8, TB], BF16), ("kbrows", [3, 2, 640], BF16),
               ("kmrows", [3, 2, 32, 16], BF16), ("onesm", [33, 128], BF16)]


class _Stop(Exception):
    pass


def build(nblk, debug=False, stage=99):
    try:
        return _build(nblk, debug, stage)
    except _Stop as e:
        return e.args[0]


def _build(nblk, debug=False, stage=99):
    S = nblk * TB
    nc = bass.Bass("TRN2", target_bir_lowering=False)
    di = lambda n, s, d=F32: nc.dram_tensor(n, s, d, kind="ExternalInput").ap()
    x = di("x", [S, D])
    meta = di("meta_tokens", [NM, D])
    lnv = [di(n, [D]) for n in ("ln_in_g", "ln_in_b", "ln1_g", "ln1_b", "ln2_g", "ln2_b")]
    w_in = di("w_in", [D, DIN])
    b_in = di("b_in", [DIN])
    wg2 = di("w_gate_lr2", [16, 256])
    bg2 = di("b_gate_lr2", [256])
    sinks = di("attn_sinks", [8])
    gnorm = di("gla_norm_g", [128])
    w_out = di("w_out", [D, D])
    w_fg = di("w_ffn_gate", [D, DFF])
    w_fu = di("w_ffn_up", [D, DFF])
    w_fd = di("w_ffn_down", [DFF, D])
    cst = {n: di(n, s, d) for n, s, d in CONST_SPECS}
    out = nc.dram_tensor("out", [S, D], F32, kind="ExternalOutput").ap()
    if debug:
        dbg_h0 = nc.dram_tensor("dbg_h0", [S, D], F32, kind="ExternalOutput").ap()
        dbg_h1 = nc.dram_tensor("dbg_h1", [S, D], F32, kind="ExternalOutput").ap()
        dbg_o = nc.dram_tensor("dbg_o", [nblk, 128, 8, TB], BF16, kind="ExternalOutput").ap()
    wgu_s = nc.dram_tensor("wgu_s", [NFC, 128, 2, 8, 128], BF16).ap()
    wd_s = nc.dram_tensor("wd_s", [2, 11, 128, 2, 512], BF16).ap()

    with ExitStack() as es:
        P = Prog(nc, es)
        sb = lambda n, s, d: es.enter_context(nc.sbuf_tensor(n, s, d))
        W_IN = sb("W_IN", [128, 8, DIN], BF16)
        W_OUT = sb("W_OUT", [128, 8, D], BF16)
        WGU = [sb(f"WGU{i}", [128, 2, 8, 128], BF16) for i in range(2)]
        WD = [sb(f"WD{i}", [128, 2, 512], BF16) for i in range(3)]
        LNC = sb("LNC", [128, 6, D], F32)
        BTOK = sb("BTOK", [128, 896], F32)
        IDENT = sb("IDENT", [128, 128], F32)
        TRI1 = sb("TRI1", [128, 128], F32)
        TRI2 = sb("TRI2", [128, 128], F32)
        ONESF = sb("ONESF", [128, 128], F32)
        ONESB = sb("ONESB", [128, 128], BF16)
        ONESM = sb("ONESM", [33, 128], BF16)
        WG2 = sb("WG2", [32, 256], F32)
        MASKU = sb("MASKU", [128, 128], BF16)
        MCUR = sb("MCUR", [128, 128], BF16)
        MPREV = sb("MPREV", [128, 128], BF16)
        BCOL = sb("BCOL", [128, 32], F32)
        NEGH = sb("NEGH", [128, 1], F32)
        SINK32 = sb("SINK32", [33, 8], F32)
        NXS = 2
        XS = [sb(f"XS{i}", [128, D], F32) for i in range(NXS)]
        H = sb("H", [128, 4, D], F32)
        HT0 = sb("HT0", [128, 8, TB], BF16)
        HT1 = sb("HT1", [128, 8, TB], BF16)
        QST = sb("QST", [67, 8, TB], BF16)
        KST = sb("KST", [67, 2, 640], BF16)
        VS = sb("VS", [128, 5, 128], BF16)
        AT = sb("AT", [128, NFC, TB], BF16)
        QTT = sb("QTT", [64, 4, 128], BF16)
        KTT = sb("KTT", [64, 4, 128], BF16)
        KH = sb("KH", [128, 256], BF16)
        VG = sb("VG", [128, 512], BF16)
        GLR = sb("GLR", [32, TB], F32)
        OST = sb("OST", [128, 4, TB], BF16)
        OGT = sb("OGT", [128, 4, TB], BF16)
        KMT = sb("KMT", [67, 2, 32, 16], BF16)
        VM = sb("VM", [33, 2, 64], BF16)
        PT = sb("PT", [128, 2, TB], BF16)
        PTM = sb("PTM", [33, 2, TB], BF16)
        GB = sb("GB", [128, TB], F32)
        EX = GB[:, 0:256]
        SPL = GB[:, 256:512]
        EB2 = GB[0:64, :].rearrange("p (h c) -> p h c", h=4)
        SGM = GB
        EB1 = sb("EB1", [64, 4, 128], F32)
        EBL = sb("EBL", [64, 4], F32)
        AM = sb("AM", [128, 4, 128], BF16)
        S32 = sb("S32", [64, 4, 128], F32)
        SBF = sb("SBF", [64, 4, 128], BF16)
        SQ = sb("SQ", [128, TB], F32)
        KTMP = SQ[:, 0:256]
        ER = SQ[:, 256:512]
        OG32 = SQ
        RSTD = sb("RSTD", [128, TB], F32)
        RDEN = sb("RDEN", [128, 256], F32)
        SGF = sb("SGF", [128, TB], F32)
        NLS = 12
        LST = [sb(f"LST{i}", [128, 16], F32) for i in range(NLS)]
        PB = [es.enter_context(nc.psum_tensor(f"PB{i}", [128, 512], F32)) for i in range(8)]
        HMT = AT[:, 0, 0:128].rearrange("p (k t) -> p k t", k=8)
        XT = [AT[:, 4 * i:4 * i + 4, :].bitcast(F32).rearrange("p a b -> p (a b)") for i in range(2)]

        B = {}
        def bufs(*names):
            for n in names:
                B[n] = Buf(n)
        bufs("W_IN", "W_OUT", "LNC", "BTOK", "CONST", "WG2", "BCOL", "SINK32", "QSTx", "KSTx", "KMT", "KMTx",
             "VM", "PTMx", "GLR", "GB", "EB1", "EBL", "AM", "S32", "SBF", "SQ", "RSTD", "RDEN", "SGF", "PTM", "KSTp", "VSp",
             "PTc", "PTp", "QTT", "KTT", "KH", "VG")
        for i in range(NXS):
            bufs(f"XS{i}")
        for i in range(NLS):
            bufs(f"LST{i}")
        for i in range(2):
            bufs(f"WGU{i}")
        for i in range(4):
            bufs(f"WD{i}", f"H{i}", f"HT0_{i}", f"HT1_{i}", f"QST{i}", f"KST{i}", f"VS{i}", f"OST{i}", f"OGT{i}")
        for i in range(8):
            bufs(f"PB{i}")
        for i in range(NFC):
            bufs(f"AT{i}", f"wgu_s{i}")
        for i in range(11):
            bufs(f"wd_s0_{i}", f"wd_s1_{i}")

        def bl(*names):
            return [B[n] for n in names]

        P.dma("pool", W_IN[:], w_in.rearrange("(k p) n -> p k n", p=128), writes=bl("W_IN"))
        P.dma("sp", XS[0][0:16, :], meta, writes=bl("XS0"))
        for i in range(6):
            P.dma("sp", LNC[:, i, :], lnv[i].partition_broadcast(128), writes=bl("LNC"))
        for (n, t) in (("ident", IDENT), ("tri1", TRI1), ("tri2", TRI2), ("masku", MASKU), ("mcur", MCUR),
                       ("mprev", MPREV), ("onesm", ONESM)):
            P.dma("sp", t[:], cst[n], writes=bl("CONST"))
        P.dma("sp", QST[64:67, :, :], cst["qrows"], writes=bl("QSTx"))
        P.dma("sp", KST[64:67, :, :], cst["kbrows"], writes=bl("KSTx"))
        P.dma("sp", KMT[64:67, :, :, :], cst["kmrows"], writes=bl("KMTx"))
        P.dma("sp", BTOK[:, 0:768], b_in[1024:1792].partition_broadcast(128), writes=bl("BTOK"))
        P.dma("sp", BTOK[:, 768:896], b_in[640:768].partition_broadcast(128), writes=bl("BTOK"))
        P.op("pool", lambda e: e.memset(WG2[:], 0.0), writes=bl("WG2"))
        P.dma("sp", WG2[0:16, :], wg2, writes=bl("WG2"))
        P.dma("sp", WG2[16:17, :], bg2.rearrange("(o n) -> o n", o=1), writes=bl("WG2"))
        P.op("pool", lambda e: e.memset(BCOL[:], 0.0), writes=bl("BCOL"))
        col = lambda off, n: b_in[off:off + n].rearrange("(p o) -> p o", o=1)
        for h in range(8):
            P.dma("sp", BCOL[0:64, h:h + 1], col(64 * h, 64), writes=bl("BCOL"))
        for k in range(2):
            P.dma("sp", BCOL[0:64, 8 + k:9 + k], col(512 + 64 * k, 64), writes=bl("BCOL"))
        for hh in range(4):
            P.dma("sp", BCOL[0:64, 10 + hh:11 + hh], col(768 + 64 * hh, 64), writes=bl("BCOL"))
            P.dma("sp", BCOL[0:64, 14 + hh:15 + hh], col(1024 + 64 * hh, 64), writes=bl("BCOL"))
            P.dma("sp", BCOL[:, 18 + hh:19 + hh], col(1792 + 128 * hh, 128), writes=bl("BCOL"))
        P.dma("sp", BCOL[0:16, 22:23], col(2304, 16), writes=bl("BCOL"))
        P.dma("sp", BCOL[:, 23:24], gnorm.rearrange("(p o) -> p o", o=1), writes=bl("BCOL"))
        P.op("act", lambda e: e.mul(out=BCOL[0:64, 0:8], in_=BCOL[0:64, 0:8], mul=0.125), reads=bl("BCOL"), writes=bl("BCOL"))
        P.op("act", lambda e: e.mul(out=BCOL[:, 24:28], in_=BCOL[:, 18:22], mul=-1.0), reads=bl("BCOL"), writes=bl("BCOL"))
        P.op("act", lambda e: e.mul(out=BCOL[:, 28:29], in_=BCOL[:, 23:24], mul=0.5), reads=bl("BCOL"), writes=bl("BCOL"))
        P.op("dve", lambda e: e.memset(NEGH[:], -0.5), writes=bl("CONST"))
        P.dma("sp", SINK32[32:33, :], sinks.rearrange("(o n) -> o n", o=1), writes=bl("SINK32"))
        P.op("act", lambda e: e.activation(out=SINK32[32:33, :], in_=SINK32[32:33, :], func=AF.Exp),
             reads=bl("SINK32"), writes=bl("SINK32"))
        P.op("dve", lambda e: e.memset(ONESF[:], 1.0), writes=bl("CONST"))
        P.op("dve", lambda e: e.memset(ONESB[:], 1.0), writes=bl("CONST"))
        P.op("dve", lambda e: e.memset(GLR[:], 1.0), writes=bl("GLR"))
        P.op("dve", lambda e: e.memset(PTM[:], 0.0), writes=bl("PTMx"))
        P.op("dve", lambda e: e.memset(VM[:], 0.0), writes=bl("VM"))
        for h in range(8):
            k, g = h // 4, h % 4
            P.op("dve", (lambda k, g, h: lambda e: e.tensor_scalar(
                out=PTM[32:33, k, g * 128:(g + 1) * 128], in0=ONESF[32:33, :], scalar1=SINK32[32:33, h:h + 1],
                scalar2=None, op0=ALU.mult))(k, g, h), reads=bl("SINK32", "CONST"), writes=bl("PTMx"))
        P.dma("pool", W_OUT[:], w_out.rearrange("(k p) n -> p k n", p=128), writes=bl("W_OUT"))
        def casts_more():
            return iter(())

        def casts():
            for fc in range(NFC):
                for m, w in enumerate((w_fg, w_fu)):
                    P.dma("pool", wgu_s[fc, :, m, :, :], w[:, fc * 128:(fc + 1) * 128].rearrange("(k p) n -> p k n", p=128),
                          writes=bl(f"wgu_s{fc}"))
                yield 1
            for hf in range(2):
                for s in range(11):
                    P.dma("pool", wd_s[hf, s], w_fd[s * 256:(s + 1) * 256, hf * 512:(hf + 1) * 512]
                          .rearrange("(f p) n -> p f n", p=128), writes=bl(f"wd_s{hf}_{s}"))
                    yield 1

        def ckpt(k):
            if stage == k:
                P.finish()
                P.emit()
                raise _Stop(nc)

        ckpt(0)
        ctr = {"ls": 0, "xs": 0}

        def layer_norm_stats(src, np_, rb):
            i = ctr["ls"] % NLS
            ctr["ls"] += 1
            L, lb = LST[i], bl(f"LST{i}")
            P.op("dve", lambda e: e.bn_stats(out=L[0:np_, 0:6], in_=src[:, 0:512]), reads=rb, writes=lb)
            P.op("dve", lambda e: e.bn_stats(out=L[0:np_, 6:12], in_=src[:, 512:1024]), reads=rb, writes=lb)
            P.op("dve", lambda e: e.bn_aggr(out=L[0:np_, 12:14], in_=L[0:np_, 0:12]), reads=lb, writes=lb)
            P.op("dve", lambda e: e.tensor_scalar(out=L[0:np_, 15:16], in0=L[0:np_, 12:13], scalar1=-1.0, scalar2=None, op0=ALU.mult),
                 reads=lb, writes=lb)
            P.op("act", lambda e: e.activation(out=L[0:np_, 14:15], in_=L[0:np_, 13:14], func=AF.Ln, bias=LN_EPS, scale=1.0), reads=lb, writes=lb)
            P.op("act", lambda e: e.activation(out=L[0:np_, 14:15], in_=L[0:np_, 14:15], func=AF.Exp, scale=-0.5), reads=lb, writes=lb)
            P.op("act", lambda e: e.activation(out=L[0:np_, 15:16], in_=L[0:np_, 15:16], func=AF.Identity, scale=L[0:np_, 14:15]), reads=lb, writes=lb)
            return L, lb

        def layer_norm_apply(src, dst, np_, gi, rb, wb, L, lb):
            P.op("act", lambda e: e.activation(out=dst, in_=src, func=AF.Identity, bias=L[0:np_, 15:16], scale=L[0:np_, 14:15]),
                 reads=rb + lb, writes=wb)
            P.op("dve", lambda e: e.tensor_tensor(out=dst, in0=dst, in1=LNC[0:np_, gi, :], op=ALU.mult),
                 reads=wb + bl("LNC"), writes=wb)
            P.op("pool", lambda e: e.tensor_tensor(out=dst, in0=dst, in1=LNC[0:np_, gi + 1, :], op=ALU.add),
                 reads=wb + bl("LNC"), writes=wb)

        def next_xs():
            i = ctr["xs"] % NXS
            ctr["xs"] += 1
            return XS[i], bl(f"XS{i}")

        def ln_in_tile(T):
            xs, xb = XS[T % 2], bl(f"XS{T % 2}")
            P.dma("sp", xs[:], x[T * 128:(T + 1) * 128, :], writes=xb)
            L, lb = layer_norm_stats(xs[:, :], 128, xb)
            layer_norm_apply(xs[:, :], xs[:, :], 128, 0, xb, xb, L, lb)
            return xs, xb

        def transpose_tile(src, srcb, HTd, dstb, j, banks):
            for half, eng in ((0, "act"), (1, "dve")):
                pb = banks[half]
                def tr(e, half=half, pb=pb):
                    for k in range(4):
                        kk = half * 4 + k
                        i = e.transpose(out=PB[pb][:, k * 128:(k + 1) * 128], in_=src[:, kk * 128:(kk + 1) * 128],
                                        identity=IDENT[:])
                    return i
                P.op("pe", tr, reads=srcb + bl("CONST"), writes=bl(f"PB{pb}"))
                dst = HTd[:, half * 4:(half + 1) * 4, j * 128:(j + 1) * 128]
                psv = PB[pb][:].rearrange("p (k t) -> p k t", k=4)
                if eng == "act":
                    P.op("act", lambda e, dst=dst, psv=psv: e.activation(out=dst, in_=psv, func=AF.Copy),
                         reads=bl(f"PB{pb}"), writes=dstb)
                else:
                    P.op("dve", lambda e, dst=dst, psv=psv: e.tensor_copy(out=dst, in_=psv),
                         reads=bl(f"PB{pb}"), writes=dstb)

        def mm_group(e, out_ap, pairs):
            n = len(pairs)
            for idx, (l, r) in enumerate(pairs):
                i = e.matmul(out_ap, lhsT=l, rhs=r, start=(idx == 0), stop=(idx == n - 1))
            return i

        HT0all = [f"HT0_{j}" for j in range(4)]
        HT1all = [f"HT1_{j}" for j in range(4)]

        def gate_pipeline(ntok, tcol, pbi):
            np_ = ntok
            pbn = f"PB{pbi}"
            P.op("pe", lambda e: e.matmul(PB[pbi][0:np_, 0:256], lhsT=GLR[0:32, tcol:tcol + ntok], rhs=WG2[:, :],
                                          start=True, stop=True), reads=bl("GLR", "WG2"), writes=bl(pbn))
            P.op("act", lambda e: e.activation(out=EX[0:np_, :], in_=PB[pbi][0:np_, 0:256], func=AF.Exp, scale=-1.0),
                 reads=bl(pbn), writes=bl("GB"))
            P.op("act", lambda e: e.activation(out=SPL[0:np_, :], in_=EX[0:np_, :], func=AF.Ln, bias=1.0),
                 reads=bl("GB"), writes=bl("GB"))
            P.op("pe", lambda e: e.matmul(PB[pbi][0:np_, 256:512], lhsT=TRI2[0:np_, 0:np_], rhs=SPL[0:np_, :],
                                          start=True, stop=True), reads=bl("GB", "CONST"), writes=bl(pbn))
            P.op("act", lambda e: e.activation(out=ER[0:np_, :], in_=PB[pbi][0:np_, 256:512], func=AF.Exp),
                 reads=bl(pbn), writes=bl("SQ"))

        L, lb = layer_norm_stats(XS[0][0:16, :], 16, bl("XS0"))
        layer_norm_apply(XS[0][0:16, :], XS[0][0:16, :], 16, 0, bl("XS0"), bl("XS0"), L, lb)
        def trm(e):
            for kk in range(8):
                i = e.transpose(out=PB[0][:, kk * 16:(kk + 1) * 16], in_=XS[0][0:16, kk * 128:(kk + 1) * 128],
                                identity=IDENT[0:16, 0:16])
            return i
        P.op("pe", trm, reads=bl("XS0", "CONST"), writes=bl("PB0"))
        P.op("act", lambda e: e.activation(out=HMT, in_=PB[0][:, 0:128].rearrange("p (k t) -> p k t", k=8), func=AF.Copy),
             reads=bl("PB0"), writes=bl("AT0"))
        P.op("pe", lambda e: mm_group(e, PB[2][0:16, 0:512], [(HMT[:, kk, :], W_IN[:, kk, 1024:1536]) for kk in range(8)]),
             reads=bl("AT0", "W_IN"), writes=bl("PB2"))
        def mmeta3(e):
            mm_group(e, PB[3][0:16, 0:256], [(HMT[:, kk, :], W_IN[:, kk, 1536:1792]) for kk in range(8)])
            return mm_group(e, PB[3][0:16, 256:384], [(HMT[:, kk, :], W_IN[:, kk, 640:768]) for kk in range(8)])
        P.op("pe", mmeta3, reads=bl("AT0", "W_IN"), writes=bl("PB3"))
        def mmeta4(e):
            for k in range(2):
                mm_group(e, PB[1][0:64, k * 16:(k + 1) * 16],
                         [(W_IN[:, kk, 512 + 64 * k:576 + 64 * k], HMT[:, kk, :]) for kk in range(8)])
            return mm_group(e, PB[1][0:16, 64:80], [(W_IN[:, kk, 2304:2320], HMT[:, kk, :]) for kk in range(8)])
        P.op("pe", mmeta4, reads=bl("AT0", "W_IN"), writes=bl("PB1"))
        for k in range(2):
            P.op("act", (lambda k: lambda e: e.activation(
                out=KMT[0:64, k, :, :], in_=PB[1][0:64, k * 16:(k + 1) * 16].unsqueeze(1).broadcast_to([64, 32, 16]),
                func=AF.Identity, bias=BCOL[0:64, 8 + k:9 + k], scale=1.0))(k), reads=bl("PB1", "BCOL"), writes=bl("KMT"))
        P.op("act", lambda e: e.activation(out=GLR[0:16, 0:16], in_=PB[1][0:16, 64:80], func=AF.Identity,
                                           bias=BCOL[0:16, 22:23], scale=1.0), reads=bl("PB1", "BCOL"), writes=bl("GLR"))
        P.op("dve", lambda e: e.tensor_tensor(out=VM[0:16, :, :], in0=PB[3][0:16, 256:384].rearrange("p (k d) -> p k d", k=2),
                                              in1=BTOK[0:16, 768:896].rearrange("p (k d) -> p k d", k=2), op=ALU.add),
             reads=bl("PB3", "BTOK"), writes=bl("VM"))
        P.op("dve", lambda e: e.tensor_tensor(out=VG[0:16, 0:256], in0=PB[2][0:16, 256:512], in1=BTOK[0:16, 256:512], op=ALU.add),
             reads=bl("PB2", "BTOK"), writes=bl("VG"))
        P.op("dve", lambda e: e.tensor_tensor(out=VG[0:16, 256:512], in0=PB[3][0:16, 0:256], in1=BTOK[0:16, 512:768], op=ALU.add),
             reads=bl("PB3", "BTOK"), writes=bl("VG"))
        P.op("dve", lambda e: e.tensor_tensor(out=KTMP[0:16, :], in0=PB[2][0:16, 0:256], in1=BTOK[0:16, 0:256], op=ALU.add),
             reads=bl("PB2", "BTOK"), writes=bl("SQ"))
        gate_pipeline(16, 0, 0)
        P.op("dve", lambda e: e.tensor_tensor(out=KH[0:16, :], in0=KTMP[0:16, :], in1=ER[0:16, :], op=ALU.mult),
             reads=bl("SQ"), writes=bl("KH"))
        def mstate0(e):
            for hh in range(4):
                i = e.matmul(PB[2][0:64, hh * 128:(hh + 1) * 128], lhsT=KH[0:16, hh * 64:(hh + 1) * 64],
                             rhs=VG[0:16, hh * 128:(hh + 1) * 128], start=True, stop=True)
            return i
        P.op("pe", mstate0, reads=bl("KH", "VG"), writes=bl("PB2"))
        P.op("dve", lambda e: e.tensor_copy(out=S32[:], in_=PB[2][0:64, :].rearrange("p (h c) -> p h c", h=4)),
             reads=bl("PB2"), writes=bl("S32"))
        P.op("act", lambda e: e.activation(out=SBF[:], in_=S32[:], func=AF.Copy), reads=bl("S32"), writes=bl("SBF"))
        ckpt(1)

        def merge(g1, g2):
            gens = [g1, g2]
            live = [True, True]
            i = 0
            while live[0] or live[1]:
                if live[i]:
                    try:
                        yield next(gens[i])
                    except StopIteration:
                        live[i] = False
                i ^= 1

        def mixer_head(b):
            def tick():
                return 1
            if b > 0:
                P.op("act", lambda e: e.activation(out=KST[0:64, :, 0:128], in_=KST[0:64, :, 512:640], func=AF.Copy),
                     reads=bl("KST3"), writes=bl("KSTp"))
                P.op("dve", lambda e: e.tensor_copy(out=VS[:, 0, :], in_=VS[:, 4, :]), reads=bl("VS3"), writes=bl("VSp"))
            for j in range(4):
                T = 4 * b + j
                xs, xb = ln_in_tile(T)
                if debug:
                    P.dma("pool", dbg_h0[T * 128:(T + 1) * 128, :], xs[:, :], reads=xb)
                yield 6.0
                transpose_tile(xs, xb, HT0, bl(f"HT0_{j}"), j, (0, 1))
                yield 1.5
            yield from m_qk(b)
            yield from merge(m_gla(b), m_swa(b))
            if debug:
                P.dma("pool", dbg_o[b, :, 0:4, :], OST[:], reads=bl(*[f"OST{j}" for j in range(4)]))
                P.dma("pool", dbg_o[b, :, 4:8, :], OGT[:], reads=bl(*[f"OGT{j}" for j in range(4)]))

        def m_gla(b):
            def tick():
                return 1
            for j in range(4):
                tc = slice(j * 128, (j + 1) * 128)
                hb = f"HT0_{j}"
                P.op("pe", lambda e, tc=tc: mm_group(e, PB[2][:, 0:512], [(HT0[:, kk, tc], W_IN[:, kk, 1024:1536]) for kk in range(8)]),
                     reads=bl(hb, "W_IN"), writes=bl("PB2"))
                P.op("pe", lambda e, tc=tc: mm_group(e, PB[3][:, 0:256], [(HT0[:, kk, tc], W_IN[:, kk, 1536:1792]) for kk in range(8)]),
                     reads=bl(hb, "W_IN"), writes=bl("PB3"))
                P.op("dve", lambda e: e.tensor_tensor(out=VG[:, 0:256], in0=PB[2][:, 256:512], in1=BTOK[:, 256:512], op=ALU.add),
                     reads=bl("PB2", "BTOK"), writes=bl("VG"))
                P.op("dve", lambda e: e.tensor_tensor(out=VG[:, 256:512], in0=PB[3][:, 0:256], in1=BTOK[:, 512:768], op=ALU.add),
                     reads=bl("PB3", "BTOK"), writes=bl("VG"))
                P.op("dve", lambda e: e.tensor_tensor(out=KTMP[:, :], in0=PB[2][:, 0:256], in1=BTOK[:, 0:256], op=ALU.add),
                     reads=bl("PB2", "BTOK"), writes=bl("SQ"))
                yield 1.0
                if j == 0:
                    P.op("pe", lambda e: mm_group(e, PB[0][0:16, :], [(W_IN[:, kk, 2304:2320], HT0[:, kk, :]) for kk in range(8)]),
                         reads=bl(*HT0all, "W_IN"), writes=bl("PB0"))
                    P.op("act", lambda e: e.activation(out=GLR[0:16, :], in_=PB[0][0:16, :], func=AF.Identity,
                                                       bias=BCOL[0:16, 22:23], scale=1.0), reads=bl("PB0", "BCOL"), writes=bl("GLR"))
                gate_pipeline(128, j * 128, 0)
                P.op("dve", lambda e: e.tensor_tensor(out=KH[:, :], in0=KTMP[:, :], in1=ER[:, :], op=ALU.mult),
                     reads=bl("SQ"), writes=bl("KH"))
                yield 3.0
                def mbt(e):
                    for hh in range(4):
                        i = e.matmul(PB[1][0:64, hh * 128:(hh + 1) * 128], lhsT=SPL[:, hh * 64:(hh + 1) * 64], rhs=TRI1[:, :],
                                     start=True, stop=True)
                    return i
                P.op("pe", mbt, reads=bl("GB", "CONST"), writes=bl("PB1"))
                pbv = PB[1][0:64, :].rearrange("p (h c) -> p h c", h=4)
                P.op("act", lambda e, pbv=pbv: e.activation(out=EB1[:], in_=pbv, func=AF.Exp, bias=LN8, scale=1.0),
                     reads=bl("PB1"), writes=bl("EB1"))
                P.op("act", lambda e, pbv=pbv: e.activation(out=EBL[:, :], in_=pbv[:, :, 127], func=AF.Exp),
                     reads=bl("PB1"), writes=bl("EBL"))
                P.op("act", lambda e, pbv=pbv: e.activation(out=EB2, in_=pbv, func=AF.Exp, scale=-1.0),
                     reads=bl("PB1"), writes=bl("GB"))
                yield 1.5
                def mqg(e, tc=tc):
                    for hh in range(4):
                        i = mm_group(e, PB[2][0:64, hh * 128:(hh + 1) * 128],
                                     [(W_IN[:, kk, 768 + 64 * hh:832 + 64 * hh], HT0[:, kk, tc]) for kk in range(8)])
                    return i
                P.op("pe", mqg, reads=bl(hb, "W_IN"), writes=bl("PB2"))
                def mkg(e, tc=tc):
                    for hh in range(4):
                        i = mm_group(e, PB[3][0:64, hh * 128:(hh + 1) * 128],
                                     [(W_IN[:, kk, 1024 + 64 * hh:1088 + 64 * hh], HT0[:, kk, tc]) for kk in range(8)])
                    return i
                P.op("pe", mkg, reads=bl(hb, "W_IN"), writes=bl("PB3"))
                for hh in range(4):
                    P.op("dve", lambda e, hh=hh: e.scalar_tensor_tensor(
                        out=QTT[:, hh, :], in0=PB[2][0:64, hh * 128:(hh + 1) * 128], scalar=BCOL[0:64, 10 + hh:11 + hh],
                        in1=EB1[:, hh, :], op0=ALU.add, op1=ALU.mult), reads=bl("PB2", "BCOL", "EB1"), writes=bl("QTT"))
                    P.op("dve", lambda e, hh=hh: e.scalar_tensor_tensor(
                        out=KTT[:, hh, :], in0=PB[3][0:64, hh * 128:(hh + 1) * 128], scalar=BCOL[0:64, 14 + hh:15 + hh],
                        in1=EB2[:, hh, :], op0=ALU.add, op1=ALU.mult), reads=bl("PB3", "BCOL", "GB"), writes=bl("KTT"))
                yield 2.5
                def ma(e):
                    for hh in range(4):
                        i = e.matmul(PB[0][:, hh * 128:(hh + 1) * 128], lhsT=KTT[:, hh, :], rhs=QTT[:, hh, :], start=True, stop=True)
                    return i
                P.op("pe", ma, reads=bl("KTT", "QTT"), writes=bl("PB0"))
                P.op("dve", lambda e: e.tensor_tensor(out=AM[:], in0=PB[0][:].rearrange("p (h c) -> p h c", h=4),
                                                      in1=MASKU[:].unsqueeze(1).broadcast_to([128, 4, 128]), op=ALU.mult),
                     reads=bl("PB0", "CONST"), writes=bl("AM"))
                yield 0.7
                def mo(e):
                    for hh in range(4):
                        e.matmul(PB[1][:, hh * 128:(hh + 1) * 128], lhsT=VG[:, hh * 128:(hh + 1) * 128], rhs=AM[:, hh, :],
                                 start=True, stop=False)
                        i = e.matmul(PB[1][:, hh * 128:(hh + 1) * 128], lhsT=SBF[:, hh, :], rhs=QTT[:, hh, :], start=False, stop=True)
                    return i
                P.op("pe", mo, reads=bl("VG", "AM", "SBF", "QTT"), writes=bl("PB1"))
                def mst(e):
                    for hh in range(4):
                        i = e.matmul(PB[2][0:64, hh * 128:(hh + 1) * 128], lhsT=KH[:, hh * 64:(hh + 1) * 64],
                                     rhs=VG[:, hh * 128:(hh + 1) * 128], start=True, stop=True)
                    return i
                P.op("pe", mst, reads=bl("KH", "VG"), writes=bl("PB2"))
                P.op("dve", lambda e: e.tensor_tensor(out=S32[:], in0=S32[:], in1=EBL[:, :].unsqueeze(2).broadcast_to([64, 4, 128]), op=ALU.mult),
                     reads=bl("S32", "EBL"), writes=bl("S32"))
                P.op("dve", lambda e: e.tensor_tensor(out=S32[:], in0=S32[:], in1=PB[2][0:64, :].rearrange("p (h c) -> p h c", h=4), op=ALU.add),
                     reads=bl("S32", "PB2"), writes=bl("S32"))
                P.op("act", lambda e: e.activation(out=SBF[:], in_=S32[:], func=AF.Copy), reads=bl("S32"), writes=bl("SBF"))
                P.op("act", lambda e: e.activation(out=SQ[:], in_=PB[1][:], func=AF.Square), reads=bl("PB1"), writes=bl("SQ"))
                P.op("pe", lambda e: e.matmul(PB[0][:, :], lhsT=ONESF[:, :], rhs=SQ[:, :], start=True, stop=True),
                     reads=bl("SQ", "CONST"), writes=bl("PB0"))
                P.op("act", lambda e: e.activation(out=RSTD[:], in_=PB[0][:], func=AF.Ln, bias=RMS_EPS, scale=1.0 / 128.0),
                     reads=bl("PB0"), writes=bl("RSTD"))
                P.op("act", lambda e: e.activation(out=RSTD[:], in_=RSTD[:], func=AF.Exp, scale=-0.5), reads=bl("RSTD"), writes=bl("RSTD"))
                P.op("dve", lambda e: e.scalar_tensor_tensor(out=OG32[:], in0=PB[1][:], scalar=BCOL[:, 23:24], in1=RSTD[:],
                                                             op0=ALU.mult, op1=ALU.mult), reads=bl("PB1", "BCOL", "RSTD"), writes=bl("SQ"))
                yield 3.0
                def mrg(e, tc=tc):
                    for hh in range(4):
                        i = mm_group(e, PB[3][:, hh * 128:(hh + 1) * 128],
                                     [(W_IN[:, kk, 1792 + 128 * hh:1920 + 128 * hh], HT0[:, kk, tc]) for kk in range(8)])
                    return i
                P.op("pe", mrg, reads=bl(hb, "W_IN"), writes=bl("PB3"))
                for hh in range(4):
                    P.op("act", lambda e, hh=hh: e.activation(out=SGM[:, hh * 128:(hh + 1) * 128], in_=PB[3][:, hh * 128:(hh + 1) * 128],
                                                              func=AF.Exp, bias=BCOL[:, 24 + hh:25 + hh], scale=-1.0),
                         reads=bl("PB3", "BCOL"), writes=bl("GB"))
                P.op("act", lambda e: e.activation(out=SGM[:], in_=SGM[:], func=AF.Ln, bias=1.0), reads=bl("GB"), writes=bl("GB"))
                P.op("act", lambda e: e.activation(out=SGM[:], in_=SGM[:], func=AF.Exp, scale=-1.0), reads=bl("GB"), writes=bl("GB"))
                P.op("dve", lambda e: e.tensor_tensor(out=SGM[:], in0=SGM[:], in1=OG32[:], op=ALU.mult),
                     reads=bl("SQ", "GB"), writes=bl("GB"))
                for hh in range(4):
                    P.op("dve", lambda e, hh=hh, tc=tc: e.scalar_tensor_tensor(
                        out=OGT[:, hh, tc], in0=PB[3][:, hh * 128:(hh + 1) * 128], scalar=BCOL[:, 18 + hh:19 + hh],
                        in1=SGM[:, hh * 128:(hh + 1) * 128], op0=ALU.add, op1=ALU.mult),
                        reads=bl("PB3", "BCOL", "GB"), writes=bl(f"OGT{j}"))
                yield 1.5
        def m_qk(b):
            def mvs(e):
                for j in range(4):
                    i = mm_group(e, PB[2][:, j * 128:(j + 1) * 128],
                                 [(HT0[:, kk, j * 128:(j + 1) * 128], W_IN[:, kk, 640:768]) for kk in range(8)])
                return i
            P.op("pe", mvs, reads=bl(*HT0all, "W_IN"), writes=bl("PB2"))
            P.op("dve", lambda e: e.tensor_tensor(out=VS[:, 1:5, :], in0=PB[2][:, :].rearrange("p (j c) -> p j c", j=4),
                                                  in1=BTOK[:, 768:896].unsqueeze(1).broadcast_to([128, 4, 128]), op=ALU.add),
                 reads=bl("PB2", "BTOK"), writes=bl("VS0", "VS1", "VS2", "VS3"))
            yield 0.3
            for h in range(8):
                pb = h % 2
                P.op("pe", lambda e, h=h, pb=pb: mm_group(e, PB[pb][0:64, :], [(W_IN[:, kk, 64 * h:64 * h + 64], HT0[:, kk, :]) for kk in range(8)]),
                     reads=bl(*HT0all, "W_IN"), writes=bl(f"PB{pb}"))
                P.op("act", lambda e, h=h, pb=pb: e.activation(out=QST[0:64, h, :], in_=PB[pb][0:64, :], func=AF.Identity,
                                                               bias=BCOL[0:64, h:h + 1], scale=0.125),
                     reads=bl(f"PB{pb}", "BCOL"), writes=bl(*[f"QST{j}" for j in range(4)]))
                yield 0.3
            for k in range(2):
                pb = 2 + k
                P.op("pe", lambda e, k=k, pb=pb: mm_group(e, PB[pb][0:64, :], [(W_IN[:, kk, 512 + 64 * k:576 + 64 * k], HT0[:, kk, :]) for kk in range(8)]),
                     reads=bl(*HT0all, "W_IN"), writes=bl(f"PB{pb}"))
                P.op("act", lambda e, k=k, pb=pb: e.activation(out=KST[0:64, k, 128:640], in_=PB[pb][0:64, :], func=AF.Identity,
                                                               bias=BCOL[0:64, 8 + k:9 + k], scale=1.0),
                     reads=bl(f"PB{pb}", "BCOL"), writes=bl(*[f"KST{j}" for j in range(4)]))
                yield 0.3

        def m_swa(b):
            def tick():
                return 1
            for j in range(4):
                T = 4 * b + j
                tc = slice(j * 128, (j + 1) * 128)
                for k in range(2):
                    qv = QST[:, 4 * k:4 * k + 4, tc]
                    kprev_b = f"KST{j - 1}" if j > 0 else "KSTp"
                    vprev_b = f"VS{j - 1}" if j > 0 else "VSp"
                    P.op("pe", lambda e, k=k, j=j, qv=qv: e.matmul(PB[0][:, :], lhsT=KST[0:66, k, 128 + 128 * j:256 + 128 * j], rhs=qv[0:66],
                                                                   start=True, stop=True),
                         reads=bl(f"KST{j}", "KSTx", f"QST{j}", "QSTx"), writes=bl("PB0"))
                    if T > 0:
                        P.op("pe", lambda e, k=k, j=j, qv=qv: e.matmul(PB[1][:, :], lhsT=KST[0:67, k, 128 * j:128 + 128 * j], rhs=qv[0:67],
                                                                       start=True, stop=True),
                             reads=bl(kprev_b, "KSTx", f"QST{j}", "QSTx"), writes=bl("PB1"))
                    P.op("pe", lambda e, k=k, T=T, qv=qv: e.matmul(PB[2][0:16, :], lhsT=KMT[0:67, k, T, :], rhs=qv[0:67], start=True, stop=True),
                         reads=bl("KMT", "KMTx", f"QST{j}", "QSTx"), writes=bl("PB2"))
                    P.op("act", lambda e: e.activation(out=PT[:, 0, :], in_=PB[0][:, :], func=AF.Exp), reads=bl("PB0"), writes=bl("PTc"))
                    P.op("pool", lambda e: e.tensor_tensor(out=PT[:, 0, :].rearrange("p (g q) -> p g q", g=4),
                                                           in0=PT[:, 0, :].rearrange("p (g q) -> p g q", g=4),
                                                           in1=MCUR[:].unsqueeze(1).broadcast_to([128, 4, 128]), op=ALU.mult),
                         reads=bl("PTc", "CONST"), writes=bl("PTc"))
                    if T > 0:
                        P.op("act", lambda e: e.activation(out=PT[:, 1, :], in_=PB[1][:, :], func=AF.Exp), reads=bl("PB1"), writes=bl("PTp"))
                        P.op("pool", lambda e: e.tensor_tensor(out=PT[:, 1, :].rearrange("p (g q) -> p g q", g=4),
                                                               in0=PT[:, 1, :].rearrange("p (g q) -> p g q", g=4),
                                                               in1=MPREV[:].unsqueeze(1).broadcast_to([128, 4, 128]), op=ALU.mult),
                             reads=bl("PTp", "CONST"), writes=bl("PTp"))
                    P.op("act", lambda e, k=k: e.activation(out=PTM[0:16, k, :], in_=PB[2][0:16, :], func=AF.Exp), reads=bl("PB2"), writes=bl("PTM"))
                    yield 3.0
                    ptc = PT[:, 0, :].rearrange("p (a b q) -> p a b q", a=2, b=2)
                    ptp = PT[:, 1, :].rearrange("p (a b q) -> p a b q", a=2, b=2)
                    def mpv(e, k=k, j=j, T=T, ptc=ptc, ptp=ptp):
                        for hf in range(2):
                            o = PB[3][64 * hf:64 * hf + 64, 0:256]
                            e.matmul(o, lhsT=VS[:, j + 1, k * 64:(k + 1) * 64], rhs=ptc[:, :, hf, :], start=True, stop=False)
                            if T > 0:
                                e.matmul(o, lhsT=VS[:, j, k * 64:(k + 1) * 64], rhs=ptp[:, :, hf, :], start=False, stop=False)
                            i = e.matmul(o, lhsT=VM[0:33, k, :], rhs=PTM[0:33, k, :].rearrange("p (a b q) -> p a b q", a=2, b=2)[:, :, hf, :],
                                         start=False, stop=True)
                        for hf in range(2):
                            o = PB[3][64 * hf:64 * hf + 64, 256:512]
                            e.matmul(o, lhsT=ONESB[:, 0:64], rhs=ptc[:, :, hf, :], start=True, stop=False)
                            if T > 0:
                                e.matmul(o, lhsT=ONESB[:, 0:64], rhs=ptp[:, :, hf, :], start=False, stop=False)
                            i = e.matmul(o, lhsT=ONESM[0:33, 0:64], rhs=PTM[0:33, k, :].rearrange("p (a b q) -> p a b q", a=2, b=2)[:, :, hf, :],
                                         start=False, stop=True)
                        return i
                    P.op("pe", mpv, reads=bl(f"VS{j}", vprev_b, "PTc", "PTp", "PTM", "PTMx", "VM", "CONST"), writes=bl("PB3"))
                    P.op("dve", lambda e: e.reciprocal(out=RDEN[:, 0:256], in_=PB[3][:, 256:512]), reads=bl("PB3"), writes=bl("RDEN"))
                    P.op("dve", lambda e, k=k, tc=tc: e.tensor_tensor(
                        out=OST[:, 2 * k:2 * k + 2, tc], in0=PB[3][:, 0:256].rearrange("p (a q) -> p a q", a=2),
                        in1=RDEN[:, 0:256].rearrange("p (a q) -> p a q", a=2), op=ALU.mult),
                        reads=bl("PB3", "RDEN"), writes=bl(f"OST{j}"))
                    yield 2.0

        def ffn(b):
            nst = 44.0
            for fc in range(NFC):
                wb = (b * NFC + fc) % 2
                P.dma("sp", WGU[wb][:], wgu_s[fc], reads=bl(f"wgu_s{fc}"), writes=bl(f"WGU{wb}"))
                pg, pu = (4, 5) if fc % 2 == 0 else (6, 7)
                P.op("pe", lambda e, wb=wb, pg=pg: mm_group(e, PB[pg][:, :], [(WGU[wb][:, 0, kk, :], HT1[:, kk, :]) for kk in range(8)]),
                     reads=bl(f"WGU{wb}", *HT1all), writes=bl(f"PB{pg}"))
                P.op("pe", lambda e, wb=wb, pu=pu: mm_group(e, PB[pu][:, :], [(WGU[wb][:, 1, kk, :], HT1[:, kk, :]) for kk in range(8)]),
                     reads=bl(f"WGU{wb}", *HT1all), writes=bl(f"PB{pu}"))
                P.op("act", lambda e, pg=pg: e.activation(out=SGF[:], in_=PB[pg][:, :], func=AF.Exp, scale=-1.0), reads=bl(f"PB{pg}"), writes=bl("SGF"))
                P.op("act", lambda e: e.activation(out=SGF[:], in_=SGF[:], func=AF.Ln, bias=1.0), reads=bl("SGF"), writes=bl("SGF"))
                P.op("act", lambda e: e.activation(out=SGF[:], in_=SGF[:], func=AF.Exp, scale=-1.0), reads=bl("SGF"), writes=bl("SGF"))
                P.op("dve", lambda e, pg=pg: e.tensor_tensor(out=SGF[:], in0=SGF[:], in1=PB[pg][:, :], op=ALU.mult),
                     reads=bl("SGF", f"PB{pg}"), writes=bl("SGF"))
                P.op("dve", lambda e, fc=fc, pu=pu: e.tensor_tensor(out=AT[:, fc, :], in0=SGF[:], in1=PB[pu][:, :], op=ALU.mult),
                     reads=bl("SGF", f"PB{pu}"), writes=bl(f"AT{fc}"))
                yield (fc + 1) / nst
            for hf in range(2):
                cs = slice(hf * 512, (hf + 1) * 512)
                for s in range(11):
                    wb = ((b * 2 + hf) * 11 + s) % 3
                    P.dma("sp", WD[wb][:], wd_s[hf, s], reads=bl(f"wd_s{hf}_{s}"), writes=bl(f"WD{wb}"))
                    def mdn(e, s=s, wb=wb):
                        for sub in range(2):
                            fc = 2 * s + sub
                            for j in range(4):
                                i = e.matmul(PB[4 + j][:, :], lhsT=AT[:, fc, j * 128:(j + 1) * 128], rhs=WD[wb][:, sub, :],
                                             start=(fc == 0), stop=(fc == NFC - 1))
                        return i
                    P.op("pe", mdn, reads=bl(f"WD{wb}", f"AT{2 * s}", f"AT{2 * s + 1}"), writes=bl("PB4", "PB5", "PB6", "PB7"))
                    if s < 10:
                        yield (22 + hf * 11 + s + 1) / nst
                for j in range(4):
                    P.op("dve", lambda e, j=j, cs=cs: e.scalar_tensor_tensor(out=H[:, j, cs], in0=H[:, j, cs], scalar=ALPHA, in1=PB[4 + j][:, :],
                                                                             op0=ALU.mult, op1=ALU.add),
                         reads=bl(f"H{j}", f"PB{4 + j}"), writes=bl(f"H{j}"))
                yield (22 + hf * 11 + 11) / nst

        def ln_a(src, rb):
            i = ctr["ls"] % NLS
            ctr["ls"] += 1
            L, lb = LST[i], bl(f"LST{i}")
            P.op("dve", lambda e: e.bn_stats(out=L[:, 0:6], in_=src[:, 0:512]), reads=rb, writes=lb)
            P.op("dve", lambda e: e.bn_stats(out=L[:, 6:12], in_=src[:, 512:1024]), reads=rb, writes=lb)
            P.op("dve", lambda e: e.bn_aggr(out=L[:, 12:14], in_=L[:, 0:12]), reads=lb, writes=lb)
            P.op("dve", lambda e: e.tensor_scalar(out=L[:, 15:16], in0=L[:, 12:13], scalar1=-1.0, scalar2=None, op0=ALU.mult),
                 reads=lb, writes=lb)
            return L, lb

        def ln_b(L, lb):
            P.op("act", lambda e: e.activation(out=L[:, 14:15], in_=L[:, 13:14], func=AF.Ln, bias=LN_EPS, scale=1.0), reads=lb, writes=lb)
            P.op("act", lambda e: e.activation(out=L[:, 14:15], in_=L[:, 14:15], func=AF.Exp, scale=-0.5), reads=lb, writes=lb)
            P.op("act", lambda e: e.activation(out=L[:, 15:16], in_=L[:, 15:16], func=AF.Identity, scale=L[:, 14:15]), reads=lb, writes=lb)

        def ln_c(L, lb):
            pass

        def ln_n(t, tb, L, lb):
            P.op("act", lambda e: e.activation(out=t, in_=t, func=AF.Identity, bias=L[:, 15:16], scale=L[:, 14:15]),
                 reads=tb + lb, writes=tb)

        def ln_g(t, tb, gi):
            P.op("dve", lambda e: e.tensor_tensor(out=t, in0=t, in1=LNC[:, gi, :], op=ALU.mult), reads=tb + bl("LNC"), writes=tb)

        def ln_bias(t, tb, gi):
            P.op("pool", lambda e: e.tensor_tensor(out=t, in0=t, in1=LNC[:, gi + 1, :], op=ALU.add), reads=tb + bl("LNC"), writes=tb)

        XTB = [bl("AT0", "AT1", "AT2", "AT3"), bl("AT4", "AT5", "AT6", "AT7")]

        def post_f(b):
            items = []
            if b >= 0:
                for j in range(4):
                    items.append((H[:, j, :], bl(f"H{j}"), 4, j))
            if b + 1 < nblk:
                for j in range(2):
                    T = 4 * (b + 1) + j
                    P.dma("sp", XT[j], x[T * 128:(T + 1) * 128, :], writes=XTB[j])
                    items.append((XT[j], XTB[j], 0, None))
            yield 1
            stats = []
            for (t, tb, gi, j) in items:
                stats.append(ln_a(t, tb))
                yield 1
            for (L, lb) in stats:
                ln_b(L, lb)
            yield 1
            for (L, lb) in stats:
                ln_c(L, lb)
            yield 1
            for (t, tb, gi, j), (L, lb) in zip(items, stats):
                ln_n(t, tb, L, lb)
                yield 1
            for (t, tb, gi, j) in items:
                ln_g(t, tb, gi)
                yield 1
            for (t, tb, gi, j) in items:
                ln_bias(t, tb, gi)
                if j is not None:
                    T = 4 * b + j
                    P.dma("pool", out[T * 128:(T + 1) * 128, :], H[:, j, :], reads=bl(f"H{j}"))
                yield 1

        def tails(b):
            XA = [(XT[0], XTB[0]), (XT[1], XTB[1]), (XS[0][:, :], bl("XS0")), (XS[1][:, :], bl("XS1"))]
            for j in range(4):
                tc = slice(j * 128, (j + 1) * 128)
                for hf in range(2):
                    cs = slice(hf * 512, (hf + 1) * 512)
                    pbi = 2 * j + hf
                    P.op("pe", lambda e, tc=tc, cs=cs, pbi=pbi: mm_group(e, PB[pbi][:, :],
                         [((OST[:, kk, tc] if kk < 4 else OGT[:, kk - 4, tc]), W_OUT[:, kk, cs]) for kk in range(8)]),
                         reads=bl(f"OST{j}", f"OGT{j}", "W_OUT"), writes=bl(f"PB{pbi}"))
            for j in range(4):
                xs, xb = XA[j]
                for hf in range(2):
                    cs = slice(hf * 512, (hf + 1) * 512)
                    pbi = 2 * j + hf
                    P.op("dve", lambda e, j=j, cs=cs, pbi=pbi, xs=xs: e.scalar_tensor_tensor(
                        out=H[:, j, cs], in0=xs[:, cs], scalar=ALPHA, in1=PB[pbi][:, :], op0=ALU.mult, op1=ALU.add),
                        reads=xb + bl(f"PB{pbi}"), writes=bl(f"H{j}"))
            items = [(H[:, j, :], bl(f"H{j}"), 2) for j in range(4)]
            stats = [ln_a(t, tb) for (t, tb, gi) in items]
            for (L, lb) in stats:
                ln_b(L, lb)
            for (L, lb) in stats:
                ln_c(L, lb)
            for (t, tb, gi), (L, lb) in zip(items, stats):
                ln_n(t, tb, L, lb)
            for (t, tb, gi) in items:
                ln_g(t, tb, gi)
            for (t, tb, gi) in items:
                ln_bias(t, tb, gi)
            for j in range(4):
                T = 4 * (b + 1) + j
                if debug:
                    P.dma("pool", dbg_h1[T * 128:(T + 1) * 128, :], H[:, j, :], reads=bl(f"H{j}"))
                transpose_tile(H[:, j, :], bl(f"H{j}"), HT1, bl(f"HT1_{j}"), j, (2 * j, 2 * j + 1))

        def run(gen):
            for _ in gen:
                pass

        NM_UNITS = 126.0
        F_FRAC = 0.75

        def step(g):
            try:
                next(g)
                return True
            except StopIteration:
                return False

        def interleave(gm, gf, extras):
            nm = nf = 0
            fl = gf is not None
            el = [True] * len(extras)
            if gm is not None:
                for w in gm:
                    nm += float(w)
                    while fl and nf / 44.0 <= nm / (F_FRAC * NM_UNITS):
                        fl = step(gf)
                        nf += 1
                    if not fl:
                        for i, g in enumerate(extras):
                            if el[i]:
                                el[i] = step(g)
                                if el[i] and getattr(g, "double", False):
                                    el[i] = step(g)
            while fl:
                fl = step(gf)
            for i, g in enumerate(extras):
                while el[i]:
                    el[i] = step(g)

        interleave(mixer_head(0), None, [casts(), casts_more(), post_f(-1)])
        tails(-1)
        ckpt(2)
        for b in range(nblk):
            gm = mixer_head(b + 1) if b + 1 < nblk else None
            interleave(gm, ffn(b), [post_f(b)])
            if b + 1 < nblk:
                tails(b)
        print("sbuf bytes remaining", nc.sbuf_bytes_remaining)
        P.finish()
        P.emit()
    return nc


WEIGHT_KEYS = ["meta_tokens", "ln_in_g", "ln_in_b", "w_in", "b_in", "w_gate_lr2", "b_gate_lr2", "attn_sinks",
               "gla_norm_g", "w_out", "ln1_g", "ln1_b", "w_ffn_gate", "w_ffn_up", "w_ffn_down", "ln2_g", "ln2_b"]


def make_in_map(inputs, bi):
    f = lambda a: np.ascontiguousarray(np.asarray(a, dtype=np.float32))
    m = {"x": f(inputs["x"][bi])}
    for k in WEIGHT_KEYS:
        a = f(inputs[k])
        if k not in ("meta_tokens", "ln_in_g", "ln_in_b"):
            a = a[0]
        m[k] = np.ascontiguousarray(a)
    m.update(host_consts())
    return m


def kernel(**inputs):
    x = np.asarray(inputs["x"])
    Bn, S, _ = x.shape
    nc = build(S // TB)
    in_maps = [make_in_map(inputs, bi) for bi in range(Bn)]
    res = run_bass_kernel_spmd(nc, in_maps, core_ids=list(range(Bn)))
    return np.stack([np.asarray(r["out"], dtype=np.float32) for r in res.results], axis=0)
```

```python
import numpy as np
from contextlib import ExitStack
import ml_dtypes
import concourse.bass as bass
import concourse.mybir as mybir
from concourse.bass_utils import run_bass_kernel_spmd

F32 = mybir.dt.float32
BF16 = mybir.dt.bfloat16
AF = mybir.ActivationFunctionType
ALU = mybir.AluOpType

D = 1024
NM = 16
DIN = 2320
DFF = 2816
NFC = DFF // 128
ALPHA = 2.0 ** 0.25
LN_EPS = 1e-5
RMS_EPS = 1e-6
TB = 512
LN8 = float(np.log(0.125))


class Tk:
    __slots__ = ("sem", "val", "key")

    def __init__(self, sem, val, key):
        self.sem, self.val, self.key = sem, val, key


class Buf:
    def __init__(self, name):
        self.name = name
        self.w = None
        self.r = {}


class Prog:
    ENG = ("pe", "act", "dve", "pool", "sp")
    R = 8

    def __init__(self, nc, es):
        self.nc = nc
        self.q = {e: [] for e in self.ENG}
        self.cnt = {e: 0 for e in self.ENG}
        self.waited = {e: {} for e in self.ENG}
        self.sem = {e: es.enter_context(nc.semaphore("s_" + e)) for e in ("pe", "act", "dve", "pool")}
        self.ring = {qn: [es.enter_context(nc.semaphore(f"d_{qn}{i}")) for i in range(self.R)]
                     for qn in ("sp", "pool")}
        self.dma_n = {"sp": 0, "pool": 0}

    def _deps(self, eng, reads, writes):
        ts = []
        for b in reads:
            if b.w is not None:
                ts.append(b.w)
            if b.name.startswith("PB"):
                ts.extend(t for e2, t in b.r.items() if e2 != eng)
        for b in writes:
            if b.w is not None:
                ts.append(b.w)
            ts.extend(b.r.values())
        best = {}
        for t in ts:
            if self.waited[eng].get(t.key, 0) < t.val:
                if t.key not in best or best[t.key].val < t.val:
                    best[t.key] = t
        for t in best.values():
            self.waited[eng][t.key] = t.val
        return list(best.values())

    def _mark(self, eng, tk, reads, writes):
        for b in reads:
            b.r[eng] = tk
        for b in writes:
            b.w = tk
            b.r = {}

    def op(self, eng, fn, reads=(), writes=()):
        waits = self._deps(eng, reads, writes)
        self.cnt[eng] += 1
        tk = Tk(self.sem[eng], self.cnt[eng], eng)
        self.q[eng].append((waits, fn, (self.sem[eng], 1)))
        self._mark(eng, tk, reads, writes)
        return tk

    def dma(self, qn, out, in_, reads=(), writes=()):
        waits = self._deps(qn, reads, writes)
        j = self.dma_n[qn]
        self.dma_n[qn] += 1
        slot, val = j % self.R, 16 * (j // self.R + 1)
        key = f"{qn}{slot}"
        if val > 16 and self.waited[qn].get(key, 0) < val - 16:
            waits.append(Tk(self.ring[qn][slot], val - 16, key))
            self.waited[qn][key] = val - 16
        tk = Tk(self.ring[qn][slot], val, key)
        self.q[qn].append((waits, lambda e: e.dma_start(out=out, in_=in_), (self.ring[qn][slot], 16)))
        self._mark(qn, tk, reads, writes)
        return tk

    def handoff(self, src, dst):
        best = {}
        for b in src:
            for t in ([b.w] if b.w is not None else []) + list(b.r.values()):
                if t.key not in best or best[t.key].val < t.val:
                    best[t.key] = t
        for b in dst:
            for k, t in best.items():
                b.r["h_" + k] = t

    def finish(self):
        waits = []
        for qn in ("sp", "pool"):
            n = self.dma_n[qn]
            for slot in range(self.R):
                cnt = (n - slot + self.R - 1) // self.R if n > slot else 0
                if cnt > 0:
                    waits.append(Tk(self.ring[qn][slot], 16 * cnt, f"{qn}{slot}"))
        self.q["sp"].append((waits, None, None))

    def emit(self):
        nc = self.nc
        with nc.Block() as block:
            def replay(name):
                def f(e):
                    for waits, fn, inc in self.q[name]:
                        for t in waits:
                            e.wait_ge(t.sem, t.val)
                        if fn is not None:
                            ins = fn(e)
                            ins.then_inc(inc[0], inc[1])
                return f
            block.tensor(replay("pe"))
            block.scalar(replay("act"))
            block.vector(replay("dve"))
            block.gpsimd(replay("pool"))
            block.sync(replay("sp"))


def host_consts():
    bf = ml_dtypes.bfloat16
    j = np.arange(128)[:, None]
    i = np.arange(128)[None, :]
    c = {}
    c["ident"] = np.eye(128, dtype=np.float32)
    c["tri1"] = np.where(j <= i, -1.0 / 16.0, 0.0).astype(np.float32)
    c["tri2"] = np.where(j > i, -1.0 / 16.0, 0.0).astype(np.float32)
    c["masku"] = (j <= i).astype(np.float32).astype(bf)
    c["mcur"] = (j <= i).astype(np.float32).astype(bf)
    c["mprev"] = (j > i).astype(np.float32).astype(bf)
    slopes = 2.0 ** (-8.0 * (np.arange(8) + 1) / 8.0)
    a = np.arange(128)
    qrows = np.zeros((3, 8, TB), np.float32)
    for h in range(8):
        qrows[0, h, :] = slopes[h]
        qrows[1, h, :] = -slopes[h] * np.tile(a, TB // 128)
        qrows[2, h, :] = -128.0 * slopes[h]
    c["qrows"] = qrows.astype(bf)
    kb = np.zeros((3, 2, 640), np.float32)
    kb[0, :, :] = np.tile(a, 5)[None, :]
    kb[1] = 1.0
    kb[2] = 1.0
    c["kbrows"] = kb.astype(bf)
    km = np.zeros((3, 2, 32, 16), np.float32)
    km[0] = (np.arange(16) - 16.0)[None, None, :]
    km[1] = 1.0
    km[2] = np.arange(32)[None, :, None]
    c["kmrows"] = km.astype(bf)
    om = np.zeros((33, 128), np.float32)
    om[0:16] = 1.0
    om[32] = 1.0
    c["onesm"] = om.astype(bf)
    for k, v in c.items():
        assert np.all(np.isfinite(v.astype(np.float32)))
    return c


CONST_SPECS = [("ident", [128, 128], F32), ("tri1", [128, 128], F32), ("tri2", [128, 128], F32),
               ("masku", [128, 128], BF16), ("mcur", [128, 128], BF16), ("mprev", [128, 128], BF16),
               ("qrows", [3, 8, TB], BF16), ("kbrows", [3, 2, 640], BF16),
               ("kmrows", [3, 2, 32, 16], BF16), ("onesm", [33, 128], BF16)]


class _Stop(Exception):
    pass


def build(nblk, debug=False, stage=99):
    try:
        return _build(nblk, debug, stage)
    except _Stop as e:
        return e.args[0]


def _build(nblk, debug=False, stage=99):
    S = nblk * TB
    nc = bass.Bass("TRN2", target_bir_lowering=False)
    di = lambda n, s, d=F32: nc.dram_tensor(n, s, d, kind="ExternalInput").ap()
    x = di("x", [S, D])
    meta = di("meta_tokens", [NM, D])
    lnv = [di(n, [D]) for n in ("ln_in_g", "ln_in_b", "ln1_g", "ln1_b", "ln2_g", "ln2_b")]
    w_in = di("w_in", [D, DIN])
    b_in = di("b_in", [DIN])
    wg2 = di("w_gate_lr2", [16, 256])
    bg2 = di("b_gate_lr2", [256])
    sinks = di("attn_sinks", [8])
    gnorm = di("gla_norm_g", [128])
    w_out = di("w_out", [D, D])
    w_fg = di("w_ffn_gate", [D, DFF])
    w_fu = di("w_ffn_up", [D, DFF])
    w_fd = di("w_ffn_down", [DFF, D])
    cst = {n: di(n, s, d) for n, s, d in CONST_SPECS}
    out = nc.dram_tensor("out", [S, D], F32, kind="ExternalOutput").ap()
    if debug:
        dbg_h0 = nc.dram_tensor("dbg_h0", [S, D], F32, kind="ExternalOutput").ap()
        dbg_h1 = nc.dram_tensor("dbg_h1", [S, D], F32, kind="ExternalOutput").ap()
        dbg_o = nc.dram_tensor("dbg_o", [nblk, 128, 8, TB], BF16, kind="ExternalOutput").ap()
    wgu_s = nc.dram_tensor("wgu_s", [NFC, 128, 2, 8, 128], BF16).ap()
    wd_s = nc.dram_tensor("wd_s", [2, 11, 128, 2, 512], BF16).ap()

    with ExitStack() as es:
        P = Prog(nc, es)
        sb = lambda n, s, d: es.enter_context(nc.sbuf_tensor(n, s, d))
        W_IN = sb("W_IN", [128, 8, DIN], BF16)
        W_OUT = sb("W_OUT", [128, 8, D], BF16)
        WGU = [sb(f"WGU{i}", [128, 2, 8, 128], BF16) for i in range(2)]
        WD = [sb(f"WD{i}", [128, 2, 512], BF16) for i in range(3)]
        LNC = sb("LNC", [128, 6, D], F32)
        BTOK = sb("BTOK", [128, 896], F32)
        IDENT = sb("IDENT", [128, 128], F32)
        TRI1 = sb("TRI1", [128, 128], F32)
        TRI2 = sb("TRI2", [128, 128], F32)
        ONESF = sb("ONESF", [128, 128], F32)
        ONESB = sb("ONESB", [128, 128], BF16)
        ONESM = sb("ONESM", [33, 128], BF16)
        WG2 = sb("WG2", [32, 256], F32)
        MASKU = sb("MASKU", [128, 128], BF16)
        MCUR = sb("MCUR", [128, 128], BF16)
        MPREV = sb("MPREV", [128, 128], BF16)
        BCOL = sb("BCOL", [128, 32], F32)
        NEGH = sb("NEGH", [128, 1], F32)
        SINK32 = sb("SINK32", [33, 8], F32)
        NXS = 2
        XS = [sb(f"XS{i}", [128, D], F32) for i in range(NXS)]
        H = sb("H", [128, 4, D], F32)
        HT0 = sb("HT0", [128, 8, TB], BF16)
        HT1 = sb("HT1", [128, 8, TB], BF16)
        QST = sb("QST", [67, 8, TB], BF16)
        KST = sb("KST", [67, 2, 640], BF16)
        VS = sb("VS", [128, 5, 128], BF16)
        AT = sb("AT", [128, NFC, TB], BF16)
        QTT = sb("QTT", [64, 4, 128], BF16)
        KTT = sb("KTT", [64, 4, 128], BF16)
        KH = sb("KH", [128, 256], BF16)
        VG = sb("VG", [128, 512], BF16)
        GLR = sb("GLR", [32, TB], F32)
        OST = sb("OST", [128, 4, TB], BF16)
        OGT = sb("OGT", [128, 4, TB], BF16)
        KMT = sb("KMT", [67, 2, 32, 16], BF16)
        VM = sb("VM", [33, 2, 64], BF16)
        PT = sb("PT", [128, 2, TB], BF16)
        PTM = sb("PTM", [33, 2, TB], BF16)
        GB = sb("GB", [128, TB], F32)
        EX = GB[:, 0:256]
        SPL = GB[:, 256:512]
        EB2 = GB[0:64, :].rearrange("p (h c) -> p h c", h=4)
        SGM = GB
        EB1 = sb("EB1", [64, 4, 128], F32)
        EBL = sb("EBL", [64, 4], F32)
        AM = sb("AM", [128, 4, 128], BF16)
        S32 = sb("S32", [64, 4, 128], F32)
        SBF = sb("SBF", [64, 4, 128], BF16)
        SQ = sb("SQ", [128, TB], F32)
        KTMP = SQ[:, 0:256]
        ER = SQ[:, 256:512]
        OG32 = SQ
        RSTD = sb("RSTD", [128, TB], F32)
        RDEN = sb("RDEN", [128, 256], F32)
        SGF = sb("SGF", [128, TB], F32)
        NLS = 12
        LST = [sb(f"LST{i}", [128, 16], F32) for i in range(NLS)]
        PB = [es.enter_context(nc.psum_tensor(f"PB{i}", [128, 512], F32)) for i in range(8)]
        HMT = AT[:, 0, 0:128].rearrange("p (k t) -> p k t", k=8)
        XT = [AT[:, 4 * i:4 * i + 4, :].bitcast(F32).rearrange("p a b -> p (a b)") for i in range(2)]

        B = {}
        def bufs(*names):
            for n in names:
                B[n] = Buf(n)
        bufs("W_IN", "W_OUT", "LNC", "BTOK", "CONST", "WG2", "BCOL", "SINK32", "QSTx", "KSTx", "KMT", "KMTx",
             "VM", "PTMx", "GLR", "GB", "EB1", "EBL", "AM", "S32", "SBF", "SQ", "RSTD", "RDEN", "SGF", "PTM", "KSTp", "VSp",
             "PTc", "PTp", "QTT", "KTT", "KH", "VG")
        for i in range(NXS):
            bufs(f"XS{i}")
        for i in range(NLS):
            bufs(f"LST{i}")
        for i in range(2):
            bufs(f"WGU{i}")
        for i in range(4):
            bufs(f"WD{i}", f"H{i}", f"HT0_{i}", f"HT1_{i}", f"QST{i}", f"KST{i}", f"VS{i}", f"OST{i}", f"OGT{i}")
        for i in range(8):
            bufs(f"PB{i}")
        for i in range(NFC):
            bufs(f"AT{i}", f"wgu_s{i}")
        for i in range(11):
            bufs(f"wd_s0_{i}", f"wd_s1_{i}")

        def bl(*names):
            return [B[n] for n in names]

        P.dma("pool", W_IN[:], w_in.rearrange("(k p) n -> p k n", p=128), writes=bl("W_IN"))
        P.dma("sp", XS[0][0:16, :], meta, writes=bl("XS0"))
        for i in range(6):
            P.dma("sp", LNC[:, i, :], lnv[i].partition_broadcast(128), writes=bl("LNC"))
        for (n, t) in (("ident", IDENT), ("tri1", TRI1), ("tri2", TRI2), ("masku", MASKU), ("mcur", MCUR),
                       ("mprev", MPREV), ("onesm", ONESM)):
            P.dma("sp", t[:], cst[n], writes=bl("CONST"))
        P.dma("sp", QST[64:67, :, :], cst["qrows"], writes=bl("QSTx"))
        P.dma("sp", KST[64:67, :, :], cst["kbrows"], writes=bl("KSTx"))
        P.dma("sp", KMT[64:67, :, :, :], cst["kmrows"], writes=bl("KMTx"))
        P.dma("sp", BTOK[:, 0:768], b_in[1024:1792].partition_broadcast(128), writes=bl("BTOK"))
        P.dma("sp", BTOK[:, 768:896], b_in[640:768].partition_broadcast(128), writes=bl("BTOK"))
        P.op("pool", lambda e: e.memset(WG2[:], 0.0), writes=bl("WG2"))
        P.dma("sp", WG2[0:16, :], wg2, writes=bl("WG2"))
        P.dma("sp", WG2[16:17, :], bg2.rearrange("(o n) -> o n", o=1), writes=bl("WG2"))
        P.op("pool", lambda e: e.memset(BCOL[:], 0.0), writes=bl("BCOL"))
        col = lambda off, n: b_in[off:off + n].rearrange("(p o) -> p o", o=1)
        for h in range(8):
            P.dma("sp", BCOL[0:64, h:h + 1], col(64 * h, 64), writes=bl("BCOL"))
        for k in range(2):
            P.dma("sp", BCOL[0:64, 8 + k:9 + k], col(512 + 64 * k, 64), writes=bl("BCOL"))
        for hh in range(4):
            P.dma("sp", BCOL[0:64, 10 + hh:11 + hh], col(768 + 64 * hh, 64), writes=bl("BCOL"))
            P.dma("sp", BCOL[0:64, 14 + hh:15 + hh], col(1024 + 64 * hh, 64), writes=bl("BCOL"))
            P.dma("sp", BCOL[:, 18 + hh:19 + hh], col(1792 + 128 * hh, 128), writes=bl("BCOL"))
        P.dma("sp", BCOL[0:16, 22:23], col(2304, 16), writes=bl("BCOL"))
        P.dma("sp", BCOL[:, 23:24], gnorm.rearrange("(p o) -> p o", o=1), writes=bl("BCOL"))
        P.op("act", lambda e: e.mul(out=BCOL[0:64, 0:8], in_=BCOL[0:64, 0:8], mul=0.125), reads=bl("BCOL"), writes=bl("BCOL"))
        P.op("act", lambda e: e.mul(out=BCOL[:, 24:28], in_=BCOL[:, 18:22], mul=-1.0), reads=bl("BCOL"), writes=bl("BCOL"))
        P.op("act", lambda e: e.mul(out=BCOL[:, 28:29], in_=BCOL[:, 23:24], mul=0.5), reads=bl("BCOL"), writes=bl("BCOL"))
        P.op("dve", lambda e: e.memset(NEGH[:], -0.5), writes=bl("CONST"))
        P.dma("sp", SINK32[32:33, :], sinks.rearrange("(o n) -> o n", o=1), writes=bl("SINK32"))
        P.op("act", lambda e: e.activation(out=SINK32[32:33, :], in_=SINK32[32:33, :], func=AF.Exp),
             reads=bl("SINK32"), writes=bl("SINK32"))
        P.op("dve", lambda e: e.memset(ONESF[:], 1.0), writes=bl("CONST"))
        P.op("dve", lambda e: e.memset(ONESB[:], 1.0), writes=bl("CONST"))
        P.op("dve", lambda e: e.memset(GLR[:], 1.0), writes=bl("GLR"))
        P.op("dve", lambda e: e.memset(PTM[:], 0.0), writes=bl("PTMx"))
        P.op("dve", lambda e: e.memset(VM[:], 0.0), writes=bl("VM"))
        for h in range(8):
            k, g = h // 4, h % 4
            P.op("dve", (lambda k, g, h: lambda e: e.tensor_scalar(
                out=PTM[32:33, k, g * 128:(g + 1) * 128], in0=ONESF[32:33, :], scalar1=SINK32[32:33, h:h + 1],
                scalar2=None, op0=ALU.mult))(k, g, h), reads=bl("SINK32", "CONST"), writes=bl("PTMx"))
        P.dma("pool", W_OUT[:], w_out.rearrange("(k p) n -> p k n", p=128), writes=bl("W_OUT"))
        def casts_more():
            return iter(())

        def casts():
            for fc in range(NFC):
                for m, w in enumerate((w_fg, w_fu)):
                    P.dma("pool", wgu_s[fc, :, m, :, :], w[:, fc * 128:(fc + 1) * 128].rearrange("(k p) n -> p k n", p=128),
                          writes=bl(f"wgu_s{fc}"))
                yield 1
            for hf in range(2):
                for s in range(11):
                    P.dma("pool", wd_s[hf, s], w_fd[s * 256:(s + 1) * 256, hf * 512:(hf + 1) * 512]
                          .rearrange("(f p) n -> p f n", p=128), writes=bl(f"wd_s{hf}_{s}"))
                    yield 1

        def ckpt(k):
            if stage == k:
                P.finish()
                P.emit()
                raise _Stop(nc)

        ckpt(0)
        ctr = {"ls": 0, "xs": 0}

        def layer_norm_stats(src, np_, rb):
            i = ctr["ls"] % NLS
            ctr["ls"] += 1
            L, lb = LST[i], bl(f"LST{i}")
            P.op("dve", lambda e: e.bn_stats(out=L[0:np_, 0:6], in_=src[:, 0:512]), reads=rb, writes=lb)
            P.op("dve", lambda e: e.bn_stats(out=L[0:np_, 6:12], in_=src[:, 512:1024]), reads=rb, writes=lb)
            P.op("dve", lambda e: e.bn_aggr(out=L[0:np_, 12:14], in_=L[0:np_, 0:12]), reads=lb, writes=lb)
            P.op("dve", lambda e: e.tensor_scalar(out=L[0:np_, 15:16], in0=L[0:np_, 12:13], scalar1=-1.0, scalar2=None, op0=ALU.mult),
                 reads=lb, writes=lb)
            P.op("act", lambda e: e.activation(out=L[0:np_, 14:15], in_=L[0:np_, 13:14], func=AF.Ln, bias=LN_EPS, scale=1.0), reads=lb, writes=lb)
            P.op("act", lambda e: e.activation(out=L[0:np_, 14:15], in_=L[0:np_, 14:15], func=AF.Exp, scale=-0.5), reads=lb, writes=lb)
            P.op("act", lambda e: e.activation(out=L[0:np_, 15:16], in_=L[0:np_, 15:16], func=AF.Identity, scale=L[0:np_, 14:15]), reads=lb, writes=lb)
            return L, lb

        def layer_norm_apply(src, dst, np_, gi, rb, wb, L, lb):
            P.op("act", lambda e: e.activation(out=dst, in_=src, func=AF.Identity, bias=L[0:np_, 15:16], scale=L[0:np_, 14:15]),
                 reads=rb + lb, writes=wb)
            P.op("dve", lambda e: e.tensor_tensor(out=dst, in0=dst, in1=LNC[0:np_, gi, :], op=ALU.mult),
                 reads=wb + bl("LNC"), writes=wb)
            P.op("pool", lambda e: e.tensor_tensor(out=dst, in0=dst, in1=LNC[0:np_, gi + 1, :], op=ALU.add),
                 reads=wb + bl("LNC"), writes=wb)

        def next_xs():
            i = ctr["xs"] % NXS
            ctr["xs"] += 1
            return XS[i], bl(f"XS{i}")

        def ln_in_tile(T):
            xs, xb = XS[T % 2], bl(f"XS{T % 2}")
            P.dma("sp", xs[:], x[T * 128:(T + 1) * 128, :], writes=xb)
            L, lb = layer_norm_stats(xs[:, :], 128, xb)
            layer_norm_apply(xs[:, :], xs[:, :], 128, 0, xb, xb, L, lb)
            return xs, xb

        def transpose_tile(src, srcb, HTd, dstb, j, banks):
            for half, eng in ((0, "act"), (1, "dve")):
                pb = banks[half]
                def tr(e, half=half, pb=pb):
                    for k in range(4):
                        kk = half * 4 + k
                        i = e.transpose(out=PB[pb][:, k * 128:(k + 1) * 128], in_=src[:, kk * 128:(kk + 1) * 128],
                                        identity=IDENT[:])
                    return i
                P.op("pe", tr, reads=srcb + bl("CONST"), writes=bl(f"PB{pb}"))
                dst = HTd[:, half * 4:(half + 1) * 4, j * 128:(j + 1) * 128]
                psv = PB[pb][:].rearrange("p (k t) -> p k t", k=4)
                if eng == "act":
                    P.op("act", lambda e, dst=dst, psv=psv: e.activation(out=dst, in_=psv, func=AF.Copy),
                         reads=bl(f"PB{pb}"), writes=dstb)
                else:
                    P.op("dve", lambda e, dst=dst, psv=psv: e.tensor_copy(out=dst, in_=psv),
                         reads=bl(f"PB{pb}"), writes=dstb)

        def mm_group(e, out_ap, pairs):
            n = len(pairs)
            for idx, (l, r) in enumerate(pairs):
                i = e.matmul(out_ap, lhsT=l, rhs=r, start=(idx == 0), stop=(idx == n - 1))
            return i

        HT0all = [f"HT0_{j}" for j in range(4)]
        HT1all = [f"HT1_{j}" for j in range(4)]

        def gate_pipeline(ntok, tcol, pbi):
            np_ = ntok
            pbn = f"PB{pbi}"
            P.op("pe", lambda e: e.matmul(PB[pbi][0:np_, 0:256], lhsT=GLR[0:32, tcol:tcol + ntok], rhs=WG2[:, :],
                                          start=True, stop=True), reads=bl("GLR", "WG2"), writes=bl(pbn))
            P.op("act", lambda e: e.activation(out=EX[0:np_, :], in_=PB[pbi][0:np_, 0:256], func=AF.Exp, scale=-1.0),
                 reads=bl(pbn), writes=bl("GB"))
            P.op("act", lambda e: e.activation(out=SPL[0:np_, :], in_=EX[0:np_, :], func=AF.Ln, bias=1.0),
                 reads=bl("GB"), writes=bl("GB"))
            P.op("pe", lambda e: e.matmul(PB[pbi][0:np_, 256:512], lhsT=TRI2[0:np_, 0:np_], rhs=SPL[0:np_, :],
                                          start=True, stop=True), reads=bl("GB", "CONST"), writes=bl(pbn))
            P.op("act", lambda e: e.activation(out=ER[0:np_, :], in_=PB[pbi][0:np_, 256:512], func=AF.Exp),
                 reads=bl(pbn), writes=bl("SQ"))

        L, lb = layer_norm_stats(XS[0][0:16, :], 16, bl("XS0"))
        layer_norm_apply(XS[0][0:16, :], XS[0][0:16, :], 16, 0, bl("XS0"), bl("XS0"), L, lb)
        def trm(e):
            for kk in range(8):
                i = e.transpose(out=PB[0][:, kk * 16:(kk + 1) * 16], in_=XS[0][0:16, kk * 128:(kk + 1) * 128],
                                identity=IDENT[0:16, 0:16])
            return i
        P.op("pe", trm, reads=bl("XS0", "CONST"), writes=bl("PB0"))
        P.op("act", lambda e: e.activation(out=HMT, in_=PB[0][:, 0:128].rearrange("p (k t) -> p k t", k=8), func=AF.Copy),
             reads=bl("PB0"), writes=bl("AT0"))
        P.op("pe", lambda e: mm_group(e, PB[2][0:16, 0:512], [(HMT[:, kk, :], W_IN[:, kk, 1024:1536]) for kk in range(8)]),
             reads=bl("AT0", "W_IN"), writes=bl("PB2"))
        def mmeta3(e):
            mm_group(e, PB[3][0:16, 0:256], [(HMT[:, kk, :], W_IN[:, kk, 1536:1792]) for kk in range(8)])
            return mm_group(e, PB[3][0:16, 256:384], [(HMT[:, kk, :], W_IN[:, kk, 640:768]) for kk in range(8)])
        P.op("pe", mmeta3, reads=bl("AT0", "W_IN"), writes=bl("PB3"))
        def mmeta4(e):
            for k in range(2):
                mm_group(e, PB[1][0:64, k * 16:(k + 1) * 16],
                         [(W_IN[:, kk, 512 + 64 * k:576 + 64 * k], HMT[:, kk, :]) for kk in range(8)])
            return mm_group(e, PB[1][0:16, 64:80], [(W_IN[:, kk, 2304:2320], HMT[:, kk, :]) for kk in range(8)])
        P.op("pe", mmeta4, reads=bl("AT0", "W_IN"), writes=bl("PB1"))
        for k in range(2):
            P.op("act", (lambda k: lambda e: e.activation(
                out=KMT[0:64, k, :, :], in_=PB[1][0:64, k * 16:(k + 1) * 16].unsqueeze(1).broadcast_to([64, 32, 16]),
                func=AF.Identity, bias=BCOL[0:64, 8 + k:9 + k], scale=1.0))(k), reads=bl("PB1", "BCOL"), writes=bl("KMT"))
        P.op("act", lambda e: e.activation(out=GLR[0:16, 0:16], in_=PB[1][0:16, 64:80], func=AF.Identity,
                                           bias=BCOL[0:16, 22:23], scale=1.0), reads=bl("PB1", "BCOL"), writes=bl("GLR"))
        P.op("dve", lambda e: e.tensor_tensor(out=VM[0:16, :, :], in0=PB[3][0:16, 256:384].rearrange("p (k d) -> p k d", k=2),
                                              in1=BTOK[0:16, 768:896].rearrange("p (k d) -> p k d", k=2), op=ALU.add),
             reads=bl("PB3", "BTOK"), writes=bl("VM"))
        P.op("dve", lambda e: e.tensor_tensor(out=VG[0:16, 0:256], in0=PB[2][0:16, 256:512], in1=BTOK[0:16, 256:512], op=ALU.add),
             reads=bl("PB2", "BTOK"), writes=bl("VG"))
        P.op("dve", lambda e: e.tensor_tensor(out=VG[0:16, 256:512], in0=PB[3][0:16, 0:256], in1=BTOK[0:16, 512:768], op=ALU.add),
             reads=bl("PB3", "BTOK"), writes=bl("VG"))
        P.op("dve", lambda e: e.tensor_tensor(out=KTMP[0:16, :], in0=PB[2][0:16, 0:256], in1=BTOK[0:16, 0:256], op=ALU.add),
             reads=bl("PB2", "BTOK"), writes=bl("SQ"))
        gate_pipeline(16, 0, 0)
        P.op("dve", lambda e: e.tensor_tensor(out=KH[0:16, :], in0=KTMP[0:16, :], in1=ER[0:16, :], op=ALU.mult),
             reads=bl("SQ"), writes=bl("KH"))
        def mstate0(e):
            for hh in range(4):
                i = e.matmul(PB[2][0:64, hh * 128:(hh + 1) * 128], lhsT=KH[0:16, hh * 64:(hh + 1) * 64],
                             rhs=VG[0:16, hh * 128:(hh + 1) * 128], start=True, stop=True)
            return i
        P.op("pe", mstate0, reads=bl("KH", "VG"), writes=bl("PB2"))
        P.op("dve", lambda e: e.tensor_copy(out=S32[:], in_=PB[2][0:64, :].rearrange("p (h c) -> p h c", h=4)),
             reads=bl("PB2"), writes=bl("S32"))
        P.op("act", lambda e: e.activation(out=SBF[:], in_=S32[:], func=AF.Copy), reads=bl("S32"), writes=bl("SBF"))
        ckpt(1)

        def merge(g1, g2):
            gens = [g1, g2]
            live = [True, True]
            i = 0
            while live[0] or live[1]:
                if live[i]:
                    try:
                        yield next(gens[i])
                    except StopIteration:
                        live[i] = False
                i ^= 1

        def mixer_head(b):
            def tick():
                return 1
            if b > 0:
                P.op("act", lambda e: e.activation(out=KST[0:64, :, 0:128], in_=KST[0:64, :, 512:640], func=AF.Copy),
                     reads=bl("KST3"), writes=bl("KSTp"))
                P.op("dve", lambda e: e.tensor_copy(out=VS[:, 0, :], in_=VS[:, 4, :]), reads=bl("VS3"), writes=bl("VSp"))
            for j in range(4):
                T = 4 * b + j
                xs, xb = ln_in_tile(T)
                if debug:
                    P.dma("pool", dbg_h0[T * 128:(T + 1) * 128, :], xs[:, :], reads=xb)
                yield 6.0
                transpose_tile(xs, xb, HT0, bl(f"HT0_{j}"), j, (0, 1))
                yield 1.5
            yield from m_qk(b)
            yield from merge(m_gla(b), m_swa(b))
            if debug:
                P.dma("pool", dbg_o[b, :, 0:4, :], OST[:], reads=bl(*[f"OST{j}" for j in range(4)]))
                P.dma("pool", dbg_o[b, :, 4:8, :], OGT[:], reads=bl(*[f"OGT{j}" for j in range(4)]))

        def m_gla(b):
            def tick():
                return 1
            for j in range(4):
                tc = slice(j * 128, (j + 1) * 128)
                hb = f"HT0_{j}"
                P.op("pe", lambda e, tc=tc: mm_group(e, PB[2][:, 0:512], [(HT0[:, kk, tc], W_IN[:, kk, 1024:1536]) for kk in range(8)]),
                     reads=bl(hb, "W_IN"), writes=bl("PB2"))
                P.op("pe", lambda e, tc=tc: mm_group(e, PB[3][:, 0:256], [(HT0[:, kk, tc], W_IN[:, kk, 1536:1792]) for kk in range(8)]),
                     reads=bl(hb, "W_IN"), writes=bl("PB3"))
                P.op("dve", lambda e: e.tensor_tensor(out=VG[:, 0:256], in0=PB[2][:, 256:512], in1=BTOK[:, 256:512], op=ALU.add),
                     reads=bl("PB2", "BTOK"), writes=bl("VG"))
                P.op("dve", lambda e: e.tensor_tensor(out=VG[:, 256:512], in0=PB[3][:, 0:256], in1=BTOK[:, 512:768], op=ALU.add),
                     reads=bl("PB3", "BTOK"), writes=bl("VG"))
                P.op("dve", lambda e: e.tensor_tensor(out=KTMP[:, :], in0=PB[2][:, 0:256], in1=BTOK[:, 0:256], op=ALU.add),
                     reads=bl("PB2", "BTOK"), writes=bl("SQ"))
                yield 1.0
                if j == 0:
                    P.op("pe", lambda e: mm_group(e, PB[0][0:16, :], [(W_IN[:, kk, 2304:2320], HT0[:, kk, :]) for kk in range(8)]),
                         reads=bl(*HT0all, "W_IN"), writes=bl("PB0"))
                    P.op("act", lambda e: e.activation(out=GLR[0:16, :], in_=PB[0][0:16, :], func=AF.Identity,
                                                       bias=BCOL[0:16, 22:23], scale=1.0), reads=bl("PB0", "BCOL"), writes=bl("GLR"))
                gate_pipeline(128, j * 128, 0)
                P.op("dve", lambda e: e.tensor_tensor(out=KH[:, :], in0=KTMP[:, :], in1=ER[:, :], op=ALU.mult),
                     reads=bl("SQ"), writes=bl("KH"))
                yield 3.0
                def mbt(e):
                    for hh in range(4):
                        i = e.matmul(PB[1][0:64, hh * 128:(hh + 1) * 128], lhsT=SPL[:, hh * 64:(hh + 1) * 64], rhs=TRI1[:, :],
                                     start=True, stop=True)
                    return i
                P.op("pe", mbt, reads=bl("GB", "CONST"), writes=bl("PB1"))
                pbv = PB[1][0:64, :].rearrange("p (h c) -> p h c", h=4)
                P.op("act", lambda e, pbv=pbv: e.activation(out=EB1[:], in_=pbv, func=AF.Exp, bias=LN8, scale=1.0),
                     reads=bl("PB1"), writes=bl("EB1"))
                P.op("act", lambda e, pbv=pbv: e.activation(out=EBL[:, :], in_=pbv[:, :, 127], func=AF.Exp),
                     reads=bl("PB1"), writes=bl("EBL"))
                P.op("act", lambda e, pbv=pbv: e.activation(out=EB2, in_=pbv, func=AF.Exp, scale=-1.0),
                     reads=bl("PB1"), writes=bl("GB"))
                yield 1.5
                def mqg(e, tc=tc):
                    for hh in range(4):
                        i = mm_group(e, PB[2][0:64, hh * 128:(hh + 1) * 128],
                                     [(W_IN[:, kk, 768 + 64 * hh:832 + 64 * hh], HT0[:, kk, tc]) for kk in range(8)])
                    return i
                P.op("pe", mqg, reads=bl(hb, "W_IN"), writes=bl("PB2"))
                def mkg(e, tc=tc):
                    for hh in range(4):
                        i = mm_group(e, PB[3][0:64, hh * 128:(hh + 1) * 128],
                                     [(W_IN[:, kk, 1024 + 64 * hh:1088 + 64 * hh], HT0[:, kk, tc]) for kk in range(8)])
                    return i
                P.op("pe", mkg, reads=bl(hb, "W_IN"), writes=bl("PB3"))
                for hh in range(4):
                    P.op("dve", lambda e, hh=hh: e.scalar_tensor_tensor(
                        out=QTT[:, hh, :], in0=PB[2][0:64, hh * 128:(hh + 1) * 128], scalar=BCOL[0:64, 10 + hh:11 + hh],
                        in1=EB1[:, hh, :], op0=ALU.add, op1=ALU.mult), reads=bl("PB2", "BCOL", "EB1"), writes=bl("QTT"))
                    P.op("dve", lambda e, hh=hh: e.scalar_tensor_tensor(
                        out=KTT[:, hh, :], in0=PB[3][0:64, hh * 128:(hh + 1) * 128], scalar=BCOL[0:64, 14 + hh:15 + hh],
                        in1=EB2[:, hh, :], op0=ALU.add, op1=ALU.mult), reads=bl("PB3", "BCOL", "GB"), writes=bl("KTT"))
                yield 2.5
                def ma(e):
                    for hh in range(4):
                        i = e.matmul(PB[0][:, hh * 128:(hh + 1) * 128], lhsT=KTT[:, hh, :], rhs=QTT[:, hh, :], start=True, stop=True)
                    return i
                P.op("pe", ma, reads=bl("KTT", "QTT"), writes=bl("PB0"))
                P.op("dve", lambda e: e.tensor_tensor(out=AM[:], in0=PB[0][:].rearrange("p (h c) -> p h c", h=4),
                                                      in1=MASKU[:].unsqueeze(1).broadcast_to([128, 4, 128]), op=ALU.mult),
                     reads=bl("PB0", "CONST"), writes=bl("AM"))
                yield 0.7
                def mo(e):
                    for hh in range(4):
                        e.matmul(PB[1][:, hh * 128:(hh + 1) * 128], lhsT=VG[:, hh * 128:(hh + 1) * 128], rhs=AM[:, hh, :],
                                 start=True, stop=False)
                        i = e.matmul(PB[1][:, hh * 128:(hh + 1) * 128], lhsT=SBF[:, hh, :], rhs=QTT[:, hh, :], start=False, stop=True)
                    return i
                P.op("pe", mo, reads=bl("VG", "AM", "SBF", "QTT"), writes=bl("PB1"))
                def mst(e):
                    for hh in range(4):
                        i = e.matmul(PB[2][0:64, hh * 128:(hh + 1) * 128], lhsT=KH[:, hh * 64:(hh + 1) * 64],
                                     rhs=VG[:, hh * 128:(hh + 1) * 128], start=True, stop=True)
                    return i
                P.op("pe", mst, reads=bl("KH", "VG"), writes=bl("PB2"))
                P.op("dve", lambda e: e.tensor_tensor(out=S32[:], in0=S32[:], in1=EBL[:, :].unsqueeze(2).broadcast_to([64, 4, 128]), op=ALU.mult),
                     reads=bl("S32", "EBL"), writes=bl("S32"))
                P.op("dve", lambda e: e.tensor_tensor(out=S32[:], in0=S32[:], in1=PB[2][0:64, :].rearrange("p (h c) -> p h c", h=4), op=ALU.add),
                     reads=bl("S32", "PB2"), writes=bl("S32"))
                P.op("act", lambda e: e.activation(out=SBF[:], in_=S32[:], func=AF.Copy), reads=bl("S32"), writes=bl("SBF"))
                P.op("act", lambda e: e.activation(out=SQ[:], in_=PB[1][:], func=AF.Square), reads=bl("PB1"), writes=bl("SQ"))
                P.op("pe", lambda e: e.matmul(PB[0][:, :], lhsT=ONESF[:, :], rhs=SQ[:, :], start=True, stop=True),
                     reads=bl("SQ", "CONST"), writes=bl("PB0"))
                P.op("act", lambda e: e.activation(out=RSTD[:], in_=PB[0][:], func=AF.Ln, bias=RMS_EPS, scale=1.0 / 128.0),
                     reads=bl("PB0"), writes=bl("RSTD"))
                P.op("act", lambda e: e.activation(out=RSTD[:], in_=RSTD[:], func=AF.Exp, scale=-0.5), reads=bl("RSTD"), writes=bl("RSTD"))
                P.op("dve", lambda e: e.scalar_tensor_tensor(out=OG32[:], in0=PB[1][:], scalar=BCOL[:, 23:24], in1=RSTD[:],
                                                             op0=ALU.mult, op1=ALU.mult), reads=bl("PB1", "BCOL", "RSTD"), writes=bl("SQ"))
                yield 3.0
                def mrg(e, tc=tc):
                    for hh in range(4):
                        i = mm_group(e, PB[3][:, hh * 128:(hh + 1) * 128],
                                     [(W_IN[:, kk, 1792 + 128 * hh:1920 + 128 * hh], HT0[:, kk, tc]) for kk in range(8)])
                    return i
                P.op("pe", mrg, reads=bl(hb, "W_IN"), writes=bl("PB3"))
                for hh in range(4):
                    P.op("act", lambda e, hh=hh: e.activation(out=SGM[:, hh * 128:(hh + 1) * 128], in_=PB[3][:, hh * 128:(hh + 1) * 128],
                                                              func=AF.Exp, bias=BCOL[:, 24 + hh:25 + hh], scale=-1.0),
                         reads=bl("PB3", "BCOL"), writes=bl("GB"))
                P.op("act", lambda e: e.activation(out=SGM[:], in_=SGM[:], func=AF.Ln, bias=1.0), reads=bl("GB"), writes=bl("GB"))
                P.op("act", lambda e: e.activation(out=SGM[:], in_=SGM[:], func=AF.Exp, scale=-1.0), reads=bl("GB"), writes=bl("GB"))
                P.op("dve", lambda e: e.tensor_tensor(out=SGM[:], in0=SGM[:], in1=OG32[:], op=ALU.mult),
                     reads=bl("SQ", "GB"), writes=bl("GB"))
                for hh in range(4):
                    P.op("dve", lambda e, hh=hh, tc=tc: e.scalar_tensor_tensor(
                        out=OGT[:, hh, tc], in0=PB[3][:, hh * 128:(hh + 1) * 128], scalar=BCOL[:, 18 + hh:19 + hh],
                        in1=SGM[:, hh * 128:(hh + 1) * 128], op0=ALU.add, op1=ALU.mult),
                        reads=bl("PB3", "BCOL", "GB"), writes=bl(f"OGT{j}"))
                yield 1.5
        def m_qk(b):
            def mvs(e):
                for j in range(4):
                    i = mm_group(e, PB[2][:, j * 128:(j + 1) * 128],
                                 [(HT0[:, kk, j * 128:(j + 1) * 128], W_IN[:, kk, 640:768]) for kk in range(8)])
                return i
            P.op("pe", mvs, reads=bl(*HT0all, "W_IN"), writes=bl("PB2"))
            P.op("dve", lambda e: e.tensor_tensor(out=VS[:, 1:5, :], in0=PB[2][:, :].rearrange("p (j c) -> p j c", j=4),
                                                  in1=BTOK[:, 768:896].unsqueeze(1).broadcast_to([128, 4, 128]), op=ALU.add),
                 reads=bl("PB2", "BTOK"), writes=bl("VS0", "VS1", "VS2", "VS3"))
            yield 0.3
            for h in range(8):
                pb = h % 2
                P.op("pe", lambda e, h=h, pb=pb: mm_group(e, PB[pb][0:64, :], [(W_IN[:, kk, 64 * h:64 * h + 64], HT0[:, kk, :]) for kk in range(8)]),
                     reads=bl(*HT0all, "W_IN"), writes=bl(f"PB{pb}"))
                P.op("act", lambda e, h=h, pb=pb: e.activation(out=QST[0:64, h, :], in_=PB[pb][0:64, :], func=AF.Identity,
                                                               bias=BCOL[0:64, h:h + 1], scale=0.125),
                     reads=bl(f"PB{pb}", "BCOL"), writes=bl(*[f"QST{j}" for j in range(4)]))
                yield 0.3
            for k in range(2):
                pb = 2 + k
                P.op("pe", lambda e, k=k, pb=pb: mm_group(e, PB[pb][0:64, :], [(W_IN[:, kk, 512 + 64 * k:576 + 64 * k], HT0[:, kk, :]) for kk in range(8)]),
                     reads=bl(*HT0all, "W_IN"), writes=bl(f"PB{pb}"))
                P.op("act", lambda e, k=k, pb=pb: e.activation(out=KST[0:64, k, 128:640], in_=PB[pb][0:64, :], func=AF.Identity,
                                                               bias=BCOL[0:64, 8 + k:9 + k], scale=1.0),
                     reads=bl(f"PB{pb}", "BCOL"), writes=bl(*[f"KST{j}" for j in range(4)]))
                yield 0.3

        def m_swa(b):
            def tick():
                return 1
            for j in range(4):
                T = 4 * b + j
                tc = slice(j * 128, (j + 1) * 128)
                for k in range(2):
                    qv = QST[:, 4 * k:4 * k + 4, tc]
                    kprev_b = f"KST{j - 1}" if j > 0 else "KSTp"
                    vprev_b = f"VS{j - 1}" if j > 0 else "VSp"
                    P.op("pe", lambda e, k=k, j=j, qv=qv: e.matmul(PB[0][:, :], lhsT=KST[0:66, k, 128 + 128 * j:256 + 128 * j], rhs=qv[0:66],
                                                                   start=True, stop=True),
                         reads=bl(f"KST{j}", "KSTx", f"QST{j}", "QSTx"), writes=bl("PB0"))
                    if T > 0:
                        P.op("pe", lambda e, k=k, j=j, qv=qv: e.matmul(PB[1][:, :], lhsT=KST[0:67, k, 128 * j:128 + 128 * j], rhs=qv[0:67],
                                                                       start=True, stop=True),
                             reads=bl(kprev_b, "KSTx", f"QST{j}", "QSTx"), writes=bl("PB1"))
                    P.op("pe", lambda e, k=k, T=T, qv=qv: e.matmul(PB[2][0:16, :], lhsT=KMT[0:67, k, T, :], rhs=qv[0:67], start=True, stop=True),
                         reads=bl("KMT", "KMTx", f"QST{j}", "QSTx"), writes=bl("PB2"))
                    P.op("act", lambda e: e.activation(out=PT[:, 0, :], in_=PB[0][:, :], func=AF.Exp), reads=bl("PB0"), writes=bl("PTc"))
                    P.op("pool", lambda e: e.tensor_tensor(out=PT[:, 0, :].rearrange("p (g q) -> p g q", g=4),
                                                           in0=PT[:, 0, :].rearrange("p (g q) -> p g q", g=4),
                                                           in1=MCUR[:].unsqueeze(1).broadcast_to([128, 4, 128]), op=ALU.mult),
                         reads=bl("PTc", "CONST"), writes=bl("PTc"))
                    if T > 0:
                        P.op("act", lambda e: e.activation(out=PT[:, 1, :], in_=PB[1][:, :], func=AF.Exp), reads=bl("PB1"), writes=bl("PTp"))
                        P.op("pool", lambda e: e.tensor_tensor(out=PT[:, 1, :].rearrange("p (g q) -> p g q", g=4),
                                                               in0=PT[:, 1, :].rearrange("p (g q) -> p g q", g=4),
                                                               in1=MPREV[:].unsqueeze(1).broadcast_to([128, 4, 128]), op=ALU.mult),
                             reads=bl("PTp", "CONST"), writes=bl("PTp"))
                    P.op("act", lambda e, k=k: e.activation(out=PTM[0:16, k, :], in_=PB[2][0:16, :], func=AF.Exp), reads=bl("PB2"), writes=bl("PTM"))
                    yield 3.0
                    ptc = PT[:, 0, :].rearrange("p (a b q) -> p a b q", a=2, b=2)
                    ptp = PT[:, 1, :].rearrange("p (a b q) -> p a b q", a=2, b=2)
                    def mpv(e, k=k, j=j, T=T, ptc=ptc, ptp=ptp):
                        for hf in range(2):
                            o = PB[3][64 * hf:64 * hf + 64, 0:256]
                            e.matmul(o, lhsT=VS[:, j + 1, k * 64:(k + 1) * 64], rhs=ptc[:, :, hf, :], start=True, stop=False)
                            if T > 0:
                                e.matmul(o, lhsT=VS[:, j, k * 64:(k + 1) * 64], rhs=ptp[:, :, hf, :], start=False, stop=False)
                            i = e.matmul(o, lhsT=VM[0:33, k, :], rhs=PTM[0:33, k, :].rearrange("p (a b q) -> p a b q", a=2, b=2)[:, :, hf, :],
                                         start=False, stop=True)
                        for hf in range(2):
                            o = PB[3][64 * hf:64 * hf + 64, 256:512]
                            e.matmul(o, lhsT=ONESB[:, 0:64], rhs=ptc[:, :, hf, :], start=True, stop=False)
                            if T > 0:
                                e.matmul(o, lhsT=ONESB[:, 0:64], rhs=ptp[:, :, hf, :], start=False, stop=False)
                            i = e.matmul(o, lhsT=ONESM[0:33, 0:64], rhs=PTM[0:33, k, :].rearrange("p (a b q) -> p a b q", a=2, b=2)[:, :, hf, :],
                                         start=False, stop=True)
                        return i
                    P.op("pe", mpv, reads=bl(f"VS{j}", vprev_b, "PTc", "PTp", "PTM", "PTMx", "VM", "CONST"), writes=bl("PB3"))
                    P.op("act", lambda e: e.activation(out=RDEN[:, 0:256], in_=PB[3][:, 256:512], func=AF.Ln), reads=bl("PB3"), writes=bl("RDEN"))
                    P.op("act", lambda e: e.activation(out=RDEN[:, 0:256], in_=RDEN[:, 0:256], func=AF.Exp, scale=-1.0), reads=bl("RDEN"), writes=bl("RDEN"))
                    P.op("dve", lambda e, k=k, tc=tc: e.tensor_tensor(
                        out=OST[:, 2 * k:2 * k + 2, tc], in0=PB[3][:, 0:256].rearrange("p (a q) -> p a q", a=2),
                        in1=RDEN[:, 0:256].rearrange("p (a q) -> p a q", a=2), op=ALU.mult),
                        reads=bl("PB3", "RDEN"), writes=bl(f"OST{j}"))
                    yield 2.0

        def ffn(b):
            nst = 44.0
            for fc in range(NFC):
                wb = (b * NFC + fc) % 2
                P.dma("sp", WGU[wb][:], wgu_s[fc], reads=bl(f"wgu_s{fc}"), writes=bl(f"WGU{wb}"))
                pg, pu = (4, 5) if fc % 2 == 0 else (6, 7)
                P.op("pe", lambda e, wb=wb, pg=pg: mm_group(e, PB[pg][:, :], [(WGU[wb][:, 0, kk, :], HT1[:, kk, :]) for kk in range(8)]),
                     reads=bl(f"WGU{wb}", *HT1all), writes=bl(f"PB{pg}"))
                P.op("pe", lambda e, wb=wb, pu=pu: mm_group(e, PB[pu][:, :], [(WGU[wb][:, 1, kk, :], HT1[:, kk, :]) for kk in range(8)]),
                     reads=bl(f"WGU{wb}", *HT1all), writes=bl(f"PB{pu}"))
                P.op("act", lambda e, pg=pg: e.activation(out=SGF[:], in_=PB[pg][:, :], func=AF.Exp, scale=-1.0), reads=bl(f"PB{pg}"), writes=bl("SGF"))
                P.op("act", lambda e: e.activation(out=SGF[:], in_=SGF[:], func=AF.Ln, bias=1.0), reads=bl("SGF"), writes=bl("SGF"))
                P.op("act", lambda e: e.activation(out=SGF[:], in_=SGF[:], func=AF.Exp, scale=-1.0), reads=bl("SGF"), writes=bl("SGF"))
                P.op("dve", lambda e, pg=pg: e.tensor_tensor(out=SGF[:], in0=SGF[:], in1=PB[pg][:, :], op=ALU.mult),
                     reads=bl("SGF", f"PB{pg}"), writes=bl("SGF"))
                P.op("dve", lambda e, fc=fc, pu=pu: e.tensor_tensor(out=AT[:, fc, :], in0=SGF[:], in1=PB[pu][:, :], op=ALU.mult),
                     reads=bl("SGF", f"PB{pu}"), writes=bl(f"AT{fc}"))
                yield (fc + 1) / nst
            for hf in range(2):
                cs = slice(hf * 512, (hf + 1) * 512)
                for s in range(11):
                    wb = ((b * 2 + hf) * 11 + s) % 3
                    P.dma("sp", WD[wb][:], wd_s[hf, s], reads=bl(f"wd_s{hf}_{s}"), writes=bl(f"WD{wb}"))
                    def mdn(e, s=s, wb=wb):
                        for sub in range(2):
                            fc = 2 * s + sub
                            for j in range(4):
                                i = e.matmul(PB[4 + j][:, :], lhsT=AT[:, fc, j * 128:(j + 1) * 128], rhs=WD[wb][:, sub, :],
                                             start=(fc == 0), stop=(fc == NFC - 1))
                        return i
                    P.op("pe", mdn, reads=bl(f"WD{wb}", f"AT{2 * s}", f"AT{2 * s + 1}"), writes=bl("PB4", "PB5", "PB6", "PB7"))
                    if s < 10:
                        yield (22 + hf * 11 + s + 1) / nst
                for j in range(4):
                    P.op("dve", lambda e, j=j, cs=cs: e.scalar_tensor_tensor(out=H[:, j, cs], in0=H[:, j, cs], scalar=ALPHA, in1=PB[4 + j][:, :],
                                                                             op0=ALU.mult, op1=ALU.add),
                         reads=bl(f"H{j}", f"PB{4 + j}"), writes=bl(f"H{j}"))
                yield (22 + hf * 11 + 11) / nst

        def ln_a(src, rb):
            i = ctr["ls"] % NLS
            ctr["ls"] += 1
            L, lb = LST[i], bl(f"LST{i}")
            P.op("dve", lambda e: e.bn_stats(out=L[:, 0:6], in_=src[:, 0:512]), reads=rb, writes=lb)
            P.op("dve", lambda e: e.bn_stats(out=L[:, 6:12], in_=src[:, 512:1024]), reads=rb, writes=lb)
            P.op("dve", lambda e: e.bn_aggr(out=L[:, 12:14], in_=L[:, 0:12]), reads=lb, writes=lb)
            P.op("dve", lambda e: e.tensor_scalar(out=L[:, 15:16], in0=L[:, 12:13], scalar1=-1.0, scalar2=None, op0=ALU.mult),
                 reads=lb, writes=lb)
            return L, lb

        def ln_b(L, lb):
            P.op("act", lambda e: e.activation(out=L[:, 14:15], in_=L[:, 13:14], func=AF.Ln, bias=LN_EPS, scale=1.0), reads=lb, writes=lb)
            P.op("act", lambda e: e.activation(out=L[:, 14:15], in_=L[:, 14:15], func=AF.Exp, scale=-0.5), reads=lb, writes=lb)
            P.op("act", lambda e: e.activation(out=L[:, 15:16], in_=L[:, 15:16], func=AF.Identity, scale=L[:, 14:15]), reads=lb, writes=lb)

        def ln_c(L, lb):
            pass

        def ln_n(t, tb, L, lb):
            P.op("act", lambda e: e.activation(out=t, in_=t, func=AF.Identity, bias=L[:, 15:16], scale=L[:, 14:15]),
                 reads=tb + lb, writes=tb)

        def ln_g(t, tb, gi):
            P.op("dve", lambda e: e.tensor_tensor(out=t, in0=t, in1=LNC[:, gi, :], op=ALU.mult), reads=tb + bl("LNC"), writes=tb)

        def ln_bias(t, tb, gi):
            P.op("pool", lambda e: e.tensor_tensor(out=t, in0=t, in1=LNC[:, gi + 1, :], op=ALU.add), reads=tb + bl("LNC"), writes=tb)

        XTB = [bl("AT0", "AT1", "AT2", "AT3"), bl("AT4", "AT5", "AT6", "AT7")]

        def post_f(b):
            items = []
            if b >= 0:
                for j in range(4):
                    items.append((H[:, j, :], bl(f"H{j}"), 4, j))
            if b + 1 < nblk:
                for j in range(2):
                    T = 4 * (b + 1) + j
                    P.dma("sp", XT[j], x[T * 128:(T + 1) * 128, :], writes=XTB[j])
                    items.append((XT[j], XTB[j], 0, None))
            yield 1
            stats = []
            for (t, tb, gi, j) in items:
                stats.append(ln_a(t, tb))
                yield 1
            for (L, lb) in stats:
                ln_b(L, lb)
            yield 1
            for (L, lb) in stats:
                ln_c(L, lb)
            yield 1
            for (t, tb, gi, j), (L, lb) in zip(items, stats):
                ln_n(t, tb, L, lb)
                yield 1
            for (t, tb, gi, j) in items:
                ln_g(t, tb, gi)
                yield 1
            for (t, tb, gi, j) in items:
                ln_bias(t, tb, gi)
                if j is not None:
                    T = 4 * b + j
                    P.dma("pool", out[T * 128:(T + 1) * 128, :], H[:, j, :], reads=bl(f"H{j}"))
                yield 1

        def tails(b):
            XA = [(XT[0], XTB[0]), (XT[1], XTB[1]), (XS[0][:, :], bl("XS0")), (XS[1][:, :], bl("XS1"))]
            for j in range(4):
                tc = slice(j * 128, (j + 1) * 128)
                for hf in range(2):
                    cs = slice(hf * 512, (hf + 1) * 512)
                    pbi = 2 * j + hf
                    P.op("pe", lambda e, tc=tc, cs=cs, pbi=pbi: mm_group(e, PB[pbi][:, :],
                         [((OST[:, kk, tc] if kk < 4 else OGT[:, kk - 4, tc]), W_OUT[:, kk, cs]) for kk in range(8)]),
                         reads=bl(f"OST{j}", f"OGT{j}", "W_OUT"), writes=bl(f"PB{pbi}"))
            for j in range(4):
                xs, xb = XA[j]
                for hf in range(2):
                    cs = slice(hf * 512, (hf + 1) * 512)
                    pbi = 2 * j + hf
                    P.op("dve", lambda e, j=j, cs=cs, pbi=pbi, xs=xs: e.scalar_tensor_tensor(
                        out=H[:, j, cs], in0=xs[:, cs], scalar=ALPHA, in1=PB[pbi][:, :], op0=ALU.mult, op1=ALU.add),
                        reads=xb + bl(f"PB{pbi}"), writes=bl(f"H{j}"))
            items = [(H[:, j, :], bl(f"H{j}"), 2) for j in range(4)]
            stats = [ln_a(t, tb) for (t, tb, gi) in items]
            for (L, lb) in stats:
                ln_b(L, lb)
            for (L, lb) in stats:
                ln_c(L, lb)
            for (t, tb, gi), (L, lb) in zip(items, stats):
                ln_n(t, tb, L, lb)
            for (t, tb, gi) in items:
                ln_g(t, tb, gi)
            for (t, tb, gi) in items:
                ln_bias(t, tb, gi)
            for j in range(4):
                T = 4 * (b + 1) + j
                if debug:
                    P.dma("pool", dbg_h1[T * 128:(T + 1) * 128, :], H[:, j, :], reads=bl(f"H{j}"))
                transpose_tile(H[:, j, :], bl(f"H{j}"), HT1, bl(f"HT1_{j}"), j, (2 * j, 2 * j + 1))

        def run(gen):
            for _ in gen:
                pass

        NM_UNITS = 126.0
        F_FRAC = 0.75

        def step(g):
            try:
                next(g)
                return True
            except StopIteration:
                return False

        def interleave(gm, gf, extras):
            nm = nf = 0
            fl = gf is not None
            el = [True] * len(extras)
            if gm is not None:
                for w in gm:
                    nm += float(w)
                    while fl and nf / 44.0 <= nm / (F_FRAC * NM_UNITS):
                        fl = step(gf)
                        nf += 1
                    if not fl:
                        for i, g in enumerate(extras):
                            if el[i]:
                                el[i] = step(g)
                                if el[i] and getattr(g, "double", False):
                                    el[i] = step(g)
            while fl:
                fl = step(gf)
            for i, g in enumerate(extras):
                while el[i]:
                    el[i] = step(g)

        interleave(mixer_head(0), None, [casts(), casts_more(), post_f(-1)])
        tails(-1)
        ckpt(2)
        for b in range(nblk):
            gm = mixer_head(b + 1) if b + 1 < nblk else None
            interleave(gm, ffn(b), [post_f(b)])
            if b + 1 < nblk:
                tails(b)
        print("sbuf bytes remaining", nc.sbuf_bytes_remaining)
        P.finish()
        P.emit()
    return nc


WEIGHT_KEYS = ["meta_tokens", "ln_in_g", "ln_in_b", "w_in", "b_in", "w_gate_lr2", "b_gate_lr2", "attn_sinks",
               "gla_norm_g", "w_out", "ln1_g", "ln1_b", "w_ffn_gate", "w_ffn_up", "w_ffn_down", "ln2_g", "ln2_b"]


def make_in_map(inputs, bi):
    f = lambda a: np.ascontiguousarray(np.asarray(a, dtype=np.float32))
    m = {"x": f(inputs["x"][bi])}
    for k in WEIGHT_KEYS:
        a = f(inputs[k])
        if k not in ("meta_tokens", "ln_in_g", "ln_in_b"):
            a = a[0]
        m[k] = np.ascontiguousarray(a)
    m.update(host_consts())
    return m


def kernel(**inputs):
    x = np.asarray(inputs["x"])
    Bn, S, _ = x.shape
    nc = build(S // TB)
    in_maps = [make_in_map(inputs, bi) for bi in range(Bn)]
    res = run_bass_kernel_spmd(nc, in_maps, core_ids=list(range(Bn)))
    return np.stack([np.asarray(r["out"], dtype=np.float32) for r in res.results], axis=0)
```
